# Optimizing a Trainium2 kernel written in Bass

```python
import jax, jax.numpy as jnp
from jax import lax
import numpy as np

D_MODEL = 1024
BATCH = 2
SEQ = 8192
DEPTH = 2

ML_HEADS = 4
ML_WIDTH = D_MODEL
ML_HEAD_DIM = ML_WIDTH // ML_HEADS
ML_CHUNK = 64
CONV_WIDTH = 4
FOX_HEADS = 8
FOX_WIDTH = D_MODEL
FOX_HEAD_DIM = FOX_WIDTH // FOX_HEADS
Q_BLOCK = 128
POOL_GROUPS = 4
POOL_WIDTH = D_MODEL
POOL_GROUP_DIM = POOL_WIDTH // POOL_GROUPS
POOL_WINDOWS = (2, 4, 8, 16)
N_BRANCH = 3
EPS = 1e-6

SPLIT_SIZES = ((ML_WIDTH,) * 5 + (ML_HEADS,) * 2 + (FOX_WIDTH,) * 4 + (FOX_HEADS,)
               + (POOL_WIDTH,) * 2 + (N_BRANCH * D_MODEL,))
N_IN = sum(SPLIT_SIZES)
SPLIT_POINTS = tuple(np.cumsum(SPLIT_SIZES)[:-1].tolist())

kernel_name = "hybrid_mlstm_fox_pool_gated"


def rmsnorm(x, g):
    xf = x.astype(jnp.float32)
    r = lax.rsqrt(jnp.mean(xf * xf, axis=-1, keepdims=True) + EPS)
    return (xf * r).astype(x.dtype) * g


def causal_depthwise_conv(x, w):
    K = w.shape[0]
    S = x.shape[1]
    xp = jnp.pad(x, ((0, 0), (K - 1, 0), (0, 0)))
    out = xp[:, 0:S] * w[0]
    for kk in range(1, K):
        out = out + xp[:, kk:kk + S] * w[kk]
    return out


def mlstm_chunkwise(q, k, v, i_pre, f_pre):
    B, S, H, dh = q.shape
    L = ML_CHUNK
    nc = S // L
    q = q.astype(jnp.float32) * (dh ** -0.5)
    k = k.astype(jnp.float32)
    v = v.astype(jnp.float32)
    log_f = jax.nn.log_sigmoid(f_pre.astype(jnp.float32))
    i_g = i_pre.astype(jnp.float32)

    def to_chunks(a):
        a = a.reshape((B, nc, L, H) + a.shape[3:])
        return jnp.moveaxis(a, (1, 3), (0, 2))

    tri = jnp.tril(jnp.ones((L, L), dtype=bool))

    def step(carry, inp):
        C, n, m = carry
        qc, kc, vc, ic, fc = inp
        b = jnp.cumsum(fc, axis=-1)
        D = b[..., :, None] - b[..., None, :] + ic[..., None, :]
        D = jnp.where(tri, D, -jnp.inf)
        inter = b + m[..., None]
        m_t = jnp.maximum(inter, jnp.max(D, axis=-1))
        w_inter = jnp.exp(inter - m_t)
        P = jnp.exp(D - m_t[..., None]) * jnp.einsum('bhtd,bhsd->bhts', qc, kc)
        num = (w_inter[..., None] * jnp.einsum('bhvk,bhtk->bhtv', C, qc)
               + jnp.einsum('bhts,bhsv->bhtv', P, vc))
        den = w_inter * jnp.einsum('bhk,bhtk->bht', n, qc) + jnp.sum(P, axis=-1)
        h = num / jnp.maximum(jnp.abs(den), jnp.exp(-m_t))[..., None]
        bL = b[..., -1]
        dec = bL[..., None] - b + ic
        m_new = jnp.maximum(bL + m, jnp.max(dec, axis=-1))
        w_s = jnp.exp(dec - m_new[..., None])
        w_old = jnp.exp(bL + m - m_new)
        C_new = w_old[..., None, None] * C + jnp.einsum('bhsv,bhsk->bhvk', vc * w_s[..., None], kc)
        n_new = w_old[..., None] * n + jnp.einsum('bhs,bhsk->bhk', w_s, kc)
        return (C_new, n_new, m_new), h

    init = (jnp.zeros((B, H, dh, dh), jnp.float32),
            jnp.zeros((B, H, dh), jnp.float32),
            jnp.zeros((B, H), jnp.float32))
    _, hs = lax.scan(step, init, (to_chunks(q), to_chunks(k), to_chunks(v),
                                  to_chunks(i_g), to_chunks(log_f)))
    hs = jnp.moveaxis(hs, (0, 2), (1, 3))
    return hs.reshape(B, S, H, dh)


def forgetting_attention(q, k, v, f_pre):
    B, S, H, dh = q.shape
    scale = dh ** -0.5
    q = jnp.moveaxis(q, 2, 1)
    k = jnp.moveaxis(k, 2, 1)
    v = jnp.moveaxis(v, 2, 1)
    F = jnp.moveaxis(jnp.cumsum(jax.nn.log_sigmoid(f_pre.astype(jnp.float32)), axis=1), 2, 1)
    k_pos = jnp.arange(S)

    def block(i):
        start = i * Q_BLOCK
        qb = lax.dynamic_slice_in_dim(q, start, Q_BLOCK, axis=2)
        Fb = lax.dynamic_slice_in_dim(F, start, Q_BLOCK, axis=2)
        s = (jnp.einsum('bhqd,bhkd->bhqk', qb, k).astype(jnp.float32) * scale
             + Fb[..., :, None] - F[..., None, :])
        q_pos = start + jnp.arange(Q_BLOCK)
        s = jnp.where(k_pos[None, :] <= q_pos[:, None], s, -jnp.inf)
        p = jax.nn.softmax(s, axis=-1).astype(v.dtype)
        return jnp.einsum('bhqk,bhkd->bhqd', p, v)

    out = lax.map(block, jnp.arange(S // Q_BLOCK))
    out = jnp.moveaxis(out, 0, 2).reshape(B, H, S, dh)
    return jnp.moveaxis(out, 1, 2)


def multiscale_pool(u, pool_w, pool_scale):
    B, S, W = u.shape
    uf = u.astype(jnp.float32)
    cs = jnp.concatenate([jnp.zeros((B, 1, W), jnp.float32), jnp.cumsum(uf, axis=1)], axis=1)
    cs = cs.reshape(B, S + 1, POOL_GROUPS, POOL_GROUP_DIM)
    ug = uf.reshape(B, S, POOL_GROUPS, POOL_GROUP_DIM)
    t = jnp.arange(S)[:, None]
    win = jnp.array(POOL_WINDOWS, dtype=jnp.int32)[None, :]
    lo = jnp.maximum(t + 1 - win, 0)
    g_idx = jnp.arange(POOL_GROUPS)[None, :]
    win_sum = cs[:, 1:] - cs[:, lo, g_idx]
    cnt = jnp.minimum(t + 1, win).astype(jnp.float32)
    d = win_sum / cnt[None, :, :, None] - ug
    y = jnp.einsum('bsgc,gcd->bsgd', d, pool_w.astype(jnp.float32)).reshape(B, S, W)
    return (y * pool_scale).astype(u.dtype)


def hybrid_layer(x, norm_g, w_in, conv_w, ml_bi, ml_bf, ml_norm_g, fox_bf,
                 pool_w, pool_scale, w_branch, w_out):
    B, S, _ = x.shape
    h = rmsnorm(x, norm_g)
    proj = jnp.einsum('bsd,dn->bsn', h, w_in)
    (aq, ak, av, ao, az, ai, af, bq, bk, bv, bz, bf, cu, cz, gates) = jnp.split(proj, SPLIT_POINTS, axis=-1)

    qk = jax.nn.silu(causal_depthwise_conv(jnp.concatenate([aq, ak], axis=-1), conv_w))
    aq, ak = jnp.split(qk, 2, axis=-1)
    hA = mlstm_chunkwise(aq.reshape(B, S, ML_HEADS, ML_HEAD_DIM),
                         ak.reshape(B, S, ML_HEADS, ML_HEAD_DIM),
                         av.reshape(B, S, ML_HEADS, ML_HEAD_DIM),
                         ai + ml_bi, af + ml_bf)
    hA = hA * lax.rsqrt(jnp.mean(hA * hA, axis=-1, keepdims=True) + EPS)
    yA = (hA.reshape(B, S, ML_WIDTH).astype(x.dtype) * ml_norm_g
          * jax.nn.sigmoid(ao) * jax.nn.silu(az))

    hB = forgetting_attention(bq.reshape(B, S, FOX_HEADS, FOX_HEAD_DIM),
                              bk.reshape(B, S, FOX_HEADS, FOX_HEAD_DIM),
                              bv.reshape(B, S, FOX_HEADS, FOX_HEAD_DIM),
                              bf + fox_bf)
    yB = hB.reshape(B, S, FOX_WIDTH) * jax.nn.silu(bz)

    yC = multiscale_pool(cu, pool_w, pool_scale) * jax.nn.silu(cz)

    ys = jnp.stack([yA, yB, yC], axis=2)
    yb = jnp.einsum('bsnw,nwd->bsnd', ys, w_branch)
    g = jax.nn.sigmoid(gates.reshape(B, S, N_BRANCH, D_MODEL))
    merged = jnp.sum(g * yb, axis=2)
    return x + jnp.einsum('bsd,de->bse', merged, w_out)


def setup_inputs(seed: int = 0) -> dict:
    key = jax.random.key(seed)
    ks = jax.random.split(key, 16)
    f32 = jnp.float32
    nrm = lambda k, shape: jax.random.normal(k, shape, f32)
    x = nrm(ks[0], (BATCH, SEQ, D_MODEL))
    norm_g = 1.0 + 0.02 * nrm(ks[1], (DEPTH, D_MODEL))
    w_in = nrm(ks[2], (DEPTH, D_MODEL, N_IN)) * D_MODEL ** -0.5
    conv_w = nrm(ks[3], (DEPTH, CONV_WIDTH, 2 * ML_WIDTH)) * CONV_WIDTH ** -0.5
    ml_bi = 0.1 * nrm(ks[4], (DEPTH, ML_HEADS))
    ml_bf = jnp.linspace(3.0, 6.0, ML_HEADS, dtype=f32)[None, :] + 0.1 * nrm(ks[5], (DEPTH, ML_HEADS))
    ml_norm_g = 1.0 + 0.02 * nrm(ks[6], (DEPTH, ML_WIDTH))
    fox_bf = jnp.linspace(0.0, 4.0, FOX_HEADS, dtype=f32)[None, :] + 0.1 * nrm(ks[7], (DEPTH, FOX_HEADS))
    pool_w = nrm(ks[8], (DEPTH, POOL_GROUPS, POOL_GROUP_DIM, POOL_GROUP_DIM)) * POOL_GROUP_DIM ** -0.5
    pool_scale = 1.0 + 0.02 * nrm(ks[9], (DEPTH, POOL_WIDTH))
    w_branch = nrm(ks[10], (DEPTH, N_BRANCH, ML_WIDTH, D_MODEL)) * ML_WIDTH ** -0.5
    w_out = nrm(ks[11], (DEPTH, D_MODEL, D_MODEL)) * D_MODEL ** -0.5
    final_g = 1.0 + 0.02 * nrm(ks[12], (D_MODEL,))
    return {"x": x, "norm_g": norm_g, "w_in": w_in, "conv_w": conv_w, "ml_bi": ml_bi,
            "ml_bf": ml_bf, "ml_norm_g": ml_norm_g, "fox_bf": fox_bf, "pool_w": pool_w,
            "pool_scale": pool_scale, "w_branch": w_branch, "w_out": w_out, "final_g": final_g}


def reference(x, norm_g, w_in, conv_w, ml_bi, ml_bf, ml_norm_g, fox_bf,
              pool_w, pool_scale, w_branch, w_out, final_g):
    for l in range(DEPTH):
        x = hybrid_layer(x, norm_g[l], w_in[l], conv_w[l], ml_bi[l], ml_bf[l], ml_norm_g[l],
                         fox_bf[l], pool_w[l], pool_scale[l], w_branch[l], w_out[l])
    return rmsnorm(x, final_g)
```

```python
import contextlib
import numpy as np
import ml_dtypes
import concourse.bass as bass
import concourse.mybir as mybir
from concourse.bass_utils import run_bass_kernel_spmd

F32 = mybir.dt.float32
BF16 = mybir.dt.bfloat16
AF = mybir.ActivationFunctionType
ALU = mybir.AluOpType

D_MODEL = 1024
SEQ = 8192
BATCH = 2
DEPTH = 2
N_IN = 14352
POOL_WINDOWS = (2, 4, 8, 16)
EPS = 1e-6

EPOCH = 20000
SAME_ENGINE_SYNC = True


class Prog:
    CE = ("pe", "act", "dve", "pool")
    ALLE = ("pe", "act", "dve", "pool", "sp")
    NEP = 4

    def __init__(self, nc, stack):
        self.nc = nc
        self.NDS = 24
        self.NCC = 8
        self.csem = {e: [stack.enter_context(nc.semaphore(f"s_{e}{i}")) for i in range(self.NEP)] for e in self.CE}
        self.dsem = [stack.enter_context(nc.semaphore(f"s_d{i}")) for i in range(self.NDS)]
        self.ccsem = [stack.enter_context(nc.semaphore(f"s_cc{i}")) for i in range(self.NCC)]
        self.sigcnt = {e: 0 for e in self.CE}
        self.dma_cnt = [0] * self.NDS
        self.dma_rr = 0
        self.ncc = 0
        self.nflush = 0
        self._reset()

    def _reset(self):
        self.ops = {e: [] for e in self.ALLE}
        self.last_write = {}
        self.readers = {}
        self.seen = {e: {} for e in self.ALLE}
        self.signaled = {e: set() for e in self.CE}
        self.cc_pending = []

    def _need(self, eng, ev, waits):
        if ev is None:
            return
        if ev[0] == "c":
            _, e2, idx = ev
            if e2 == eng and (eng == "pe" or not SAME_ENGINE_SYNC):
                return
            k = ("c", e2)
            if self.seen[eng].get(k, -1) >= idx:
                return
            self.seen[eng][k] = idx
            waits.append(ev)
            self.signaled[e2].add(idx)
        elif ev[0] == "d":
            _, slot, cnt = ev
            k = ("d", slot)
            if self.seen[eng].get(k, 0) >= cnt:
                return
            self.seen[eng][k] = cnt
            waits.append(ev)
        else:
            k = ("x", ev[1])
            if k in self.seen[eng]:
                return
            self.seen[eng][k] = 1
            waits.append(ev)

    def _deps(self, eng, reads, writes):
        waits = []
        for r in reads:
            self._need(eng, self.last_write.get(r), waits)
        for w in writes:
            self._need(eng, self.last_write.get(w), waits)
            for ev in self.readers.get(w, ()):
                self._need(eng, ev, waits)
        return waits

    def _commit(self, ev, reads, writes):
        for r in reads:
            self.readers.setdefault(r, []).append(ev)
        for w in writes:
            self.last_write[w] = ev
            self.readers[w] = []

    def capture_begin(self):
        self._cap = []

    def capture_end(self):
        c, self._cap = self._cap, None
        return c

    def op(self, eng, fn, reads=(), writes=()):
        if getattr(self, "_cap", None) is not None:
            self._cap.append((eng, fn, reads, writes))
            return
        bk = [r for r in reads if isinstance(r, str) and r.startswith("bk")]
        if bk:
            writes = list(writes) + [b for b in bk if b not in writes]
            reads = [r for r in reads if r not in bk]
        waits = self._deps(eng, reads, writes)
        idx = len(self.ops[eng])
        self.ops[eng].append(dict(kind="c", fn=fn, waits=waits))
        self._commit(("c", eng, idx), reads, writes)

    def dma(self, out, in_, reads=(), writes=(), q="sp"):
        if getattr(self, "_cap", None) is not None:
            self._cap.append(("__dma__", out, in_, reads, writes))
            return
        slot = self.dma_rr
        self.dma_rr = (self.dma_rr + 1) % self.NDS
        waits = self._deps(q, reads, writes)
        if self.dma_cnt[slot] > 0:
            self._need(q, ("d", slot, self.dma_cnt[slot]), waits)
        self.dma_cnt[slot] += 1
        ev = ("d", slot, self.dma_cnt[slot])
        self.ops[q].append(dict(kind="d", out=out, in_=in_, waits=waits, slot=slot))
        self._commit(ev, reads, writes)

    def cc(self, kind, groups, out, in_, reads=(), writes=()):
        waits = self._deps("pool", reads, writes)
        n = self.ncc
        self.ncc += 1
        ev = ("x", n)
        self.ops["pool"].append(dict(kind="cc", cck=kind, groups=groups, out=out, in_=in_, waits=waits, n=n))
        self.cc_pending.append(ev)
        self._commit(ev, reads, writes)

    def mm(self, out, lhsT, rhs, start, stop, reads, writes):
        self.op("pe", lambda e: e.matmul(out, lhsT=lhsT, rhs=rhs, start=start, stop=stop, skip_group_check=True),
                reads, writes)

    def tr(self, out, in_, ident, reads, writes):
        self.op("pe", lambda e: e.transpose(out=out, in_=in_, identity=ident), reads, writes)

    def act(self, out, in_, func, reads, writes, **kw):
        self.op("act", lambda e: e.activation(out=out, in_=in_, func=func, **kw), reads, writes)

    def ts(self, out, in0, s1, s2, op0, op1, reads, writes, eng="dve"):
        if op1 is None:
            self.op(eng, lambda e: e.tensor_scalar(out=out, in0=in0, scalar1=s1, scalar2=None, op0=op0), reads, writes)
        else:
            self.op(eng, lambda e: e.tensor_scalar(out=out, in0=in0, scalar1=s1, scalar2=s2, op0=op0, op1=op1),
                    reads, writes)

    def tt(self, out, in0, in1, op, reads, writes, eng="dve"):
        self.op(eng, lambda e: e.tensor_tensor(out=out, in0=in0, in1=in1, op=op), reads, writes)

    def stt(self, out, in0, scalar, in1, op0, op1, reads, writes, eng="dve"):
        self.op(eng, lambda e: e.scalar_tensor_tensor(out=out, in0=in0, scalar=scalar, in1=in1, op0=op0, op1=op1),
                reads, writes)

    def cp(self, eng, out, in_, reads, writes):
        if eng == "act":
            self.op("act", lambda e: e.copy(out=out, in_=in_), reads, writes)
        else:
            self.op(eng, lambda e: e.tensor_copy(out=out, in_=in_), reads, writes)

    def flush(self, fin, bank7):
        nc = self.nc
        fk = [("fin", e) for e in self.CE]
        waits = []
        for ev in self.cc_pending:
            self._need("pool", ev, waits)
        self.ops["pool"].append(dict(kind="w", waits=waits))
        self.op("act", lambda e: e.copy(out=fin[0:1, 0:1], in_=fin[0:1, 1:2]), ["fin0"], [fk[1]])
        self.op("dve", lambda e: e.memset(fin[0:1, 2:3], 0.0), (), [fk[2]])
        self.op("pool", lambda e: e.memset(fin[0:1, 3:4], 0.0), (), [fk[3]])
        self.op("pe", lambda e: e.matmul(bank7[0:1, 511:512], lhsT=fin[0:1, 4:5], rhs=fin[0:1, 5:6], start=True, stop=True,
                                         skip_group_check=True), ["bk7", "fin0"], [fk[0]])
        for e in self.ALLE:
            waits = []
            for k in fk:
                self._need(e, self.last_write.get(k), waits)
            for slot in range(self.NDS):
                if self.dma_cnt[slot] > 0:
                    self._need(e, ("d", slot, self.dma_cnt[slot]), waits)
            self.ops[e].append(dict(kind="w", waits=waits))
        semval = {}
        for e in self.CE:
            c = self.sigcnt[e]
            for idx in range(len(self.ops[e])):
                if idx in self.signaled[e]:
                    semval[(e, idx)] = (c // EPOCH, c % EPOCH + 1)
                    c += 1
            self.sigcnt[e] = c
            assert c <= EPOCH * self.NEP, (e, c)
        csem, dsem, ccsem = self.csem, self.dsem, self.ccsem
        need_q = any(o["kind"] == "d" and (callable(o["out"]) or callable(o["in_"])) for o in self.ops["sp"])
        with nc.Block() as block:
            def run(e):
                def body(h):
                    qv = None
                    if e == "sp" and need_q:
                        qv = _QCtx(h)
                    for i, o in enumerate(self.ops[e]):
                        for ev in o["waits"]:
                            if ev[0] == "c":
                                ep, v = semval[(ev[1], ev[2])]
                                h.wait_ge(csem[ev[1]][ep], v)
                            elif ev[0] == "d":
                                h.wait_ge(dsem[ev[1]], 16 * ev[2])
                            else:
                                h.wait_ge(ccsem[ev[1] % self.NCC], ev[1] // self.NCC + 1)
                        if o["kind"] == "c":
                            ins = o["fn"](h)
                            if (e, i) in semval:
                                ep, v = semval[(e, i)]
                                ins.then_inc(csem[e][ep], 1)
                        elif o["kind"] == "d":
                            o_ = o["out"](qv) if callable(o["out"]) else o["out"]
                            i_ = o["in_"](qv) if callable(o["in_"]) else o["in_"]
                            try:
                                h.dma_start(out=o_, in_=i_).then_inc(dsem[o["slot"]], 16)
                            except Exception:
                                print("DMA FAIL", o_, i_, flush=True)
                                raise
                        elif o["kind"] == "cc":
                            h.collective_compute(o["cck"], ALU.bypass, replica_groups=o["groups"], ins=[o["in_"].opt()],
                                                 outs=[o["out"].opt()]).then_inc(ccsem[o["n"] % self.NCC], 1)
                return body
            block.sync(run("sp"))
            block.tensor(run("pe"))
            block.scalar(run("act"))
            block.vector(run("dve"))
            block.gpsimd(run("pool"))
        self.nflush += 1
        self._reset()


NW1 = 2820
NFM = 1024


def p1_record(P, nc, st, bank, S, xtile, w1_d, vec_d, row_d, pw_d, cst_d, ys_d, after_st=None, stage=9, dbg=()):
    NT = S // 128
    NST = S // 512
    GT = (NT + 2) // 3
    if True:
        tag = f"sb{P.nflush}_"
        sb = lambda n, s, d: st.enter_context(nc.sbuf_tensor(tag + n, s, d))
        bkey = [f"bk{b}" for b in range(8)]
        b5b = bank[5][:].bitcast(BF16)

        cst_f = sb("cst_f", [128, 8, 128], F32)
        cst_b = sb("cst_b", [128, 8, 128], BF16)
        vec = sb("vec", [128, 32], F32)
        rows = sb("rows", [128, 512], F32)
        pw_b = sb("pw_b", [128, 2, 256], BF16)
        W1 = sb("W1", [128, 8, NW1], BF16)
        stg = [sb(f"stg{i}", [128, 8, 64], F32) for i in range(2)]
        KT = sb("KT", [128, 2, S], BF16)
        VA = sb("VA", [128, NT, 2, 129], BF16)
        KF = sb("KF", [128, 2, GT * 128], BF16)
        hT = sb("hT", [128, 8, 512], BF16)
        QTs = [sb(f"QT{i}", [128, 2, 512], BF16) for i in range(2)]
        QFs = [sb(f"QF{i}", [128, 2, 512], BF16) for i in range(2)]
        cin = sb("cin", [128, 515], F32)
        halo = sb("halo", [128, 4, 3], F32)
        ctmp = sb("ctmp", [128, 512], F32)
        csig = sb("csig", [128, 512], F32)
        qkT = sb("qkT", [128, 4, 512], BF16)
        xt = [sb(f"xt{i}", [128, 1024], F32) for i in range(2)]
        ss = sb("ss", [128, 2], F32)
        xn = sb("xn", [128, 1024], BF16)
        junk_b = xn
        gateBs = [[sb(f"gateB{p}_{i}", [128, 256], F32) for i in range(4)] for p in range(2)]
        VAm4 = [sb(f"VAm{i}", [128, 257], BF16) for i in range(4)]
        gateA4 = [sb(f"gateA{i}", [128, 256], F32) for i in range(4)]
        gateC4 = [sb(f"gateC{i}", [128, 256], F32) for i in range(4)]
        ut = [sb(f"ut{i}", [128, 256], BF16) for i in range(5)]
        sp4 = [sb(f"sp{i}", [128, 4], F32) for i in range(4)]
        cs4 = [sb(f"cs{i}", [128, 12], F32) for i in range(4)]
        ls4 = [sb(f"ls{i}", [128, 4], F32) for i in range(4)]
        sgs = [sb("sg0", [128, 256], F32), sb("sg1", [128, 256], F32)]
        sg = sgs[0]
        sm = sb("sm", [128, 32], F32)
        Fcar = sb("Fcar", [128, 2], F32)
        f3b = sb("f3b", [128, 2, 3], BF16)
        f3n = sb("f3n", [128, 2, 3], BF16)
        fw = sb("fw", [128, 8], F32)
        qfk = sb("qfk", [128, 2, 2, 128], BF16)
        CT = sb("CT", [128, 2, 257], F32)
        CTb = sb("CTb", [128, 2, 257], BF16)
        Bm = sb("Bm", [128, 128], F32)
        ET = sb("ET", [128, 128], F32)
        PTm = sb("PTm", [128, 128], BF16)
        Ktok = sb("Ktok", [128, 256], BF16)
        Vw = sb("Vw", [128, 257], BF16)
        Gs = sb("Gs", [128, 257], F32)
        dTb = sb("dTb", [128, 2, 128], BF16)
        PT = [sb(f"PT{i}", [128, 512], BF16) for i in range(2)]
        yst = [sb(f"yst{i}", [128, 768], BF16) for i in range(4)]

        IDb = cst_b[:, 0, :]
        IDf = cst_f[:, 0, :]
        UIN = cst_f[:, 1, :]
        UREV = cst_f[:, 2, :]
        ONESf = cst_f[:, 3, :]
        ONESb = cst_b[:, 3, :]
        MASKb = cst_b[:, 4, :]
        MCUR = cst_b[:, 5, :]
        MFIRST = cst_b[:, 6, :]
        MPREV = cst_b[:, 7, :]

        P.dma(cst_f[:], cst_d.rearrange("p (c n) -> p c n", n=128), writes=["cst_f"])
        P.dma(vec[:], vec_d, writes=["vec"])
        P.dma(rows[:], row_d, writes=["rows"])
        P.ts(rows[:, 0:256], rows[:, 0:256], 0.25, None, ALU.mult, None, ["rows"], ["rows"])
        P.ts(rows[:, 256:512], rows[:, 256:512], 0.5, None, ALU.mult, None, ["rows"], ["rows"])
        P.cp("dve", cst_b[:], cst_f[:], ["cst_f"], ["cst_b"])
        pwst = stg[0][:].rearrange("p k n -> p (k n)")[:, 0:512].rearrange("p (k n) -> p k n", n=256)
        assert 8 * 64 >= 512
        P.dma(pwst, pw_d.rearrange("(k p) n -> p k n", p=128), writes=[("stg", 0)])
        P.cp("dve", pw_b[:], pwst, [("stg", 0)], ["pw_b"])
        P.op("pool", lambda e: e.memset(VA[:], 1.0), (), [("VA", j) for j in range(NST)])
        for i4 in range(4):
            P.op("pool", lambda e, i4=i4: e.memset(VAm4[i4][:], 1.0), (), [("VAm", i4)])
        P.op("pool", lambda e: e.memset(halo[:], 0.0), (), [("halo", c) for c in range(4)])
        P.op("pool", lambda e: e.memset(Fcar[:], 0.0), (), ["Fcar"])
        P.op("pool", lambda e: e.memset(CT[:], 0.0), (), ["CT"])
        P.op("pool", lambda e: e.memset(CTb[:], 0.0), (), ["CTb"])
        P.op("pool", lambda e: e.memset(qfk[:], 0.0), (), ["qfk"])
        P.op("pool", lambda e: e.memset(KF[:], 0.0), (), [("KF", j) for j in range(NST)])
        for h in range(2):
            qv = qfk[:, h, 0, :].rearrange("p (a c) -> p a c", c=32)
            kv = qfk[:, h, 1, :].rearrange("p (a c) -> p a c", c=32)
            P.op("pool", lambda e, qv=qv: e.memset(qv[:, :, 3:6], 1.0), (), ["qfk"])
            P.op("pool", lambda e, kv=kv: e.memset(kv[:, :, 0:3], 1.0), (), ["qfk"])

        chunks = [(c0, min(c0 + 64, NW1)) for c0 in range(0, NW1, 64)]
        w1v = w1_d.rearrange("(k p) n -> p k n", p=128)
        for ci, (c0, c1) in enumerate(chunks if 'now' not in dbg else []):
            s_ = stg[ci % 2]
            n = c1 - c0
            P.dma(s_[:, :, 0:n], w1v[:, :, c0:c1], writes=[("stg", ci % 2)])
            for k in range(8):
                if k % 2 == 0:
                    P.ts(W1[:, k, c0:c1], s_[:, k, 0:n], vec[:, k:k + 1], None, ALU.mult, None,
                         [("stg", ci % 2), "vec"], [("W1", ci, k)])
                else:
                    P.act(W1[:, k, c0:c1], s_[:, k, 0:n], AF.Copy, [("stg", ci % 2), "vec"], [("W1", ci, k)],
                          scale=vec[:, k:k + 1])

        def w1keys(c0, c1, k):
            return [("W1", ci, k) for ci, (a, b) in enumerate(chunks) if a < c1 and b > c0]

        def abc(j):
            QT, QF, gateB = QTs[j % 2], QFs[j % 2], gateBs[j % 2]
            kQT, kQF = ("QT", j % 2), ("QF", j % 2)
            for a in range(4 if 'noA' not in dbg else 0):
                i = 4 * j + a
                xb = xt[i % 2]
                xk = ("xt", i % 2)
                P.dma(xb[:], xtile(i), writes=[xk])
                P.act(junk_b[:], xb[:], AF.Square, [xk], ["xn", "ss"], accum_out=ss[:, 0:1])
                P.act(ss[:, 1:2], ss[:, 0:1], AF.Ln, ["ss"], ["ss1"], scale=1.0 / 1024, bias=EPS)
                P.act(ss[:, 1:2], ss[:, 1:2], AF.Exp, ["ss1"], ["ss1"], scale=-0.5)
                P.ts(xn[:], xb[:], ss[:, 1:2], None, ALU.mult, None, [xk, "ss1"], ["xn"])
                for k in range(8):
                    P.tr(b5b[:, k * 128:(k + 1) * 128], xn[:, k * 128:(k + 1) * 128], IDb, ["xn", "cst_b"], [bkey[5]])
                P.cp("act", hT[:, :, a * 128:(a + 1) * 128], b5b[:, 0:1024].rearrange("p (k n) -> p k n", n=128),
                     [bkey[5]], ["hT"])

            for c in range(8 if stage >= 1 else 0):
                bank4, bkey4 = bank[4], bkey[4]
                for k in range(8):
                    P.mm(bank4[:, 0:512], W1[:, k, c * 128:(c + 1) * 128], hT[:, k, :], k == 0, k == 7,
                         ["hT"] + w1keys(c * 128, (c + 1) * 128, k), [bkey4])
                if c < 2:
                    P.act(QT[:, c, :], bank4[:, 0:512], AF.Copy, [bkey4], [kQT], scale=128 ** -0.5)
                elif c < 4:
                    P.cp("dve", KT[:, c - 2, j * 512:(j + 1) * 512], bank4[:, 0:512], [bkey4], [("KT", j)])
                else:
                    cc = c - 4
                    P.cp("act", cin[:, 3:515], bank4[:, 0:512], [bkey4], ["cin"])
                    P.cp("dve", cin[:, 0:3], halo[:, cc, :], [("halo", cc)], ["cin"])
                    P.ts(ctmp[:], cin[:, 0:512], vec[:, 8 + cc * 4:9 + cc * 4], None, ALU.mult, None,
                         ["cin", "vec"], ["ctmp"])
                    for t in range(1, 4):
                        P.stt(ctmp[:], cin[:, t:t + 512], vec[:, 8 + cc * 4 + t:9 + cc * 4 + t], ctmp[:],
                              ALU.mult, ALU.add, ["cin", "vec", "ctmp"], ["ctmp"])
                    P.cp("dve", halo[:, cc, :], cin[:, 512:515], ["cin"], [("halo", cc)])
                    P.act(csig[:], ctmp[:], AF.Tanh, ["ctmp"], ["csig"], scale=0.5)
                    P.stt(csig[:], csig[:], 1.0, ctmp[:], ALU.add, ALU.mult, ["ctmp", "csig"], ["csig"])
                    P.act(qkT[:, cc, :], csig[:], AF.Copy, ["csig"], ["qkT"], scale=0.5 * ((256 ** -0.5) if cc < 2 else 1.0))

            for a in range(4 if stage >= 2 else 0):
                i = 4 * j + a
                ts_ = slice(a * 128, (a + 1) * 128)
                VAm, gateA, gateC, sp_, cs, lsv = VAm4[a], gateA4[a], gateC4[a], sp4[a], cs4[a], ls4[a]
                kVAm, kgA, kgC, ksp, kcs, kls = ("VAm", a), ("gateA", a), ("gateC", a), ("sp", a), ("cs", a), ("ls", a)
                groups = [(0, 512), (512, 1024), (1024, 1536), (1536, 1796)]
                for gi, (c0, c1) in enumerate(groups):
                    n = c1 - c0
                    bank4, bkey4 = bank[4], bkey[4]
                    sg = sgs[gi % 2]
                    sgk = ("sg", gi % 2)
                    for k in range(8):
                        P.mm(bank4[:, 0:n], hT[:, k, ts_], W1[:, k, NFM + c0:NFM + c1], k == 0, k == 7,
                             ["hT"] + w1keys(NFM + c0, NFM + c1, k), [bkey4])
                    lo = bank4[:, 0:256]
                    hi = bank4[:, 256:512]
                    if gi == 0:
                        P.cp("dve", VA[:, i, :, 0:128], lo.rearrange("p (h d) -> p h d", d=128), [bkey4], [("VA", j)])
                        P.act(sg[:], hi, AF.Tanh, [bkey4], [sgk], scale=0.5)
                        P.stt(gateB[a][:], sg[:], 1.0, hi, ALU.add, ALU.mult, [sgk, bkey4], [("gateB", j % 2, a)])
                    elif gi == 1:
                        P.cp("dve", VAm[:, 0:256], lo, [bkey4], [kVAm])
                        P.act(sg[:], hi, AF.Tanh, [bkey4], [sgk], scale=0.5)
                        P.stt(gateA[:], sg[:], 1.0, rows[:, 0:256], ALU.add, ALU.mult, [sgk, "rows"], [kgA])
                    elif gi == 2:
                        P.act(sg[:], lo, AF.Tanh, [bkey4], [sgk], scale=0.5)
                        P.stt(sg[:], sg[:], 1.0, lo, ALU.add, ALU.mult, [sgk, bkey4], [sgk])
                        P.tt(gateA[:], gateA[:], sg[:], ALU.mult, [sgk, kgA], [kgA])
                        P.cp("act", ut[i % 5][:], hi, [bkey4], [("ut", i % 5)])
                    else:
                        P.act(sg[:], lo, AF.Tanh, [bkey4], [sgk], scale=0.5)
                        P.stt(sg[:], sg[:], 1.0, lo, ALU.add, ALU.mult, [sgk, bkey4], [sgk])
                        P.tt(gateC[:], sg[:], rows[:, 256:512], ALU.mult, [sgk, "rows"], [kgC])
                        P.tt(sp_[:], vec[:, 24:28], bank4[:, 256:260], ALU.add, ["vec", bkey4], [ksp])

                if stage < 3:
                    continue
                P.act(sm[:, 0:3], sp_[:, 1:4], AF.Abs, [ksp], ["sm0"])
                P.act(sm[:, 3:6], sm[:, 0:3], AF.Exp, ["sm0"], ["sm3"], scale=-1.0)
                P.act(sm[:, 6:9], sm[:, 3:6], AF.Ln, ["sm3"], ["sm6"], bias=1.0)
                P.ts(sm[:, 9:12], sp_[:, 1:4], 0.0, None, ALU.min, None, [ksp], ["sm9"])
                P.tt(lsv[:, 0:3], sm[:, 9:12], sm[:, 6:9], ALU.subtract, ["sm9", "sm6"], [kls])
                ls = lsv[:, 0:3]
                P.mm(bank[7][:, 260:263], UIN, ls, True, True, [kls, "cst_f"], [bkey[7]])
                P.mm(bank[7][:, 263:264], UREV, lsv[:, 0:1], True, True, [kls, "cst_f"], [bkey[7]])
                P.mm(bank[7][:, 264:267], ONESf, ls, True, True, [kls, "cst_f"], [bkey[7]])
                P.cp("dve", cs[:, 0:7], bank[7][:, 260:267], [bkey[7]], [kcs])
                P.tt(fw[:, 0:2], cs[:, 1:3], Fcar[:], ALU.add, [kcs, "Fcar"], ["fw0"])
                P.tt(Fcar[:], Fcar[:], cs[:, 5:7], ALU.add, [kcs, "Fcar"], ["Fcar"])
                P.cp("dve", f3b[:, :, 0], fw[:, 0:2], ["fw0"], ["f3b0"])
                P.cp("dve", fw[:, 2:4], f3b[:, :, 0], ["f3b0"], ["fw2"])
                P.tt(fw[:, 4:6], fw[:, 0:2], fw[:, 2:4], ALU.subtract, ["fw0", "fw2"], ["fw4"])
                P.cp("dve", f3b[:, :, 1], fw[:, 4:6], ["fw4"], ["f3b1"])
                P.cp("dve", fw[:, 2:4], f3b[:, :, 1], ["f3b1"], ["fw2"])
                P.tt(fw[:, 6:8], fw[:, 4:6], fw[:, 2:4], ALU.subtract, ["fw4", "fw2"], ["fw6"])
                P.cp("dve", f3b[:, :, 2], fw[:, 6:8], ["fw6"], ["f3b2"])
                P.ts(f3n[:], f3b[:], -1.0, None, ALU.mult, None, ["f3b0", "f3b1", "f3b2"], ["f3n"])
                ak = i // GT
                for h in range(2):
                    qv = qfk[:, h, 0, :].rearrange("p (a c) -> p a c", c=32)
                    for a4 in range(4):
                        P.cp("dve", qv[:, a4, 0:3], f3b[:, h, :], ["f3b0", "f3b1", "f3b2"], ["qfk"])
                    P.cp("dve", qfk[:, h, 1, 32 * ak + 3:32 * ak + 6], f3n[:, h, :], ["f3n"], ["qfk"])
                for h in range(2):
                    for w in range(2):
                        P.tr(b5b[:, (2 * h + w) * 128:(2 * h + w + 1) * 128], qfk[:, h, w, :], IDb,
                             ["qfk", "cst_b"], [bkey[5]])
                for h in range(2):
                    P.cp("act", QF[:, h, ts_], b5b[:, (2 * h) * 128:(2 * h + 1) * 128], [bkey[5]], [kQF])
                    kcol = (i % GT) * 128
                    P.cp("dve", KF[32 * ak:32 * ak + 6, h, kcol:kcol + 128],
                         b5b[32 * ak:32 * ak + 6, (2 * h + 1) * 128:(2 * h + 2) * 128], [bkey[5]], [("KF", j)])

        def de(j):
            for a in range(4 if stage >= 4 else 0):
                i = 4 * j + a
                ts_ = slice(a * 128, (a + 1) * 128)
                VAm, gateA, gateC, sp_, cs, lsv = VAm4[a], gateA4[a], gateC4[a], sp4[a], cs4[a], ls4[a]
                kVAm, kgA, kgC, ksp, kcs, kls = ("VAm", a), ("gateA", a), ("gateC", a), ("sp", a), ("cs", a), ("ls", a)
                P.mm(bank[6][:, 0:128], qkT[:, 2, ts_], qkT[:, 0, ts_], True, False, ["qkT"], [bkey[6]])
                P.mm(bank[6][:, 0:128], qkT[:, 3, ts_], qkT[:, 1, ts_], False, True, ["qkT"], [bkey[6]])
                P.ts(Bm[:], UIN, lsv[:, 0:1], None, ALU.mult, None, ["cst_f", kls], ["Bm"])
                P.mm(bank[6][:, 128:256], UREV, Bm[:], True, False, ["Bm", "cst_f"], [bkey[6]])
                P.mm(bank[6][:, 128:256], IDb, MASKb, False, True, ["cst_b"], [bkey[6]])
                P.act(ET[:], bank[6][:, 128:256], AF.Exp, [bkey[6], ksp], ["ET"], bias=sp_[:, 0:1])
                P.tt(PTm[:], ET[:], bank[6][:, 0:128], ALU.mult, ["ET", bkey[6]], ["PTm"])
                P.tr(b5b[:, 0:128], qkT[:, 2, ts_], IDb, ["qkT", "cst_b"], [bkey[5]])
                P.tr(b5b[:, 128:256], qkT[:, 3, ts_], IDb, ["qkT", "cst_b"], [bkey[5]])
                P.cp("act", Ktok[:], b5b[:, 0:256], [bkey[5]], ["Ktok"])
                P.act(sm[:, 16:17], cs[:, 0:1], AF.Exp, [kcs], ["sm16"])
                P.act(sm[:, 17:18], cs[:, 3:4], AF.Exp, [kcs, ksp], ["sm17"], bias=sp_[:, 0:1])
                P.act(sm[:, 18:19], cs[:, 4:5], AF.Exp, [kcs], ["sm18"])
                P.ts(Vw[:], VAm[:], sm[:, 17:18], None, ALU.mult, None, [kVAm, "sm17"], ["Vw"])
                P.mm(bank[7][:, 0:257], qkT[:, 0, ts_], CTb[:, 0, :], True, False, ["qkT", "CTb"], [bkey[7]])
                P.mm(bank[7][:, 0:257], qkT[:, 1, ts_], CTb[:, 1, :], False, True, ["qkT", "CTb"], [bkey[7]])
                P.act(Gs[:], bank[7][:, 0:257], AF.Copy, [bkey[7], "sm16"], ["Gs"], scale=sm[:, 16:17])
                P.mm(bank[7][:, 0:257], PTm[:], VAm[:], True, True, ["PTm", kVAm], [bkey[7]])
                P.tt(Gs[:], Gs[:], bank[7][:, 0:257], ALU.add, ["Gs", bkey[7]], ["Gs"])
                for c in range(2):
                    P.mm(bank[7][:, 0:257], Ktok[:, c * 128:(c + 1) * 128], Vw[:], True, True, ["Ktok", "Vw"], [bkey[7]])
                    P.stt(CT[:, c, :], CT[:, c, :], sm[:, 18:19], bank[7][:, 0:257], ALU.mult, ALU.add,
                          ["CT", "sm18", bkey[7]], ["CT"])
                    P.cp("act", CTb[:, c, :], CT[:, c, :], ["CT"], ["CTb"])
                P.act(sm[:, 19:20], Gs[:, 256:257], AF.Abs, ["Gs"], ["sm19"])
                P.ts(sm[:, 20:21], sm[:, 19:20], 1.0, None, ALU.max, None, ["sm19"], ["sm20"])
                P.op("dve", lambda e: e.reciprocal(out=sm[:, 21:22], in_=sm[:, 20:21]), ["sm20"], ["sm21"])
                P.act(sgs[0][:], Gs[:, 0:256], AF.Square, ["Gs"], [("sg", 0), "sm22"], accum_out=sm[:, 22:23])
                P.tt(sm[:, 23:24], sm[:, 21:22], sm[:, 21:22], ALU.mult, ["sm21"], ["sm23"])
                P.tt(sm[:, 24:25], sm[:, 23:24], sm[:, 22:23], ALU.mult, ["sm23", "sm22"], ["sm24"])
                P.act(sm[:, 25:26], sm[:, 24:25], AF.Ln, ["sm24"], ["sm25"], scale=1.0 / 256, bias=EPS)
                P.act(sm[:, 26:27], sm[:, 25:26], AF.Exp, ["sm25"], ["sm26"], scale=-0.5)
                P.tt(sm[:, 27:28], sm[:, 26:27], sm[:, 21:22], ALU.mult, ["sm26", "sm21"], ["sm27"])
                yk = ("yst", a)
                P.stt(yst[a][:, 0:256], Gs[:, 0:256], sm[:, 27:28], gateA[:], ALU.mult, ALU.mult,
                      ["Gs", "sm27", kgA], [yk])

                if stage < 5:
                    continue
                ucur = ut[i % 5]
                uprev = ut[(i - 1) % 5]
                for cc in range(2):
                    o_ = bank[6][:, 256 + cc * 128:384 + cc * 128]
                    if i == 0:
                        P.mm(o_, ucur[:, cc * 128:(cc + 1) * 128], MFIRST, True, True, [("ut", i % 5), "cst_b"], [bkey[6]])
                    else:
                        P.mm(o_, ucur[:, cc * 128:(cc + 1) * 128], MCUR, True, False, [("ut", i % 5), "cst_b"], [bkey[6]])
                        P.mm(o_, uprev[:, cc * 128:(cc + 1) * 128], MPREV, False, True,
                             [("ut", (i - 1) % 5), "cst_b"], [bkey[6]])
                P.cp("act", dTb[:], bank[6][:, 256:512].rearrange("p (c n) -> p c n", n=128), [bkey[6]], ["dTb"])
                for cc in range(2):
                    P.mm(bank[7][:, 0:256], dTb[:, cc, :], pw_b[:, cc, :], cc == 0, cc == 1, ["dTb", "pw_b"], [bkey[7]])
                P.tt(yst[a][:, 512:768], gateC[:], bank[7][:, 0:256], ALU.mult, [kgC, bkey[7]], [yk])

        def fout(j, deferred):
            QT, QF, gateB = QTs[j % 2], QFs[j % 2], gateBs[j % 2]
            kQT, kQF = ("QT", j % 2), ("QF", j % 2)
            its = [(h, kt) for h in range(2 if stage >= 6 else 0) for kt in range(4 * j + 4)]

            def scores(idx):
                h, kt = its[idx]
                n0 = max(0, kt - 4 * j)
                q0 = n0 * 128
                ak = kt // GT
                kcol = (kt % GT) * 128
                sb_ = idx % 2
                stt_ = bank[sb_][:, q0:512]
                diag = kt >= 4 * j
                P.mm(stt_, KT[:, h, kt * 128:(kt + 1) * 128], QT[:, h, q0:512], True, False,
                     [("KT", kt // 4), kQT], [bkey[sb_]])
                P.mm(stt_, KF[32 * ak:32 * ak + 6, h, kcol:kcol + 128], QF[32 * ak:32 * ak + 6, h, q0:512],
                     False, not diag, [("KF", kt // 4), kQF], [bkey[sb_]])
                if diag:
                    P.mm(bank[sb_][:, q0:q0 + 128], IDb, MASKb, False, True, ["cst_b"], [bkey[sb_]])
                P.act(PT[sb_][:, q0:512], stt_, AF.Exp, [bkey[sb_]], [("PT", sb_)])

            per = (len(deferred) + max(1, len(its) - 2) - 1) // max(1, len(its) - 2)

            def replay(n):
                for _ in range(n):
                    if deferred:
                        t_ = deferred.pop()
                        if t_[0] == "__dma__":
                            P.dma(*t_[1:])
                        else:
                            P.op(*t_)
            if its:
                scores(0)
            for idx, (h, kt) in enumerate(its):
                if idx + 1 < len(its):
                    scores(idx + 1)
                replay(per)
                n0 = max(0, kt - 4 * j)
                q0 = n0 * 128
                sb_ = idx % 2
                first = (kt == 0)
                lastk = (kt == 4 * j + 3)
                P.mm(bank[2][:, q0:512], VA[:, kt, h, 0:128], PT[sb_][:, q0:512], first, lastk,
                     [("PT", sb_), ("VA", kt // 4)], [bkey[2]])
                P.mm(bank[3][:, q0:512], ONESb, PT[sb_][:, q0:512], first, lastk, [("PT", sb_), "cst_b"], [bkey[3]])
                if lastk:
                    P.op("dve", lambda e: e.reciprocal(out=ctmp[:], in_=bank[3][:, 0:512]), [bkey[3]], ["ctmp"])
                    P.tt(csig[:], ctmp[:], bank[2][:, 0:512], ALU.mult, ["ctmp", bkey[2]], ["csig"])
                    for n in range(4):
                        P.tr(bank[3][:, n * 128:(n + 1) * 128], csig[:, n * 128:(n + 1) * 128], IDf, ["csig", "cst_f"], [bkey[3]])
                    for n in range(4):
                        P.stt(yst[n][:, 256 + h * 128:384 + h * 128], bank[3][:, n * 128:(n + 1) * 128], 0.5,
                              gateB[n][:, h * 128:(h + 1) * 128], ALU.mult, ALU.mult,
                              [("gateB", j % 2, n), bkey[3]], [("yst", n)])
            replay(len(deferred))
            for n in range(4):
                i = 4 * j + n
                P.dma(ys_d[i * 128:(i + 1) * 128, :], yst[n][:], reads=[("yst", n)], writes=[("ys", i)])
            if after_st is not None:
                after_st(j, [("ys", 4 * j + n) for n in range(4)])

        abc(0)
        for j in range(NST):
            P.capture_begin()
            de(j)
            if j + 1 < NST:
                abc(j + 1)
            deferred = P.capture_end()
            deferred.reverse()
            fout(j, deferred)
    return [("ys", i) for i in range(NT)]


def _consts(g):
    s = np.arange(128)[:, None]
    t = np.arange(128)[None, :]
    ident = (s == t).astype(np.float32)
    uin = (s <= t).astype(np.float32)
    urev = (s > t).astype(np.float32)
    ones = np.ones((128, 128), np.float32)
    maskT = np.where(s > t, -30000.0, 0.0).astype(np.float32)
    W = POOL_WINDOWS[g]
    def mt(first):
        M = np.zeros((128, 256), np.float32)
        for tt in range(128):
            cnt = min(tt + 1, W) if first else W
            for jj in range(cnt):
                M[tt, 128 + tt - jj] += 1.0 / cnt
            M[tt, 128 + tt] -= 1.0
        return M
    Mg = mt(False)
    Mf = mt(True)
    mcur = Mg[:, 128:].T.copy()
    mprev = Mg[:, :128].T.copy()
    mfirst = Mf[:, 128:].T.copy()
    return np.concatenate([ident, uin, urev, ones, maskT, mcur, mfirst, mprev], axis=1).astype(np.float32)


def _p1_cols(g):
    W = 1024
    off = {}
    names = ["aq", "ak", "av", "ao", "az"]
    o = 0
    for nme in names:
        off[nme] = o
        o += W
    off["ai"] = o; o += 4
    off["af"] = o; o += 4
    for nme in ["bq", "bk", "bv", "bz"]:
        off[nme] = o
        o += W
    off["bf"] = o; o += 8
    off["cu"] = o; o += W
    off["cz"] = o; o += W
    off["gates"] = o
    r = lambda nme: np.arange(off[nme] + g * 256, off[nme] + (g + 1) * 256)
    cols = np.concatenate([
        r("bq"), r("bk"), r("aq"), r("ak"),
        r("bv"), r("bz"), r("av"), r("ao"), r("az"), r("cu"), r("cz"),
        np.array([off["ai"] + g, off["af"] + g, off["bf"] + 2 * g, off["bf"] + 2 * g + 1]),
    ])
    return cols, off


def p1_inputs(x_b, l, g, inp):
    cols, off = _p1_cols(g)
    w1 = np.ascontiguousarray(inp["w_in"][l][:, cols])
    vec = np.zeros((128, 32), np.float32)
    vec[:, 0:8] = inp["norm_g"][l].reshape(8, 128).T
    cw = inp["conv_w"][l]
    for cc in range(4):
        base = (0 if cc < 2 else 1024) + g * 256 + (cc % 2) * 128
        vec[:, 8 + cc * 4:12 + cc * 4] = cw[:, base:base + 128].T
    vec[:, 24] = inp["ml_bi"][l][g]
    vec[:, 25] = inp["ml_bf"][l][g]
    vec[:, 26] = inp["fox_bf"][l][2 * g]
    vec[:, 27] = inp["fox_bf"][l][2 * g + 1]
    rows = np.zeros((128, 512), np.float32)
    rows[:, 0:256] = inp["ml_norm_g"][l][g * 256:(g + 1) * 256][None, :]
    rows[:, 256:512] = inp["pool_scale"][l][g * 256:(g + 1) * 256][None, :]
    return dict(x=np.ascontiguousarray(x_b), w1=w1, vecs=vec, rows=rows,
                poolw=np.ascontiguousarray(inp["pool_w"][l][g]), consts=_consts(g))


def p2_record(P, nc, st, bank, T, last, xsrc, ysload, wg_d, wb_d, wo_d, vec_d, fg_d, id_d, odst, pre=(), after_tile=None):
    NT = T // 128
    if True:
        tag = f"sb{P.nflush}_"
        sb = lambda n, s, d: st.enter_context(nc.sbuf_tensor(tag + n, s, d))
        bkey = [f"bk{b}" for b in range(8)]
        Wg = sb("Wg", [128, 8, 3072], BF16)
        Wb = sb("Wb", [128, 24, 1024], BF16)
        Wo = sb("Wo", [128, 8, 1024], BF16)
        stg = [sb(f"stg{i}", [128, 8, 256], F32) for i in range(2)]
        vec = sb("vec2", [128, 8], F32)
        fg = sb("fg", [128, 1024], F32)
        idf = sb("idf", [128, 128], F32)
        idb = sb("idb", [128, 128], BF16)
        xt = [sb(f"xt{i}", [128, 1024], F32) for i in range(2)]
        yst = sb("yst", [128, 3072], BF16)
        ysT = sb("ysT", [128, 24, 128], BF16)
        junk_b = sb("junk_b", [128, 1024], BF16)
        ss = sb("ss", [128, 4], F32)
        xn = sb("xn", [128, 1024], BF16)
        hT = sb("hT", [128, 8, 128], BF16)
        gsb = sb("gsb", [128, 3072], F32)
        mrg = sb("mrg", [128, 1024], F32)
        tmp = sb("tmp", [128, 512], F32)
        mb = sb("mb", [128, 1024], BF16)
        mT = sb("mT", [128, 8, 128], BF16)
        xo = [sb(f"xo{i}", [128, 1024], F32) for i in range(2)]

        P.dma(vec[:], vec_d, writes=["vec"])
        P.dma(fg[:], fg_d, writes=["fg"])
        P.dma(idf[:], id_d, writes=["idf"])
        P.cp("dve", idb[:], idf[:], ["idf"], ["idb"])

        si = 0
        def load_w(dview, dest, nk, ncols, fold, key):
            nonlocal si
            for k0 in range(0, nk, 8):
                for c0 in range(0, ncols, 256):
                    s_ = stg[si % 2]
                    sk = ("stg", si % 2)
                    si += 1
                    P.dma(s_[:], dview[:, k0:k0 + 8, c0:c0 + 256], writes=[sk])
                    for k in range(8):
                        o_ = dest[:, k0 + k, c0:c0 + 256]
                        if fold:
                            if k % 2 == 0:
                                P.ts(o_, s_[:, k, :], vec[:, k:k + 1], None, ALU.mult, None, [sk, "vec"], [(key, k0 + k, c0)])
                            else:
                                P.act(o_, s_[:, k, :], AF.Copy, [sk, "vec"], [(key, k0 + k, c0)], scale=vec[:, k:k + 1])
                        else:
                            P.cp("dve" if k % 2 == 0 else "act", o_, s_[:, k, :], [sk], [(key, k0 + k, c0)])
        load_w(wg_d.rearrange("(k p) n -> p k n", p=128), Wg, 8, 3072, True, "Wg")
        load_w(wb_d.rearrange("(k p) n -> p k n", p=128), Wb, 24, 1024, False, "Wb")
        load_w(wo_d.rearrange("(k p) n -> p k n", p=128), Wo, 8, 1024, False, "Wo")
        wkeys = lambda key, k, c0, c1: [(key, k, c) for c in range(0, 4096, 256) if c < c1 and c + 256 > c0]

        mmb = 0
        trb = 0
        for i in range(NT):
            xb = xt[i % 2]
            xk = ("xt", i % 2)
            P.dma(xb[:], xsrc(i), reads=list(pre), writes=[xk])
            for (o_, i_) in ysload(i, yst):
                P.dma(o_, i_, reads=list(pre), writes=["yst"])
            P.act(junk_b[:], xb[:], AF.Square, [xk], ["junk_b", "ss"], accum_out=ss[:, 0:1])
            P.act(ss[:, 1:2], ss[:, 0:1], AF.Ln, ["ss"], ["ss1"], scale=1.0 / 1024, bias=EPS)
            P.act(ss[:, 1:2], ss[:, 1:2], AF.Exp, ["ss1"], ["ss1"], scale=-0.5)
            P.ts(xn[:], xb[:], ss[:, 1:2], None, ALU.mult, None, [xk, "ss1"], ["xn"])
            tb = 4 + trb % 2; trb += 1
            tbv = bank[tb][:].bitcast(BF16)
            for k in range(8):
                P.tr(tbv[:, k * 128:(k + 1) * 128], xn[:, k * 128:(k + 1) * 128], idb[:], ["xn", "idb"], [bkey[tb]])
            P.cp("act", hT[:], tbv[:, 0:1024].rearrange("p (k n) -> p k n", n=128), [bkey[tb]], ["hT"])
            for n in range(3):
                tb = 4 + trb % 2; trb += 1
                tbv = bank[tb][:].bitcast(BF16)
                for k in range(8):
                    c = n * 1024 + k * 128
                    P.tr(tbv[:, k * 128:(k + 1) * 128], yst[:, c:c + 128], idb[:], ["yst", "idb"], [bkey[tb]])
                P.cp("dve" if n % 2 == 0 else "act", ysT[:, n * 8:(n + 1) * 8, :],
                     tbv[:, 0:1024].rearrange("p (k n) -> p k n", n=128), [bkey[tb]], [("ysT", n)])
            for cg in range(6):
                b = mmb % 4; mmb += 1
                for k in range(8):
                    P.mm(bank[b][:, 0:512], hT[:, k, :], Wg[:, k, cg * 512:(cg + 1) * 512], k == 0, k == 7,
                         ["hT"] + wkeys("Wg", k, cg * 512, (cg + 1) * 512), [bkey[b]])
                P.act(gsb[:, cg * 512:(cg + 1) * 512], bank[b][:, 0:512], AF.Sigmoid, [bkey[b]], [("gsb", cg)])
            for n in range(3):
                for hf in range(2):
                    b = mmb % 4; mmb += 1
                    for k in range(8):
                        P.mm(bank[b][:, 0:512], ysT[:, n * 8 + k, :], Wb[:, n * 8 + k, hf * 512:(hf + 1) * 512], k == 0, k == 7,
                             [("ysT", n)] + wkeys("Wb", n * 8 + k, hf * 512, (hf + 1) * 512), [bkey[b]])
                    gv = gsb[:, n * 1024 + hf * 512:n * 1024 + (hf + 1) * 512]
                    gk = ("gsb", n * 2 + hf)
                    mk = ("mrg", hf)
                    mv = mrg[:, hf * 512:(hf + 1) * 512]
                    if n == 0:
                        P.tt(mv, gv, bank[b][:, 0:512], ALU.mult, [gk, bkey[b]], [mk])
                    else:
                        P.tt(tmp[:], gv, bank[b][:, 0:512], ALU.mult, [gk, bkey[b]], ["tmp"])
                        if n == 1:
                            P.tt(mv, mv, tmp[:], ALU.add, [mk, "tmp"], [mk], eng="pool")
                        else:
                            P.tt(mb[:, hf * 512:(hf + 1) * 512], mv, tmp[:], ALU.add, [mk, "tmp"], [("mb", hf)], eng="pool")
            tb = 4 + trb % 2; trb += 1
            tbv = bank[tb][:].bitcast(BF16)
            for k in range(8):
                P.tr(tbv[:, k * 128:(k + 1) * 128], mb[:, k * 128:(k + 1) * 128], idb[:], [("mb", k // 4), "idb"], [bkey[tb]])
            P.cp("act", mT[:], tbv[:, 0:1024].rearrange("p (k n) -> p k n", n=128), [bkey[tb]], ["mT"])
            ob = xo[i % 2]
            ok = ("xo", i % 2)
            for hf in range(2):
                b = mmb % 4; mmb += 1
                for k in range(8):
                    P.mm(bank[b][:, 0:512], mT[:, k, :], Wo[:, k, hf * 512:(hf + 1) * 512], k == 0, k == 7,
                         ["mT"] + wkeys("Wo", k, hf * 512, (hf + 1) * 512), [bkey[b]])
                P.tt(ob[:, hf * 512:(hf + 1) * 512], xb[:, hf * 512:(hf + 1) * 512], bank[b][:, 0:512], ALU.add,
                     [xk, bkey[b]], [ok])
            if last:
                P.act(junk_b[:], ob[:], AF.Square, [ok], ["junk_b", "ss2"], accum_out=ss[:, 2:3])
                P.act(ss[:, 3:4], ss[:, 2:3], AF.Ln, ["ss2"], ["ss3"], scale=1.0 / 1024, bias=EPS)
                P.act(ss[:, 3:4], ss[:, 3:4], AF.Exp, ["ss3"], ["ss3"], scale=-0.5)
                P.stt(ob[:], ob[:], ss[:, 3:4], fg[:], ALU.mult, ALU.mult, [ok, "ss3", "fg"], [ok])
            P.dma(odst(i), ob[:], reads=[ok], writes=[("o", i)])
            if after_tile is not None:
                after_tile(i)
    return [("o", i) for i in range(NT)]


def p2_inputs(x_sh, ys_sh, l, inp):
    o = 5 * 1024 + 8 + 4 * 1024 + 8 + 2 * 1024
    return dict(x=np.ascontiguousarray(x_sh), ysin=np.ascontiguousarray(ys_sh),
                wg=np.ascontiguousarray(inp["w_in"][l][:, o:o + 3072]),
                wb=np.ascontiguousarray(inp["w_branch"][l].reshape(3072, 1024)),
                wo=np.ascontiguousarray(inp["w_out"][l]),
                vecs=np.ascontiguousarray(inp["norm_g"][l].reshape(8, 128).T),
                fgrows=np.ascontiguousarray(np.broadcast_to(inp["final_g"][None, :], (128, 1024))),
                ident=np.eye(128, dtype=np.float32))


GROUPS = [[0, 1, 2, 3], [4, 5, 6, 7]]


class _QCtx:
    def __init__(self, h):
        self.q = h.partition_id() % 4
        self.cache = {}

    def mul(self, m):
        if m == 1:
            return self.q
        if m not in self.cache:
            self.cache[m] = self.q * m
        return self.cache[m]


def build_fused(S=SEQ):
    TS = S // 4
    nc = bass.Bass("TRN2", target_bir_lowering=False)
    dram = lambda n, sh, dt, kind: nc.dram_tensor(n, sh, dt, kind=kind).ap()
    x_d = dram("x", [S, 1024], F32, "ExternalInput")
    cst_d = dram("consts", [128, 1024], F32, "ExternalInput")
    fg_d = dram("fgrows", [128, 1024], F32, "ExternalInput")
    L = []
    for l in range(DEPTH):
        L.append(dict(
            w1=dram(f"w1_{l}", [1024, NW1], F32, "ExternalInput"),
            vec=dram(f"vecs_{l}", [128, 32], F32, "ExternalInput"),
            rows=dram(f"rows_{l}", [128, 512], F32, "ExternalInput"),
            pw=dram(f"poolw_{l}", [256, 256], F32, "ExternalInput"),
            wg=dram(f"wg_{l}", [1024, 3072], F32, "ExternalInput"),
            wb=dram(f"wb_{l}", [3072, 1024], F32, "ExternalInput"),
            wo=dram(f"wo_{l}", [1024, 1024], F32, "ExternalInput"),
        ))
    out_d = dram("out", [TS, 1024], F32, "ExternalOutput")
    ysrc = nc.dram_tensor("ysrc", [S, 768], BF16).ap()
    yall = nc.dram_tensor("yall", [4 * S, 768], BF16).ap()
    x1src = nc.dram_tensor("x1src", [TS, 1024], F32).ap()
    x1all = nc.dram_tensor("x1all", [S, 1024], F32).ap()
    ysh = nc.dram_tensor("ysh", [4 * TS, 768], BF16).ap()
    xsh = nc.dram_tensor("xsh", [TS, 1024], F32).ap()

    with contextlib.ExitStack() as top:
        bank = [top.enter_context(nc.psum_tensor(f"bk{b}", [128, 512], F32)) for b in range(8)]
        fin = top.enter_context(nc.sbuf_tensor("sb_fin", [1, 8], F32))
        P = Prog(nc, top)
        P.op("pool", lambda e: e.memset(fin[:], 0.0), (), ["fin0"])
        NB = TS // 512
        for l in range(DEPTH):
            d = L[l]
            last = (l == DEPTH - 1)
            with contextlib.ExitStack() as st:
                if l == 0:
                    xtile = lambda i: x_d[i * 128:(i + 1) * 128, :]
                else:
                    def xtile(i):
                        tok = i * 128
                        r, k, t = tok // TS, (tok % TS) // 256, tok % 256
                        row = k * 1024 + r * 256 + t
                        return x1all[row:row + 128, :]

                def after_st(j, keys):
                    P.cc("AllGather", GROUPS, yall[j * 2048:(j + 1) * 2048, :], ysrc[j * 512:(j + 1) * 512, :],
                         reads=keys, writes=[("yall", j)])
                p1_record(P, nc, st, bank, S, xtile, d["w1"], d["vec"], d["rows"], d["pw"], cst_d, ysrc, after_st=after_st)
                P.flush(fin, bank[7])
            with contextlib.ExitStack() as st:
                yv = yall.rearrange("(blk p r) c -> blk p (r c)", p=128, r=16)
                yshv = ysh.rearrange("(k p r) c -> k p (r c)", p=128, r=16)
                for jj in range(NB):
                    P.dma(yshv[jj:jj + 1], (lambda q, jj=jj: yv[jj:][bass.ds(q.mul(NB), 1)]), writes=["ysh"])
                if l == 0:
                    R = TS // 128
                    xv = x_d.rearrange("(blk p r) c -> blk p (r c)", p=128, r=R)
                    xshv = xsh.rearrange("(k p r) c -> k p (r c)", p=128, r=R)
                    P.dma(xshv[0:1], (lambda q: xv[bass.ds(q.mul(1), 1)]), writes=["xsh"])
                    xsrc = lambda i: xsh[i * 128:(i + 1) * 128, :]
                else:
                    xsrc = lambda i: x1src[i * 128:(i + 1) * 128, :]

                def ysload(i, yst):
                    prs = []
                    jj, t0 = i // 4, (i % 4) * 128
                    for r in range(4):
                        o_ = yst[:].rearrange("p (n r c) -> p n r c", n=3, r=4)[:, :, r, :]
                        row = jj * 2048 + r * 512 + t0
                        i_ = ysh[row:row + 128, :].rearrange("p (n c) -> p n c", n=3)
                        prs.append((o_, i_))
                    return prs
                if last:
                    odst = lambda i: out_d[i * 128:(i + 1) * 128, :]
                    after_tile = None
                else:
                    odst = lambda i: x1src[i * 128:(i + 1) * 128, :]

                    def after_tile(i):
                        if i % 2 == 1:
                            k = i // 2
                            P.cc("AllGather", GROUPS, x1all[k * 1024:(k + 1) * 1024, :], x1src[k * 256:(k + 1) * 256, :],
                                 reads=[("o", i - 1), ("o", i)], writes=[("x1all", k)])
                p2_record(P, nc, st, bank, TS, last, xsrc, ysload, d["wg"], d["wb"], d["wo"],
                          d["vec"][:, 0:8], fg_d, cst_d[:, 0:128], odst, pre=["ysh", "xsh"], after_tile=after_tile)
                P.flush(fin, bank[7])
    return nc


_CACHE = {}


def kernel(x, norm_g, w_in, conv_w, ml_bi, ml_bf, ml_norm_g, fox_bf, pool_w, pool_scale, w_branch, w_out, final_g):
    inp = dict(norm_g=np.asarray(norm_g), w_in=np.asarray(w_in), conv_w=np.asarray(conv_w), ml_bi=np.asarray(ml_bi),
               ml_bf=np.asarray(ml_bf), ml_norm_g=np.asarray(ml_norm_g), fox_bf=np.asarray(fox_bf),
               pool_w=np.asarray(pool_w), pool_scale=np.asarray(pool_scale), w_branch=np.asarray(w_branch),
               w_out=np.asarray(w_out), final_g=np.asarray(final_g))
    x = np.asarray(x, dtype=np.float32)
    B, S, _ = x.shape
    TS = S // 4
    if "nc" not in _CACHE:
        _CACHE["nc"] = build_fused(S)
    nc = _CACHE["nc"]
    go = 5 * 1024 + 8 + 4 * 1024 + 8 + 2 * 1024
    fgrows = np.ascontiguousarray(np.broadcast_to(inp["final_g"][None, :], (128, 1024)))
    ins = []
    for c in range(8):
        b, g = c // 4, c % 4
        m = dict(x=np.ascontiguousarray(x[b]), consts=_consts(g), fgrows=fgrows)
        for l in range(DEPTH):
            p1 = p1_inputs(x[b], l, g, inp)
            m[f"w1_{l}"] = p1["w1"]
            m[f"vecs_{l}"] = p1["vecs"]
            m[f"rows_{l}"] = p1["rows"]
            m[f"poolw_{l}"] = p1["poolw"]
            m[f"wg_{l}"] = np.ascontiguousarray(inp["w_in"][l][:, go:go + 3072])
            m[f"wb_{l}"] = np.ascontiguousarray(inp["w_branch"][l].reshape(3072, 1024))
            m[f"wo_{l}"] = np.ascontiguousarray(inp["w_out"][l])
        ins.append(m)
    res = run_bass_kernel_spmd(nc, ins, core_ids=list(range(8)))
    out = np.empty((B, S, 1024), np.float32)
    for c in range(8):
        b, q = c // 4, c % 4
        out[b, q * TS:(q + 1) * TS] = np.asarray(res.results[c]["out"])
    return out
```

```python
import contextlib
import numpy as np
import ml_dtypes
import concourse.bass as bass
import concourse.mybir as mybir
from concourse.bass_utils import run_bass_kernel_spmd

F32 = mybir.dt.float32
BF16 = mybir.dt.bfloat16
AF = mybir.ActivationFunctionType
ALU = mybir.AluOpType

D_MODEL = 1024
SEQ = 8192
BATCH = 2
DEPTH = 2
N_IN = 14352
POOL_WINDOWS = (2, 4, 8, 16)
EPS = 1e-6

EPOCH = 20000
SAME_ENGINE_SYNC = True


class Prog:
    CE = ("pe", "act", "dve", "pool")
    ALLE = ("pe", "act", "dve", "pool", "sp")
    NEP = 4

    def __init__(self, nc, stack):
        self.nc = nc
        self.NDS = 24
        self.NCC = 8
        self.csem = {e: [stack.enter_context(nc.semaphore(f"s_{e}{i}")) for i in range(self.NEP)] for e in self.CE}
        self.dsem = [stack.enter_context(nc.semaphore(f"s_d{i}")) for i in range(self.NDS)]
        self.ccsem = [stack.enter_context(nc.semaphore(f"s_cc{i}")) for i in range(self.NCC)]
        self.sigcnt = {e: 0 for e in self.CE}
        self.dma_cnt = [0] * self.NDS
        self.dma_rr = 0
        self.ncc = 0
        self.nflush = 0
        self._reset()

    def _reset(self):
        self.ops = {e: [] for e in self.ALLE}
        self.last_write = {}
        self.readers = {}
        self.seen = {e: {} for e in self.ALLE}
        self.signaled = {e: set() for e in self.CE}
        self.cc_pending = []

    def _need(self, eng, ev, waits):
        if ev is None:
            return
        if ev[0] == "c":
            _, e2, idx = ev
            if e2 == eng and (eng == "pe" or not SAME_ENGINE_SYNC):
                return
            k = ("c", e2)
            if self.seen[eng].get(k, -1) >= idx:
                return
            self.seen[eng][k] = idx
            waits.append(ev)
            self.signaled[e2].add(idx)
        elif ev[0] == "d":
            _, slot, cnt = ev
            k = ("d", slot)
            if self.seen[eng].get(k, 0) >= cnt:
                return
            self.seen[eng][k] = cnt
            waits.append(ev)
        else:
            k = ("x", ev[1])
            if k in self.seen[eng]:
                return
            self.seen[eng][k] = 1
            waits.append(ev)

    def _deps(self, eng, reads, writes):
        waits = []
        for r in reads:
            self._need(eng, self.last_write.get(r), waits)
        for w in writes:
            self._need(eng, self.last_write.get(w), waits)
            for ev in self.readers.get(w, ()):
                self._need(eng, ev, waits)
        return waits

    def _commit(self, ev, reads, writes):
        for r in reads:
            self.readers.setdefault(r, []).append(ev)
        for w in writes:
            self.last_write[w] = ev
            self.readers[w] = []

    def capture_begin(self):
        self._cap = []

    def capture_end(self):
        c, self._cap = self._cap, None
        return c

    def op(self, eng, fn, reads=(), writes=()):
        if getattr(self, "_cap", None) is not None:
            self._cap.append((eng, fn, reads, writes))
            return
        bk = [r for r in reads if isinstance(r, str) and r.startswith("bk")]
        if bk:
            writes = list(writes) + [b for b in bk if b not in writes]
            reads = [r for r in reads if r not in bk]
        waits = self._deps(eng, reads, writes)
        idx = len(self.ops[eng])
        self.ops[eng].append(dict(kind="c", fn=fn, waits=waits))
        self._commit(("c", eng, idx), reads, writes)

    def dma(self, out, in_, reads=(), writes=(), q="sp"):
        slot = self.dma_rr
        self.dma_rr = (self.dma_rr + 1) % self.NDS
        waits = self._deps(q, reads, writes)
        if self.dma_cnt[slot] > 0:
            self._need(q, ("d", slot, self.dma_cnt[slot]), waits)
        self.dma_cnt[slot] += 1
        ev = ("d", slot, self.dma_cnt[slot])
        self.ops[q].append(dict(kind="d", out=out, in_=in_, waits=waits, slot=slot))
        self._commit(ev, reads, writes)

    def cc(self, kind, groups, out, in_, reads=(), writes=()):
        waits = self._deps("pool", reads, writes)
        n = self.ncc
        self.ncc += 1
        ev = ("x", n)
        self.ops["pool"].append(dict(kind="cc", cck=kind, groups=groups, out=out, in_=in_, waits=waits, n=n))
        self.cc_pending.append(ev)
        self._commit(ev, reads, writes)

    def mm(self, out, lhsT, rhs, start, stop, reads, writes):
        self.op("pe", lambda e: e.matmul(out, lhsT=lhsT, rhs=rhs, start=start, stop=stop, skip_group_check=True),
                reads, writes)

    def tr(self, out, in_, ident, reads, writes):
        self.op("pe", lambda e: e.transpose(out=out, in_=in_, identity=ident), reads, writes)

    def act(self, out, in_, func, reads, writes, **kw):
        self.op("act", lambda e: e.activation(out=out, in_=in_, func=func, **kw), reads, writes)

    def ts(self, out, in0, s1, s2, op0, op1, reads, writes, eng="dve"):
        if op1 is None:
            self.op(eng, lambda e: e.tensor_scalar(out=out, in0=in0, scalar1=s1, scalar2=None, op0=op0), reads, writes)
        else:
            self.op(eng, lambda e: e.tensor_scalar(out=out, in0=in0, scalar1=s1, scalar2=s2, op0=op0, op1=op1),
                    reads, writes)

    def tt(self, out, in0, in1, op, reads, writes, eng="dve"):
        self.op(eng, lambda e: e.tensor_tensor(out=out, in0=in0, in1=in1, op=op), reads, writes)

    def stt(self, out, in0, scalar, in1, op0, op1, reads, writes, eng="dve"):
        self.op(eng, lambda e: e.scalar_tensor_tensor(out=out, in0=in0, scalar=scalar, in1=in1, op0=op0, op1=op1),
                reads, writes)

    def cp(self, eng, out, in_, reads, writes):
        if eng == "act":
            self.op("act", lambda e: e.copy(out=out, in_=in_), reads, writes)
        else:
            self.op(eng, lambda e: e.tensor_copy(out=out, in_=in_), reads, writes)

    def flush(self, fin, bank7):
        nc = self.nc
        fk = [("fin", e) for e in self.CE]
        waits = []
        for ev in self.cc_pending:
            self._need("pool", ev, waits)
        self.ops["pool"].append(dict(kind="w", waits=waits))
        self.op("act", lambda e: e.copy(out=fin[0:1, 0:1], in_=fin[0:1, 1:2]), ["fin0"], [fk[1]])
        self.op("dve", lambda e: e.memset(fin[0:1, 2:3], 0.0), (), [fk[2]])
        self.op("pool", lambda e: e.memset(fin[0:1, 3:4], 0.0), (), [fk[3]])
        self.op("pe", lambda e: e.matmul(bank7[0:1, 511:512], lhsT=fin[0:1, 4:5], rhs=fin[0:1, 5:6], start=True, stop=True,
                                         skip_group_check=True), ["bk7", "fin0"], [fk[0]])
        for e in self.ALLE:
            waits = []
            for k in fk:
                self._need(e, self.last_write.get(k), waits)
            for slot in range(self.NDS):
                if self.dma_cnt[slot] > 0:
                    self._need(e, ("d", slot, self.dma_cnt[slot]), waits)
            self.ops[e].append(dict(kind="w", waits=waits))
        semval = {}
        for e in self.CE:
            c = self.sigcnt[e]
            for idx in range(len(self.ops[e])):
                if idx in self.signaled[e]:
                    semval[(e, idx)] = (c // EPOCH, c % EPOCH + 1)
                    c += 1
            self.sigcnt[e] = c
            assert c <= EPOCH * self.NEP, (e, c)
        csem, dsem, ccsem = self.csem, self.dsem, self.ccsem
        need_q = any(o["kind"] == "d" and (callable(o["out"]) or callable(o["in_"])) for o in self.ops["sp"])
        with nc.Block() as block:
            def run(e):
                def body(h):
                    qv = None
                    if e == "sp" and need_q:
                        qv = _QCtx(h)
                    for i, o in enumerate(self.ops[e]):
                        for ev in o["waits"]:
                            if ev[0] == "c":
                                ep, v = semval[(ev[1], ev[2])]
                                h.wait_ge(csem[ev[1]][ep], v)
                            elif ev[0] == "d":
                                h.wait_ge(dsem[ev[1]], 16 * ev[2])
                            else:
                                h.wait_ge(ccsem[ev[1] % self.NCC], ev[1] // self.NCC + 1)
                        if o["kind"] == "c":
                            ins = o["fn"](h)
                            if (e, i) in semval:
                                ep, v = semval[(e, i)]
                                ins.then_inc(csem[e][ep], 1)
                        elif o["kind"] == "d":
                            o_ = o["out"](qv) if callable(o["out"]) else o["out"]
                            i_ = o["in_"](qv) if callable(o["in_"]) else o["in_"]
                            try:
                                h.dma_start(out=o_, in_=i_).then_inc(dsem[o["slot"]], 16)
                            except Exception:
                                print("DMA FAIL", o_, i_, flush=True)
                                raise
                        elif o["kind"] == "cc":
                            h.collective_compute(o["cck"], ALU.bypass, replica_groups=o["groups"], ins=[o["in_"].opt()],
                                                 outs=[o["out"].opt()]).then_inc(ccsem[o["n"] % self.NCC], 1)
                return body
            block.sync(run("sp"))
            block.tensor(run("pe"))
            block.scalar(run("act"))
            block.vector(run("dve"))
            block.gpsimd(run("pool"))
        self.nflush += 1
        self._reset()


NW1 = 2820
NFM = 1024


def p1_record(P, nc, st, bank, S, xtile, w1_d, vec_d, row_d, pw_d, cst_d, ys_d, after_st=None, stage=9, dbg=()):
    NT = S // 128
    NST = S // 512
    GT = (NT + 2) // 3
    if True:
        tag = f"sb{P.nflush}_"
        sb = lambda n, s, d: st.enter_context(nc.sbuf_tensor(tag + n, s, d))
        bkey = [f"bk{b}" for b in range(8)]
        b5b = bank[5][:].bitcast(BF16)

        cst_f = sb("cst_f", [128, 8, 128], F32)
        cst_b = sb("cst_b", [128, 8, 128], BF16)
        vec = sb("vec", [128, 32], F32)
        rows = sb("rows", [128, 512], F32)
        pw_b = sb("pw_b", [128, 2, 256], BF16)
        W1 = sb("W1", [128, 8, NW1], BF16)
        stg = [sb(f"stg{i}", [128, 8, 128], F32) for i in range(2)]
        KT = sb("KT", [128, 2, S], BF16)
        VA = sb("VA", [128, NT, 2, 129], BF16)
        KF = sb("KF", [128, 2, GT * 128], BF16)
        hT = sb("hT", [128, 8, 512], BF16)
        QT = sb("QT", [128, 2, 512], BF16)
        QF = sb("QF", [128, 2, 512], BF16)
        cin = sb("cin", [128, 515], F32)
        halo = sb("halo", [128, 4, 3], F32)
        ctmp = sb("ctmp", [128, 512], F32)
        csig = sb("csig", [128, 512], F32)
        qkT = sb("qkT", [128, 4, 512], BF16)
        xt = [sb(f"xt{i}", [128, 1024], F32) for i in range(2)]
        ss = sb("ss", [128, 2], F32)
        xn = sb("xn", [128, 1024], BF16)
        junk_b = xn
        gateB = [sb(f"gateB{i}", [128, 256], F32) for i in range(4)]
        VAm4 = [sb(f"VAm{i}", [128, 257], BF16) for i in range(4)]
        gateA4 = [sb(f"gateA{i}", [128, 256], F32) for i in range(4)]
        gateC4 = [sb(f"gateC{i}", [128, 256], F32) for i in range(4)]
        ut = [sb(f"ut{i}", [128, 256], BF16) for i in range(5)]
        sp4 = [sb(f"sp{i}", [128, 4], F32) for i in range(4)]
        cs4 = [sb(f"cs{i}", [128, 12], F32) for i in range(4)]
        ls4 = [sb(f"ls{i}", [128, 4], F32) for i in range(4)]
        sgs = [sb("sg0", [128, 256], F32), sb("sg1", [128, 256], F32)]
        sg = sgs[0]
        sm = sb("sm", [128, 32], F32)
        Fcar = sb("Fcar", [128, 2], F32)
        f3b = sb("f3b", [128, 2, 3], BF16)
        f3n = sb("f3n", [128, 2, 3], BF16)
        fw = sb("fw", [128, 8], F32)
        qfk = sb("qfk", [128, 2, 2, 128], BF16)
        CT = sb("CT", [128, 2, 257], F32)
        CTb = sb("CTb", [128, 2, 257], BF16)
        Bm = sb("Bm", [128, 128], F32)
        ET = sb("ET", [128, 128], F32)
        PTm = sb("PTm", [128, 128], BF16)
        Ktok = sb("Ktok", [128, 256], BF16)
        Vw = sb("Vw", [128, 257], BF16)
        Gs = sb("Gs", [128, 257], F32)
        dTb = sb("dTb", [128, 2, 128], BF16)
        PT = [sb(f"PT{i}", [128, 512], BF16) for i in range(2)]
        yst = [sb(f"yst{i}", [128, 768], BF16) for i in range(4)]

        IDb = cst_b[:, 0, :]
        IDf = cst_f[:, 0, :]
        UIN = cst_f[:, 1, :]
        UREV = cst_f[:, 2, :]
        ONESf = cst_f[:, 3, :]
        MASKb = cst_b[:, 4, :]
        MCUR = cst_b[:, 5, :]
        MFIRST = cst_b[:, 6, :]
        MPREV = cst_b[:, 7, :]

        P.dma(cst_f[:], cst_d.rearrange("p (c n) -> p c n", n=128), writes=["cst_f"])
        P.dma(vec[:], vec_d, writes=["vec"])
        P.dma(rows[:], row_d, writes=["rows"])
        P.cp("dve", cst_b[:], cst_f[:], ["cst_f"], ["cst_b"])
        pwst = stg[0][:].rearrange("p k n -> p (k n)")[:, 0:512].rearrange("p (k n) -> p k n", n=256)
        P.dma(pwst, pw_d.rearrange("(k p) n -> p k n", p=128), writes=[("stg", 0)])
        P.cp("dve", pw_b[:], pwst, [("stg", 0)], ["pw_b"])
        P.op("pool", lambda e: e.memset(VA[:], 1.0), (), [("VA", j) for j in range(NST)])
        for i4 in range(4):
            P.op("pool", lambda e, i4=i4: e.memset(VAm4[i4][:], 1.0), (), [("VAm", i4)])
        P.op("pool", lambda e: e.memset(halo[:], 0.0), (), [("halo", c) for c in range(4)])
        P.op("pool", lambda e: e.memset(Fcar[:], 0.0), (), ["Fcar"])
        P.op("pool", lambda e: e.memset(CT[:], 0.0), (), ["CT"])
        P.op("pool", lambda e: e.memset(CTb[:], 0.0), (), ["CTb"])
        P.op("pool", lambda e: e.memset(qfk[:], 0.0), (), ["qfk"])
        P.op("pool", lambda e: e.memset(KF[:], 0.0), (), [("KF", j) for j in range(NST)])
        for h in range(2):
            qv = qfk[:, h, 0, :].rearrange("p (a c) -> p a c", c=32)
            kv = qfk[:, h, 1, :].rearrange("p (a c) -> p a c", c=32)
            P.op("pool", lambda e, qv=qv: e.memset(qv[:, :, 3:6], 1.0), (), ["qfk"])
            P.op("pool", lambda e, kv=kv: e.memset(kv[:, :, 0:3], 1.0), (), ["qfk"])

        chunks = [(c0, min(c0 + 128, NW1)) for c0 in range(0, NW1, 128)]
        w1v = w1_d.rearrange("(k p) n -> p k n", p=128)
        for ci, (c0, c1) in enumerate(chunks if 'now' not in dbg else []):
            s_ = stg[ci % 2]
            n = c1 - c0
            P.dma(s_[:, :, 0:n], w1v[:, :, c0:c1], writes=[("stg", ci % 2)])
            for k in range(8):
                if k % 2 == 0:
                    P.ts(W1[:, k, c0:c1], s_[:, k, 0:n], vec[:, k:k + 1], None, ALU.mult, None,
                         [("stg", ci % 2), "vec"], [("W1", ci, k)])
                else:
                    P.act(W1[:, k, c0:c1], s_[:, k, 0:n], AF.Copy, [("stg", ci % 2), "vec"], [("W1", ci, k)],
                          scale=vec[:, k:k + 1])

        def w1keys(c0, c1, k):
            return [("W1", ci, k) for ci, (a, b) in enumerate(chunks) if a < c1 and b > c0]

        prot = 0
        for j in range(NST):
            for a in range(4 if 'noA' not in dbg else 0):
                i = 4 * j + a
                xb = xt[i % 2]
                xk = ("xt", i % 2)
                P.dma(xb[:], xtile(i), writes=[xk])
                P.act(junk_b[:], xb[:], AF.Square, [xk], ["xn", "ss"], accum_out=ss[:, 0:1])
                P.act(ss[:, 1:2], ss[:, 0:1], AF.Ln, ["ss"], ["ss1"], scale=1.0 / 1024, bias=EPS)
                P.act(ss[:, 1:2], ss[:, 1:2], AF.Exp, ["ss1"], ["ss1"], scale=-0.5)
                P.ts(xn[:], xb[:], ss[:, 1:2], None, ALU.mult, None, [xk, "ss1"], ["xn"])
                for k in range(8):
                    P.tr(b5b[:, k * 128:(k + 1) * 128], xn[:, k * 128:(k + 1) * 128], IDb, ["xn", "cst_b"], [bkey[5]])
                P.cp("act", hT[:, :, a * 128:(a + 1) * 128], b5b[:, 0:1024].rearrange("p (k n) -> p k n", n=128),
                     [bkey[5]], ["hT"])

            PROT = [4, 0, 1, 2, 3]
            for c in range(8 if stage >= 1 else 0):
                pb = PROT[prot % 5]; prot += 1
                bank4, bkey4 = bank[pb], bkey[pb]
                for k in range(8):
                    P.mm(bank4[:, 0:512], W1[:, k, c * 128:(c + 1) * 128], hT[:, k, :], k == 0, k == 7,
                         ["hT"] + w1keys(c * 128, (c + 1) * 128, k), [bkey4])
                if c < 2:
                    P.act(QT[:, c, :], bank4[:, 0:512], AF.Copy, [bkey4], ["QT"], scale=128 ** -0.5)
                elif c < 4:
                    P.cp("dve", KT[:, c - 2, j * 512:(j + 1) * 512], bank4[:, 0:512], [bkey4], [("KT", j)])
                else:
                    cc = c - 4
                    P.cp("act", cin[:, 3:515], bank4[:, 0:512], [bkey4], ["cin"])
                    P.cp("dve", cin[:, 0:3], halo[:, cc, :], [("halo", cc)], ["cin"])
                    P.ts(ctmp[:], cin[:, 0:512], vec[:, 8 + cc * 4:9 + cc * 4], None, ALU.mult, None,
                         ["cin", "vec"], ["ctmp"])
                    for t in range(1, 4):
                        P.stt(ctmp[:], cin[:, t:t + 512], vec[:, 8 + cc * 4 + t:9 + cc * 4 + t], ctmp[:],
                              ALU.mult, ALU.add, ["cin", "vec", "ctmp"], ["ctmp"])
                    P.cp("dve", halo[:, cc, :], cin[:, 512:515], ["cin"], [("halo", cc)])
                    P.act(csig[:], ctmp[:], AF.Sigmoid, ["ctmp"], ["csig"])
                    P.stt(qkT[:, cc, :], ctmp[:], (256 ** -0.5) if cc < 2 else 1.0, csig[:], ALU.mult, ALU.mult,
                          ["ctmp", "csig"], ["qkT"])

            for a in range(4 if stage >= 2 else 0):
                i = 4 * j + a
                ts_ = slice(a * 128, (a + 1) * 128)
                VAm, gateA, gateC, sp_, cs, lsv = VAm4[a], gateA4[a], gateC4[a], sp4[a], cs4[a], ls4[a]
                kVAm, kgA, kgC, ksp, kcs, kls = ("VAm", a), ("gateA", a), ("gateC", a), ("sp", a), ("cs", a), ("ls", a)
                groups = [(0, 512), (512, 1024), (1024, 1536), (1536, 1796)]
                for gi, (c0, c1) in enumerate(groups):
                    n = c1 - c0
                    pb = PROT[prot % 5]; prot += 1
                    bank4, bkey4 = bank[pb], bkey[pb]
                    sg = sgs[prot % 2]
                    sgk = ("sg", prot % 2)
                    for k in range(8):
                        P.mm(bank4[:, 0:n], hT[:, k, ts_], W1[:, k, NFM + c0:NFM + c1], k == 0, k == 7,
                             ["hT"] + w1keys(NFM + c0, NFM + c1, k), [bkey4])
                    lo = bank4[:, 0:256]
                    hi = bank4[:, 256:512]
                    if gi == 0:
                        P.cp("dve", VA[:, i, :, 0:128], lo.rearrange("p (h d) -> p h d", d=128), [bkey4], [("VA", j)])
                        P.act(sg[:], hi, AF.Sigmoid, [bkey4], [sgk])
                        P.tt(gateB[a][:], sg[:], hi, ALU.mult, [sgk, bkey4], [("gateB", a)])
                    elif gi == 1:
                        P.cp("dve", VAm[:, 0:256], lo, [bkey4], [kVAm])
                        P.act(sg[:], hi, AF.Sigmoid, [bkey4], [sgk])
                        P.tt(gateA[:], sg[:], rows[:, 0:256], ALU.mult, [sgk, "rows"], [kgA])
                    elif gi == 2:
                        P.act(sg[:], lo, AF.Sigmoid, [bkey4], [sgk])
                        P.tt(sg[:], sg[:], lo, ALU.mult, [sgk, bkey4], [sgk])
                        P.tt(gateA[:], gateA[:], sg[:], ALU.mult, [sgk, kgA], [kgA])
                        P.cp("act", ut[i % 5][:], hi, [bkey4], [("ut", i % 5)])
                    else:
                        P.act(sg[:], lo, AF.Sigmoid, [bkey4], [sgk])
                        P.tt(sg[:], sg[:], lo, ALU.mult, [sgk, bkey4], [sgk])
                        P.tt(gateC[:], sg[:], rows[:, 256:512], ALU.mult, [sgk, "rows"], [kgC])
                        P.tt(sp_[:], vec[:, 24:28], bank4[:, 256:260], ALU.add, ["vec", bkey4], [ksp])

                if stage < 3:
                    continue
                P.act(sm[:, 0:3], sp_[:, 1:4], AF.Abs, [ksp], ["sm0"])
                P.act(sm[:, 3:6], sm[:, 0:3], AF.Exp, ["sm0"], ["sm3"], scale=-1.0)
                P.act(sm[:, 6:9], sm[:, 3:6], AF.Ln, ["sm3"], ["sm6"], bias=1.0)
                P.ts(sm[:, 9:12], sp_[:, 1:4], 0.0, None, ALU.min, None, [ksp], ["sm9"])
                P.tt(lsv[:, 0:3], sm[:, 9:12], sm[:, 6:9], ALU.subtract, ["sm9", "sm6"], [kls])
                ls = lsv[:, 0:3]
                P.mm(bank[7][:, 260:263], UIN, ls, True, True, [kls, "cst_f"], [bkey[7]])
                P.mm(bank[7][:, 263:264], UREV, lsv[:, 0:1], True, True, [kls, "cst_f"], [bkey[7]])
                P.mm(bank[7][:, 264:267], ONESf, ls, True, True, [kls, "cst_f"], [bkey[7]])
                P.cp("dve", cs[:, 0:7], bank[7][:, 260:267], [bkey[7]], [kcs])
                P.tt(fw[:, 0:2], cs[:, 1:3], Fcar[:], ALU.add, [kcs, "Fcar"], ["fw0"])
                P.tt(Fcar[:], Fcar[:], cs[:, 5:7], ALU.add, [kcs, "Fcar"], ["Fcar"])
                P.cp("dve", f3b[:, :, 0], fw[:, 0:2], ["fw0"], ["f3b0"])
                P.cp("dve", fw[:, 2:4], f3b[:, :, 0], ["f3b0"], ["fw2"])
                P.tt(fw[:, 4:6], fw[:, 0:2], fw[:, 2:4], ALU.subtract, ["fw0", "fw2"], ["fw4"])
                P.cp("dve", f3b[:, :, 1], fw[:, 4:6], ["fw4"], ["f3b1"])
                P.cp("dve", fw[:, 2:4], f3b[:, :, 1], ["f3b1"], ["fw2"])
                P.tt(fw[:, 6:8], fw[:, 4:6], fw[:, 2:4], ALU.subtract, ["fw4", "fw2"], ["fw6"])
                P.cp("dve", f3b[:, :, 2], fw[:, 6:8], ["fw6"], ["f3b2"])
                P.ts(f3n[:], f3b[:], -1.0, None, ALU.mult, None, ["f3b0", "f3b1", "f3b2"], ["f3n"])
                ak = i // GT
                for h in range(2):
                    qv = qfk[:, h, 0, :].rearrange("p (a c) -> p a c", c=32)
                    for a4 in range(4):
                        P.cp("dve", qv[:, a4, 0:3], f3b[:, h, :], ["f3b0", "f3b1", "f3b2"], ["qfk"])
                    P.cp("dve", qfk[:, h, 1, 32 * ak + 3:32 * ak + 6], f3n[:, h, :], ["f3n"], ["qfk"])
                for h in range(2):
                    for w in range(2):
                        P.tr(b5b[:, (2 * h + w) * 128:(2 * h + w + 1) * 128], qfk[:, h, w, :], IDb,
                             ["qfk", "cst_b"], [bkey[5]])
                for h in range(2):
                    P.cp("act", QF[:, h, ts_], b5b[:, (2 * h) * 128:(2 * h + 1) * 128], [bkey[5]], ["QF"])
                    kcol = (i % GT) * 128
                    P.cp("dve", KF[32 * ak:32 * ak + 6, h, kcol:kcol + 128],
                         b5b[32 * ak:32 * ak + 6, (2 * h + 1) * 128:(2 * h + 2) * 128], [bkey[5]], [("KF", j)])

            P.capture_begin()
            for a in range(4 if stage >= 4 else 0):
                i = 4 * j + a
                ts_ = slice(a * 128, (a + 1) * 128)
                VAm, gateA, gateC, sp_, cs, lsv = VAm4[a], gateA4[a], gateC4[a], sp4[a], cs4[a], ls4[a]
                kVAm, kgA, kgC, ksp, kcs, kls = ("VAm", a), ("gateA", a), ("gateC", a), ("sp", a), ("cs", a), ("ls", a)
                P.mm(bank[6][:, 0:128], qkT[:, 2, ts_], qkT[:, 0, ts_], True, False, ["qkT"], [bkey[6]])
                P.mm(bank[6][:, 0:128], qkT[:, 3, ts_], qkT[:, 1, ts_], False, True, ["qkT"], [bkey[6]])
                P.ts(Bm[:], UIN, lsv[:, 0:1], None, ALU.mult, None, ["cst_f", kls], ["Bm"])
                P.mm(bank[6][:, 128:256], UREV, Bm[:], True, False, ["Bm", "cst_f"], [bkey[6]])
                P.mm(bank[6][:, 128:256], IDb, MASKb, False, True, ["cst_b"], [bkey[6]])
                P.act(ET[:], bank[6][:, 128:256], AF.Exp, [bkey[6], ksp], ["ET"], bias=sp_[:, 0:1])
                P.tt(PTm[:], ET[:], bank[6][:, 0:128], ALU.mult, ["ET", bkey[6]], ["PTm"])
                P.tr(b5b[:, 0:128], qkT[:, 2, ts_], IDb, ["qkT", "cst_b"], [bkey[5]])
                P.tr(b5b[:, 128:256], qkT[:, 3, ts_], IDb, ["qkT", "cst_b"], [bkey[5]])
                P.cp("act", Ktok[:], b5b[:, 0:256], [bkey[5]], ["Ktok"])
                P.act(sm[:, 16:17], cs[:, 0:1], AF.Exp, [kcs], ["sm16"])
                P.act(sm[:, 17:18], cs[:, 3:4], AF.Exp, [kcs, ksp], ["sm17"], bias=sp_[:, 0:1])
                P.act(sm[:, 18:19], cs[:, 4:5], AF.Exp, [kcs], ["sm18"])
                P.ts(Vw[:], VAm[:], sm[:, 17:18], None, ALU.mult, None, [kVAm, "sm17"], ["Vw"])
                P.mm(bank[7][:, 0:257], qkT[:, 0, ts_], CTb[:, 0, :], True, False, ["qkT", "CTb"], [bkey[7]])
                P.mm(bank[7][:, 0:257], qkT[:, 1, ts_], CTb[:, 1, :], False, True, ["qkT", "CTb"], [bkey[7]])
                P.act(Gs[:], bank[7][:, 0:257], AF.Copy, [bkey[7], "sm16"], ["Gs"], scale=sm[:, 16:17])
                P.mm(bank[7][:, 0:257], PTm[:], VAm[:], True, True, ["PTm", kVAm], [bkey[7]])
                P.tt(Gs[:], Gs[:], bank[7][:, 0:257], ALU.add, ["Gs", bkey[7]], ["Gs"])
                for c in range(2):
                    P.mm(bank[7][:, 0:257], Ktok[:, c * 128:(c + 1) * 128], Vw[:], True, True, ["Ktok", "Vw"], [bkey[7]])
                    P.stt(CT[:, c, :], CT[:, c, :], sm[:, 18:19], bank[7][:, 0:257], ALU.mult, ALU.add,
                          ["CT", "sm18", bkey[7]], ["CT"])
                    P.cp("act", CTb[:, c, :], CT[:, c, :], ["CT"], ["CTb"])
                P.act(sm[:, 19:20], Gs[:, 256:257], AF.Abs, ["Gs"], ["sm19"])
                P.ts(sm[:, 20:21], sm[:, 19:20], 1.0, None, ALU.max, None, ["sm19"], ["sm20"])
                P.op("dve", lambda e: e.reciprocal(out=sm[:, 21:22], in_=sm[:, 20:21]), ["sm20"], ["sm21"])
                P.act(sgs[0][:], Gs[:, 0:256], AF.Square, ["Gs"], [("sg", 0), "sm22"], accum_out=sm[:, 22:23])
                P.tt(sm[:, 23:24], sm[:, 21:22], sm[:, 21:22], ALU.mult, ["sm21"], ["sm23"])
                P.tt(sm[:, 24:25], sm[:, 23:24], sm[:, 22:23], ALU.mult, ["sm23", "sm22"], ["sm24"])
                P.act(sm[:, 25:26], sm[:, 24:25], AF.Ln, ["sm24"], ["sm25"], scale=1.0 / 256, bias=EPS)
                P.act(sm[:, 26:27], sm[:, 25:26], AF.Exp, ["sm25"], ["sm26"], scale=-0.5)
                P.tt(sm[:, 27:28], sm[:, 26:27], sm[:, 21:22], ALU.mult, ["sm26", "sm21"], ["sm27"])
                yk = ("yst", a)
                P.stt(yst[a][:, 0:256], Gs[:, 0:256], sm[:, 27:28], gateA[:], ALU.mult, ALU.mult,
                      ["Gs", "sm27", kgA], [yk])

                if stage < 5:
                    continue
                ucur = ut[i % 5]
                uprev = ut[(i - 1) % 5]
                for cc in range(2):
                    o_ = bank[6][:, 256 + cc * 128:384 + cc * 128]
                    if i == 0:
                        P.mm(o_, ucur[:, cc * 128:(cc + 1) * 128], MFIRST, True, True, [("ut", i % 5), "cst_b"], [bkey[6]])
                    else:
                        P.mm(o_, ucur[:, cc * 128:(cc + 1) * 128], MCUR, True, False, [("ut", i % 5), "cst_b"], [bkey[6]])
                        P.mm(o_, uprev[:, cc * 128:(cc + 1) * 128], MPREV, False, True,
                             [("ut", (i - 1) % 5), "cst_b"], [bkey[6]])
                P.cp("act", dTb[:], bank[6][:, 256:512].rearrange("p (c n) -> p c n", n=128), [bkey[6]], ["dTb"])
                for cc in range(2):
                    P.mm(bank[7][:, 0:256], dTb[:, cc, :], pw_b[:, cc, :], cc == 0, cc == 1, ["dTb", "pw_b"], [bkey[7]])
                P.tt(yst[a][:, 512:768], gateC[:], bank[7][:, 0:256], ALU.mult, [kgC, bkey[7]], [yk])

            deferred = P.capture_end()
            deferred.reverse()
            its = [(h, kt) for h in range(2 if stage >= 6 else 0) for kt in range(4 * j + 4)]

            def scores(idx):
                h, kt = its[idx]
                n0 = max(0, kt - 4 * j)
                q0 = n0 * 128
                ak = kt // GT
                kcol = (kt % GT) * 128
                sb_ = idx % 2
                stt_ = bank[sb_][:, q0:512]
                diag = kt >= 4 * j
                P.mm(stt_, KT[:, h, kt * 128:(kt + 1) * 128], QT[:, h, q0:512], True, False,
                     [("KT", kt // 4), "QT"], [bkey[sb_]])
                P.mm(stt_, KF[32 * ak:32 * ak + 6, h, kcol:kcol + 128], QF[32 * ak:32 * ak + 6, h, q0:512],
                     False, not diag, [("KF", kt // 4), "QF"], [bkey[sb_]])
                if diag:
                    P.mm(bank[sb_][:, q0:q0 + 128], IDb, MASKb, False, True, ["cst_b"], [bkey[sb_]])
                P.act(PT[sb_][:, q0:512], stt_, AF.Exp, [bkey[sb_]], [("PT", sb_)])

            per = (len(deferred) + max(1, len(its) - 2) - 1) // max(1, len(its) - 2)

            def replay(n):
                for _ in range(n):
                    if deferred:
                        P.op(*deferred.pop())
            if its:
                scores(0)
            for idx, (h, kt) in enumerate(its):
                if idx + 1 < len(its):
                    scores(idx + 1)
                replay(per)
                n0 = max(0, kt - 4 * j)
                sb_ = idx % 2
                for n in range(n0, 4):
                    ob = 2 + n // 2
                    oc = (n % 2) * 256
                    first = (kt == 0)
                    P.mm(bank[ob][:, oc:oc + 129], PT[sb_][:, n * 128:(n + 1) * 128], VA[:, kt, h, :],
                         first and (n % 2 == 0), kt == 4 * j + n, [("PT", sb_), ("VA", kt // 4)], [bkey[ob]])
                if kt == 4 * j + 3:
                    for n in range(4):
                        ob = 2 + n // 2
                        oc = (n % 2) * 256
                        P.op("dve", lambda e, ob=ob, oc=oc: e.reciprocal(out=sm[:, 28:29], in_=bank[ob][:, oc + 128:oc + 129]),
                             [bkey[ob]], ["sm28"])
                        P.stt(yst[n][:, 256 + h * 128:384 + h * 128], bank[ob][:, oc:oc + 128], sm[:, 28:29],
                              gateB[n][:, h * 128:(h + 1) * 128], ALU.mult, ALU.mult,
                              [bkey[ob], "sm28", ("gateB", n)], [("yst", n)])
            replay(len(deferred))
            for n in range(4):
                i = 4 * j + n
                P.dma(ys_d[i * 128:(i + 1) * 128, :], yst[n][:], reads=[("yst", n)], writes=[("ys", i)])
            if after_st is not None:
                after_st(j, [("ys", 4 * j + n) for n in range(4)])
    return [("ys", i) for i in range(NT)]


def _consts(g):
    s = np.arange(128)[:, None]
    t = np.arange(128)[None, :]
    ident = (s == t).astype(np.float32)
    uin = (s <= t).astype(np.float32)
    urev = (s > t).astype(np.float32)
    ones = np.ones((128, 128), np.float32)
    maskT = np.where(s > t, -30000.0, 0.0).astype(np.float32)
    W = POOL_WINDOWS[g]
    def mt(first):
        M = np.zeros((128, 256), np.float32)
        for tt in range(128):
            cnt = min(tt + 1, W) if first else W
            for jj in range(cnt):
                M[tt, 128 + tt - jj] += 1.0 / cnt
            M[tt, 128 + tt] -= 1.0
        return M
    Mg = mt(False)
    Mf = mt(True)
    mcur = Mg[:, 128:].T.copy()
    mprev = Mg[:, :128].T.copy()
    mfirst = Mf[:, 128:].T.copy()
    return np.concatenate([ident, uin, urev, ones, maskT, mcur, mfirst, mprev], axis=1).astype(np.float32)


def _p1_cols(g):
    W = 1024
    off = {}
    names = ["aq", "ak", "av", "ao", "az"]
    o = 0
    for nme in names:
        off[nme] = o
        o += W
    off["ai"] = o; o += 4
    off["af"] = o; o += 4
    for nme in ["bq", "bk", "bv", "bz"]:
        off[nme] = o
        o += W
    off["bf"] = o; o += 8
    off["cu"] = o; o += W
    off["cz"] = o; o += W
    off["gates"] = o
    r = lambda nme: np.arange(off[nme] + g * 256, off[nme] + (g + 1) * 256)
    cols = np.concatenate([
        r("bq"), r("bk"), r("aq"), r("ak"),
        r("bv"), r("bz"), r("av"), r("ao"), r("az"), r("cu"), r("cz"),
        np.array([off["ai"] + g, off["af"] + g, off["bf"] + 2 * g, off["bf"] + 2 * g + 1]),
    ])
    return cols, off


def p1_inputs(x_b, l, g, inp):
    cols, off = _p1_cols(g)
    w1 = np.ascontiguousarray(inp["w_in"][l][:, cols])
    vec = np.zeros((128, 32), np.float32)
    vec[:, 0:8] = inp["norm_g"][l].reshape(8, 128).T
    cw = inp["conv_w"][l]
    for cc in range(4):
        base = (0 if cc < 2 else 1024) + g * 256 + (cc % 2) * 128
        vec[:, 8 + cc * 4:12 + cc * 4] = cw[:, base:base + 128].T
    vec[:, 24] = inp["ml_bi"][l][g]
    vec[:, 25] = inp["ml_bf"][l][g]
    vec[:, 26] = inp["fox_bf"][l][2 * g]
    vec[:, 27] = inp["fox_bf"][l][2 * g + 1]
    rows = np.zeros((128, 512), np.float32)
    rows[:, 0:256] = inp["ml_norm_g"][l][g * 256:(g + 1) * 256][None, :]
    rows[:, 256:512] = inp["pool_scale"][l][g * 256:(g + 1) * 256][None, :]
    return dict(x=np.ascontiguousarray(x_b), w1=w1, vecs=vec, rows=rows,
                poolw=np.ascontiguousarray(inp["pool_w"][l][g]), consts=_consts(g))


def p2_record(P, nc, st, bank, T, last, xsrc, ysload, wg_d, wb_d, wo_d, vec_d, fg_d, id_d, odst, pre=None, after_tile=None, post_w=None):
    NT = T // 128
    if True:
        tag = f"sb{P.nflush}_"
        sb = lambda n, s, d: st.enter_context(nc.sbuf_tensor(tag + n, s, d))
        bkey = [f"bk{b}" for b in range(8)]
        Wg = sb("Wg", [128, 8, 3072], BF16)
        Wb = sb("Wb", [128, 24, 1024], BF16)
        Wo = sb("Wo", [128, 8, 1024], BF16)
        stg = [sb(f"stg{i}", [128, 8, 256], F32) for i in range(2)]
        vec = sb("vec2", [128, 8], F32)
        fg = sb("fg", [128, 1024], F32)
        idf = sb("idf", [128, 128], F32)
        idb = sb("idb", [128, 128], BF16)
        xt = [sb(f"xt{i}", [128, 1024], F32) for i in range(2)]
        yst = sb("yst", [128, 3072], BF16)
        ysT = sb("ysT", [128, 24, 128], BF16)
        junk_b = sb("junk_b", [128, 1024], BF16)
        ss = sb("ss", [128, 4], F32)
        xn = sb("xn", [128, 1024], BF16)
        hT = sb("hT", [128, 8, 128], BF16)
        gsb = sb("gsb", [128, 3072], F32)
        mrg = sb("mrg", [128, 1024], F32)
        tmp = sb("tmp", [128, 512], F32)
        mb = sb("mb", [128, 1024], BF16)
        mT = sb("mT", [128, 8, 128], BF16)
        xo = [sb(f"xo{i}", [128, 1024], F32) for i in range(2)]

        P.dma(vec[:], vec_d, writes=["vec"])
        P.dma(fg[:], fg_d, writes=["fg"])
        P.dma(idf[:], id_d, writes=["idf"])
        P.cp("dve", idb[:], idf[:], ["idf"], ["idb"])

        si = 0
        def load_w(dview, dest, nk, ncols, fold, key):
            nonlocal si
            for k0 in range(0, nk, 8):
                for c0 in range(0, ncols, 256):
                    s_ = stg[si % 2]
                    sk = ("stg", si % 2)
                    si += 1
                    P.dma(s_[:], dview[:, k0:k0 + 8, c0:c0 + 256], writes=[sk])
                    for k in range(8):
                        o_ = dest[:, k0 + k, c0:c0 + 256]
                        if fold:
                            if k % 2 == 0:
                                P.ts(o_, s_[:, k, :], vec[:, k:k + 1], None, ALU.mult, None, [sk, "vec"], [(key, k0 + k, c0)])
                            else:
                                P.act(o_, s_[:, k, :], AF.Copy, [sk, "vec"], [(key, k0 + k, c0)], scale=vec[:, k:k + 1])
                        else:
                            P.cp("dve" if k % 2 == 0 else "act", o_, s_[:, k, :], [sk], [(key, k0 + k, c0)])
        load_w(wg_d.rearrange("(k p) n -> p k n", p=128), Wg, 8, 3072, True, "Wg")
        load_w(wb_d.rearrange("(k p) n -> p k n", p=128), Wb, 24, 1024, False, "Wb")
        load_w(wo_d.rearrange("(k p) n -> p k n", p=128), Wo, 8, 1024, False, "Wo")
        if post_w is not None:
            post_w()
        wkeys = lambda key, k, c0, c1: [(key, k, c) for c in range(0, 4096, 256) if c < c1 and c + 256 > c0]

        mmb = 0
        trb = 0
        for i in range(NT):
            xb = xt[i % 2]
            xk = ("xt", i % 2)
            pr = list(pre(i)) if pre is not None else []
            P.dma(xb[:], xsrc(i), reads=pr, writes=[xk])
            for (o_, i_) in ysload(i, yst):
                P.dma(o_, i_, reads=pr, writes=["yst"])
            P.act(junk_b[:], xb[:], AF.Square, [xk], ["junk_b", "ss"], accum_out=ss[:, 0:1])
            P.act(ss[:, 1:2], ss[:, 0:1], AF.Ln, ["ss"], ["ss1"], scale=1.0 / 1024, bias=EPS)
            P.act(ss[:, 1:2], ss[:, 1:2], AF.Exp, ["ss1"], ["ss1"], scale=-0.5)
            P.ts(xn[:], xb[:], ss[:, 1:2], None, ALU.mult, None, [xk, "ss1"], ["xn"])
            tb = 4 + trb % 2; trb += 1
            tbv = bank[tb][:].bitcast(BF16)
            for k in range(8):
                P.tr(tbv[:, k * 128:(k + 1) * 128], xn[:, k * 128:(k + 1) * 128], idb[:], ["xn", "idb"], [bkey[tb]])
            P.cp("act", hT[:], tbv[:, 0:1024].rearrange("p (k n) -> p k n", n=128), [bkey[tb]], ["hT"])
            for n in range(3):
                tb = 4 + trb % 2; trb += 1
                tbv = bank[tb][:].bitcast(BF16)
                for k in range(8):
                    c = n * 1024 + k * 128
                    P.tr(tbv[:, k * 128:(k + 1) * 128], yst[:, c:c + 128], idb[:], ["yst", "idb"], [bkey[tb]])
                P.cp("dve" if n % 2 == 0 else "act", ysT[:, n * 8:(n + 1) * 8, :],
                     tbv[:, 0:1024].rearrange("p (k n) -> p k n", n=128), [bkey[tb]], [("ysT", n)])
            for cg in range(6):
                b = mmb % 4; mmb += 1
                for k in range(8):
                    P.mm(bank[b][:, 0:512], hT[:, k, :], Wg[:, k, cg * 512:(cg + 1) * 512], k == 0, k == 7,
                         ["hT"] + wkeys("Wg", k, cg * 512, (cg + 1) * 512), [bkey[b]])
                P.act(gsb[:, cg * 512:(cg + 1) * 512], bank[b][:, 0:512], AF.Sigmoid, [bkey[b]], [("gsb", cg)])
            for n in range(3):
                for hf in range(2):
                    b = mmb % 4; mmb += 1
                    for k in range(8):
                        P.mm(bank[b][:, 0:512], ysT[:, n * 8 + k, :], Wb[:, n * 8 + k, hf * 512:(hf + 1) * 512], k == 0, k == 7,
                             [("ysT", n)] + wkeys("Wb", n * 8 + k, hf * 512, (hf + 1) * 512), [bkey[b]])
                    gv = gsb[:, n * 1024 + hf * 512:n * 1024 + (hf + 1) * 512]
                    gk = ("gsb", n * 2 + hf)
                    mk = ("mrg", hf)
                    mv = mrg[:, hf * 512:(hf + 1) * 512]
                    if n == 0:
                        P.tt(mv, gv, bank[b][:, 0:512], ALU.mult, [gk, bkey[b]], [mk])
                    else:
                        P.tt(tmp[:], gv, bank[b][:, 0:512], ALU.mult, [gk, bkey[b]], ["tmp"])
                        if n == 1:
                            P.tt(mv, mv, tmp[:], ALU.add, [mk, "tmp"], [mk], eng="pool")
                        else:
                            P.tt(mb[:, hf * 512:(hf + 1) * 512], mv, tmp[:], ALU.add, [mk, "tmp"], [("mb", hf)], eng="pool")
            tb = 4 + trb % 2; trb += 1
            tbv = bank[tb][:].bitcast(BF16)
            for k in range(8):
                P.tr(tbv[:, k * 128:(k + 1) * 128], mb[:, k * 128:(k + 1) * 128], idb[:], [("mb", k // 4), "idb"], [bkey[tb]])
            P.cp("act", mT[:], tbv[:, 0:1024].rearrange("p (k n) -> p k n", n=128), [bkey[tb]], ["mT"])
            ob = xo[i % 2]
            ok = ("xo", i % 2)
            for hf in range(2):
                b = mmb % 4; mmb += 1
                for k in range(8):
                    P.mm(bank[b][:, 0:512], mT[:, k, :], Wo[:, k, hf * 512:(hf + 1) * 512], k == 0, k == 7,
                         ["mT"] + wkeys("Wo", k, hf * 512, (hf + 1) * 512), [bkey[b]])
                P.tt(ob[:, hf * 512:(hf + 1) * 512], xb[:, hf * 512:(hf + 1) * 512], bank[b][:, 0:512], ALU.add,
                     [xk, bkey[b]], [ok])
            if last:
                P.act(junk_b[:], ob[:], AF.Square, [ok], ["junk_b", "ss2"], accum_out=ss[:, 2:3])
                P.act(ss[:, 3:4], ss[:, 2:3], AF.Ln, ["ss2"], ["ss3"], scale=1.0 / 1024, bias=EPS)
                P.act(ss[:, 3:4], ss[:, 3:4], AF.Exp, ["ss3"], ["ss3"], scale=-0.5)
                P.stt(ob[:], ob[:], ss[:, 3:4], fg[:], ALU.mult, ALU.mult, [ok, "ss3", "fg"], [ok])
            P.dma(odst(i), ob[:], reads=[ok], writes=[("o", i)])
            if after_tile is not None:
                after_tile(i)
    return [("o", i) for i in range(NT)]


def p2_inputs(x_sh, ys_sh, l, inp):
    o = 5 * 1024 + 8 + 4 * 1024 + 8 + 2 * 1024
    return dict(x=np.ascontiguousarray(x_sh), ysin=np.ascontiguousarray(ys_sh),
                wg=np.ascontiguousarray(inp["w_in"][l][:, o:o + 3072]),
                wb=np.ascontiguousarray(inp["w_branch"][l].reshape(3072, 1024)),
                wo=np.ascontiguousarray(inp["w_out"][l]),
                vecs=np.ascontiguousarray(inp["norm_g"][l].reshape(8, 128).T),
                fgrows=np.ascontiguousarray(np.broadcast_to(inp["final_g"][None, :], (128, 1024))),
                ident=np.eye(128, dtype=np.float32))


GROUPS = [[0, 1, 2, 3], [4, 5, 6, 7]]


class _QCtx:
    def __init__(self, h):
        self.q = h.partition_id() % 4
        self.cache = {}

    def mul(self, m):
        if m == 1:
            return self.q
        if m not in self.cache:
            self.cache[m] = self.q * m
        return self.cache[m]


def build_fused(S=SEQ):
    TS = S // 4
    nc = bass.Bass("TRN2", target_bir_lowering=False)
    dram = lambda n, sh, dt, kind: nc.dram_tensor(n, sh, dt, kind=kind).ap()
    x_d = dram("x", [S, 1024], F32, "ExternalInput")
    cst_d = dram("consts", [128, 1024], F32, "ExternalInput")
    fg_d = dram("fgrows", [128, 1024], F32, "ExternalInput")
    L = []
    for l in range(DEPTH):
        L.append(dict(
            w1=dram(f"w1_{l}", [1024, NW1], F32, "ExternalInput"),
            vec=dram(f"vecs_{l}", [128, 32], F32, "ExternalInput"),
            rows=dram(f"rows_{l}", [128, 512], F32, "ExternalInput"),
            pw=dram(f"poolw_{l}", [256, 256], F32, "ExternalInput"),
            wg=dram(f"wg_{l}", [1024, 3072], F32, "ExternalInput"),
            wb=dram(f"wb_{l}", [3072, 1024], F32, "ExternalInput"),
            wo=dram(f"wo_{l}", [1024, 1024], F32, "ExternalInput"),
        ))
    out_d = dram("out", [TS, 1024], F32, "ExternalOutput")
    ysrc = nc.dram_tensor("ysrc", [S, 768], BF16).ap()
    yall = nc.dram_tensor("yall", [4 * S, 768], BF16).ap()
    x1src = nc.dram_tensor("x1src", [TS, 1024], F32).ap()
    x1all = nc.dram_tensor("x1all", [S, 1024], F32).ap()
    ysh = nc.dram_tensor("ysh", [4 * TS, 768], BF16).ap()
    xsh = nc.dram_tensor("xsh", [TS, 1024], F32).ap()

    with contextlib.ExitStack() as top:
        bank = [top.enter_context(nc.psum_tensor(f"bk{b}", [128, 512], F32)) for b in range(8)]
        fin = top.enter_context(nc.sbuf_tensor("sb_fin", [1, 8], F32))
        P = Prog(nc, top)
        P.op("pool", lambda e: e.memset(fin[:], 0.0), (), ["fin0"])
        NB = TS // 512
        for l in range(DEPTH):
            d = L[l]
            last = (l == DEPTH - 1)
            with contextlib.ExitStack() as st:
                if l == 0:
                    xtile = lambda i: x_d[i * 128:(i + 1) * 128, :]
                else:
                    def xtile(i):
                        tok = i * 128
                        r, k, t = tok // TS, (tok % TS) // 256, tok % 256
                        row = k * 1024 + r * 256 + t
                        return x1all[row:row + 128, :]

                def after_st(j, keys):
                    P.cc("AllGather", GROUPS, yall[j * 2048:(j + 1) * 2048, :], ysrc[j * 512:(j + 1) * 512, :],
                         reads=keys, writes=[("yall", j)])
                p1_record(P, nc, st, bank, S, xtile, d["w1"], d["vec"], d["rows"], d["pw"], cst_d, ysrc, after_st=after_st)
                P.flush(fin, bank[7])
            with contextlib.ExitStack() as st:
                yv = yall.rearrange("(blk p r) c -> blk p (r c)", p=128, r=16)
                yshv = ysh.rearrange("(k p r) c -> k p (r c)", p=128, r=16)
                def shard_copy(jj):
                    P.dma(yshv[jj:jj + 1], (lambda q, jj=jj: yv[jj:][bass.ds(q.mul(NB), 1)]), writes=[("ysh", jj)])
                    if l == 0:
                        P.dma(xshv[jj:jj + 1], (lambda q, jj=jj: xv[jj:][bass.ds(q.mul(NB), 1)]), writes=[("xsh", jj)])
                if l == 0:
                    xv = x_d.rearrange("(blk p r) c -> blk p (r c)", p=128, r=4)
                    xshv = xsh.rearrange("(k p r) c -> k p (r c)", p=128, r=4)
                    xsrc = lambda i: xsh[i * 128:(i + 1) * 128, :]
                else:
                    xsrc = lambda i: x1src[i * 128:(i + 1) * 128, :]
                shard_copy(0)

                def post_w():
                    for jj in range(1, NB):
                        shard_copy(jj)

                def ysload(i, yst):
                    prs = []
                    jj, t0 = i // 4, (i % 4) * 128
                    for r in range(4):
                        o_ = yst[:].rearrange("p (n r c) -> p n r c", n=3, r=4)[:, :, r, :]
                        row = jj * 2048 + r * 512 + t0
                        i_ = ysh[row:row + 128, :].rearrange("p (n c) -> p n c", n=3)
                        prs.append((o_, i_))
                    return prs
                if last:
                    odst = lambda i: out_d[i * 128:(i + 1) * 128, :]
                    after_tile = None
                else:
                    odst = lambda i: x1src[i * 128:(i + 1) * 128, :]

                    def after_tile(i):
                        if i % 2 == 1:
                            k = i // 2
                            P.cc("AllGather", GROUPS, x1all[k * 1024:(k + 1) * 1024, :], x1src[k * 256:(k + 1) * 256, :],
                                 reads=[("o", i - 1), ("o", i)], writes=[("x1all", k)])
                p2_record(P, nc, st, bank, TS, last, xsrc, ysload, d["wg"], d["wb"], d["wo"],
                          d["vec"][:, 0:8], fg_d, cst_d[:, 0:128], odst, pre=lambda i: [("ysh", i // 4), ("xsh", i // 4)],
                          after_tile=after_tile, post_w=post_w)
                P.flush(fin, bank[7])
    return nc


_CACHE = {}


def kernel(x, norm_g, w_in, conv_w, ml_bi, ml_bf, ml_norm_g, fox_bf, pool_w, pool_scale, w_branch, w_out, final_g):
    inp = dict(norm_g=np.asarray(norm_g), w_in=np.asarray(w_in), conv_w=np.asarray(conv_w), ml_bi=np.asarray(ml_bi),
               ml_bf=np.asarray(ml_bf), ml_norm_g=np.asarray(ml_norm_g), fox_bf=np.asarray(fox_bf),
               pool_w=np.asarray(pool_w), pool_scale=np.asarray(pool_scale), w_branch=np.asarray(w_branch),
               w_out=np.asarray(w_out), final_g=np.asarray(final_g))
    x = np.asarray(x, dtype=np.float32)
    B, S, _ = x.shape
    TS = S // 4
    if "nc" not in _CACHE:
        _CACHE["nc"] = build_fused(S)
    nc = _CACHE["nc"]
    go = 5 * 1024 + 8 + 4 * 1024 + 8 + 2 * 1024
    fgrows = np.ascontiguousarray(np.broadcast_to(inp["final_g"][None, :], (128, 1024)))
    ins = []
    for c in range(8):
        b, g = c // 4, c % 4
        m = dict(x=np.ascontiguousarray(x[b]), consts=_consts(g), fgrows=fgrows)
        for l in range(DEPTH):
            p1 = p1_inputs(x[b], l, g, inp)
            m[f"w1_{l}"] = p1["w1"]
            m[f"vecs_{l}"] = p1["vecs"]
            m[f"rows_{l}"] = p1["rows"]
            m[f"poolw_{l}"] = p1["poolw"]
            m[f"wg_{l}"] = np.ascontiguousarray(inp["w_in"][l][:, go:go + 3072])
            m[f"wb_{l}"] = np.ascontiguousarray(inp["w_branch"][l].reshape(3072, 1024))
            m[f"wo_{l}"] = np.ascontiguousarray(inp["w_out"][l])
        ins.append(m)
    res = run_bass_kernel_spmd(nc, ins, core_ids=list(range(8)))
    out = np.empty((B, S, 1024), np.float32)
    for c in range(8):
        b, q = c // 4, c % 4
        out[b, q * TS:(q + 1) * TS] = np.asarray(res.results[c]["out"])
    return out
```

```python
import contextlib
import numpy as np
import ml_dtypes
import concourse.bass as bass
import concourse.mybir as mybir
from concourse.bass_utils import run_bass_kernel_spmd

F32 = mybir.dt.float32
BF16 = mybir.dt.bfloat16
AF = mybir.ActivationFunctionType
ALU = mybir.AluOpType

D_MODEL = 1024
SEQ = 8192
BATCH = 2
DEPTH = 2
N_IN = 14352
POOL_WINDOWS = (2, 4, 8, 16)
EPS = 1e-6

EPOCH = 20000
SAME_ENGINE_SYNC = True


class Prog:
    CE = ("pe", "act", "dve", "pool")
    ALLE = ("pe", "act", "dve", "pool", "sp")
    NEP = 4

    def __init__(self, nc, stack):
        self.nc = nc
        self.NDS = 24
        self.NCC = 8
        self.csem = {e: [stack.enter_context(nc.semaphore(f"s_{e}{i}")) for i in range(self.NEP)] for e in self.CE}
        self.dsem = [stack.enter_context(nc.semaphore(f"s_d{i}")) for i in range(self.NDS)]
        self.ccsem = [stack.enter_context(nc.semaphore(f"s_cc{i}")) for i in range(self.NCC)]
        self.sigcnt = {e: 0 for e in self.CE}
        self.dma_cnt = [0] * self.NDS
        self.dma_rr = 0
        self.dma_rr_pool = 0
        self.ncc = 0
        self.nflush = 0
        self._reset()

    def _reset(self):
        self.ops = {e: [] for e in self.ALLE}
        self.last_write = {}
        self.readers = {}
        self.seen = {e: {} for e in self.ALLE}
        self.signaled = {e: set() for e in self.CE}
        self.cc_pending = []

    def _need(self, eng, ev, waits):
        if ev is None:
            return
        if ev[0] == "c":
            _, e2, idx = ev
            if e2 == eng and (eng == "pe" or not SAME_ENGINE_SYNC):
                return
            k = ("c", e2)
            if self.seen[eng].get(k, -1) >= idx:
                return
            self.seen[eng][k] = idx
            waits.append(ev)
            self.signaled[e2].add(idx)
        elif ev[0] == "d":
            _, slot, cnt = ev
            k = ("d", slot)
            if self.seen[eng].get(k, 0) >= cnt:
                return
            self.seen[eng][k] = cnt
            waits.append(ev)
        else:
            k = ("x", ev[1])
            if k in self.seen[eng]:
                return
            self.seen[eng][k] = 1
            waits.append(ev)

    def _deps(self, eng, reads, writes):
        waits = []
        for r in reads:
            self._need(eng, self.last_write.get(r), waits)
        for w in writes:
            self._need(eng, self.last_write.get(w), waits)
            for ev in self.readers.get(w, ()):
                self._need(eng, ev, waits)
        return waits

    def _commit(self, ev, reads, writes):
        for r in reads:
            self.readers.setdefault(r, []).append(ev)
        for w in writes:
            self.last_write[w] = ev
            self.readers[w] = []

    def capture_begin(self):
        self._cap = []

    def capture_end(self):
        c, self._cap = self._cap, None
        return c

    def op(self, eng, fn, reads=(), writes=()):
        if getattr(self, "_cap", None) is not None:
            self._cap.append((eng, fn, reads, writes))
            return
        bk = [r for r in reads if isinstance(r, str) and r.startswith("bk")]
        if bk:
            writes = list(writes) + [b for b in bk if b not in writes]
            reads = [r for r in reads if r not in bk]
        waits = self._deps(eng, reads, writes)
        idx = len(self.ops[eng])
        self.ops[eng].append(dict(kind="c", fn=fn, waits=waits))
        self._commit(("c", eng, idx), reads, writes)

    def dma(self, out, in_, reads=(), writes=(), q="sp"):
        if q == "pool":
            slot = 16 + self.dma_rr_pool
            self.dma_rr_pool = (self.dma_rr_pool + 1) % 8
        else:
            slot = self.dma_rr
            self.dma_rr = (self.dma_rr + 1) % 16
        waits = self._deps(q, reads, writes)
        if self.dma_cnt[slot] > 0:
            self._need(q, ("d", slot, self.dma_cnt[slot]), waits)
        self.dma_cnt[slot] += 1
        ev = ("d", slot, self.dma_cnt[slot])
        self.ops[q].append(dict(kind="d", out=out, in_=in_, waits=waits, slot=slot))
        self._commit(ev, reads, writes)

    def cc(self, kind, groups, out, in_, reads=(), writes=()):
        waits = self._deps("pool", reads, writes)
        n = self.ncc
        self.ncc += 1
        ev = ("x", n)
        self.ops["pool"].append(dict(kind="cc", cck=kind, groups=groups, out=out, in_=in_, waits=waits, n=n))
        self.cc_pending.append(ev)
        self._commit(ev, reads, writes)

    def mm(self, out, lhsT, rhs, start, stop, reads, writes):
        self.op("pe", lambda e: e.matmul(out, lhsT=lhsT, rhs=rhs, start=start, stop=stop, skip_group_check=True),
                reads, writes)

    def tr(self, out, in_, ident, reads, writes):
        self.op("pe", lambda e: e.transpose(out=out, in_=in_, identity=ident), reads, writes)

    def act(self, out, in_, func, reads, writes, **kw):
        self.op("act", lambda e: e.activation(out=out, in_=in_, func=func, **kw), reads, writes)

    def ts(self, out, in0, s1, s2, op0, op1, reads, writes, eng="dve"):
        if op1 is None:
            self.op(eng, lambda e: e.tensor_scalar(out=out, in0=in0, scalar1=s1, scalar2=None, op0=op0), reads, writes)
        else:
            self.op(eng, lambda e: e.tensor_scalar(out=out, in0=in0, scalar1=s1, scalar2=s2, op0=op0, op1=op1),
                    reads, writes)

    def tt(self, out, in0, in1, op, reads, writes, eng="dve"):
        self.op(eng, lambda e: e.tensor_tensor(out=out, in0=in0, in1=in1, op=op), reads, writes)

    def stt(self, out, in0, scalar, in1, op0, op1, reads, writes, eng="dve"):
        self.op(eng, lambda e: e.scalar_tensor_tensor(out=out, in0=in0, scalar=scalar, in1=in1, op0=op0, op1=op1),
                reads, writes)

    def cp(self, eng, out, in_, reads, writes):
        if eng == "act":
            self.op("act", lambda e: e.copy(out=out, in_=in_), reads, writes)
        else:
            self.op(eng, lambda e: e.tensor_copy(out=out, in_=in_), reads, writes)

    def flush(self, fin, bank7):
        nc = self.nc
        fk = [("fin", e) for e in self.CE]
        waits = []
        for ev in self.cc_pending:
            self._need("pool", ev, waits)
        self.ops["pool"].append(dict(kind="w", waits=waits))
        self.op("act", lambda e: e.copy(out=fin[0:1, 0:1], in_=fin[0:1, 1:2]), ["fin0"], [fk[1]])
        self.op("dve", lambda e: e.memset(fin[0:1, 2:3], 0.0), (), [fk[2]])
        self.op("pool", lambda e: e.memset(fin[0:1, 3:4], 0.0), (), [fk[3]])
        self.op("pe", lambda e: e.matmul(bank7[0:1, 511:512], lhsT=fin[0:1, 4:5], rhs=fin[0:1, 5:6], start=True, stop=True,
                                         skip_group_check=True), ["bk7", "fin0"], [fk[0]])
        for e in self.ALLE:
            waits = []
            for k in fk:
                self._need(e, self.last_write.get(k), waits)
            for slot in range(self.NDS):
                if self.dma_cnt[slot] > 0:
                    self._need(e, ("d", slot, self.dma_cnt[slot]), waits)
            self.ops[e].append(dict(kind="w", waits=waits))
        semval = {}
        for e in self.CE:
            c = self.sigcnt[e]
            for idx in range(len(self.ops[e])):
                if idx in self.signaled[e]:
                    semval[(e, idx)] = (c // EPOCH, c % EPOCH + 1)
                    c += 1
            self.sigcnt[e] = c
            assert c <= EPOCH * self.NEP, (e, c)
        csem, dsem, ccsem = self.csem, self.dsem, self.ccsem
        need_q = any(o["kind"] == "d" and (callable(o["out"]) or callable(o["in_"])) for o in self.ops["sp"])
        with nc.Block() as block:
            def run(e):
                def body(h):
                    qv = None
                    if e == "sp" and need_q:
                        qv = _QCtx(h)
                    for i, o in enumerate(self.ops[e]):
                        for ev in o["waits"]:
                            if ev[0] == "c":
                                ep, v = semval[(ev[1], ev[2])]
                                h.wait_ge(csem[ev[1]][ep], v)
                            elif ev[0] == "d":
                                h.wait_ge(dsem[ev[1]], 16 * ev[2])
                            else:
                                h.wait_ge(ccsem[ev[1] % self.NCC], ev[1] // self.NCC + 1)
                        if o["kind"] == "c":
                            ins = o["fn"](h)
                            if (e, i) in semval:
                                ep, v = semval[(e, i)]
                                ins.then_inc(csem[e][ep], 1)
                        elif o["kind"] == "d":
                            o_ = o["out"](qv) if callable(o["out"]) else o["out"]
                            i_ = o["in_"](qv) if callable(o["in_"]) else o["in_"]
                            try:
                                h.dma_start(out=o_, in_=i_).then_inc(dsem[o["slot"]], 16)
                            except Exception:
                                print("DMA FAIL", o_, i_, flush=True)
                                raise
                        elif o["kind"] == "cc":
                            h.collective_compute(o["cck"], ALU.bypass, replica_groups=o["groups"], ins=[o["in_"].opt()],
                                                 outs=[o["out"].opt()]).then_inc(ccsem[o["n"] % self.NCC], 1)
                return body
            block.sync(run("sp"))
            block.tensor(run("pe"))
            block.scalar(run("act"))
            block.vector(run("dve"))
            block.gpsimd(run("pool"))
        self.nflush += 1
        self._reset()


NW1 = 2820
NFM = 1024


def p1_record(P, nc, st, bank, S, xtile, w1_d, vec_d, row_d, pw_d, cst_d, ys_d, after_st=None, stage=9, dbg=()):
    NT = S // 128
    NST = S // 512
    GT = (NT + 2) // 3
    if True:
        tag = f"sb{P.nflush}_"
        sb = lambda n, s, d: st.enter_context(nc.sbuf_tensor(tag + n, s, d))
        bkey = [f"bk{b}" for b in range(8)]
        b5b = bank[5][:].bitcast(BF16)

        cst_f = sb("cst_f", [128, 8, 128], F32)
        cst_b = sb("cst_b", [128, 8, 128], BF16)
        vec = sb("vec", [128, 32], F32)
        rows = sb("rows", [128, 1536], F32)
        pw_b = sb("pw_b", [128, 2, 256], BF16)
        W1 = sb("W1", [128, 8, NW1], BF16)
        KT = sb("KT", [128, 2, S], BF16)
        VA = sb("VA", [128, NT, 2, 129], BF16)
        KF = sb("KF", [128, 2, GT * 128], BF16)
        hT = sb("hT", [128, 8, 512], BF16)
        QT = sb("QT", [128, 2, 512], BF16)
        QF = sb("QF", [128, 2, 512], BF16)
        cin = sb("cin", [128, 515], F32)
        halo = sb("halo", [128, 4, 3], F32)
        ctmp = sb("ctmp", [128, 512], F32)
        csig = sb("csig", [128, 512], F32)
        qkT = sb("qkT", [128, 4, 512], BF16)
        xt = [sb(f"xt{i}", [128, 1024], F32) for i in range(2)]
        ss = sb("ss", [128, 2], F32)
        xn = sb("xn", [128, 1024], BF16)
        junk_b = xn
        gateB = [sb(f"gateB{i}", [128, 256], F32) for i in range(4)]
        VAm4 = [sb(f"VAm{i}", [128, 257], BF16) for i in range(4)]
        gateA4 = [sb(f"gateA{i}", [128, 256], F32) for i in range(4)]
        gateC4 = [sb(f"gateC{i}", [128, 256], F32) for i in range(4)]
        ut = [sb(f"ut{i}", [128, 256], BF16) for i in range(5)]
        sp4 = [sb(f"sp{i}", [128, 4], F32) for i in range(4)]
        cs4 = [sb(f"cs{i}", [128, 12], F32) for i in range(4)]
        ls4 = [sb(f"ls{i}", [128, 4], F32) for i in range(4)]
        sgs = [sb("sg0", [128, 256], F32), sb("sg1", [128, 256], F32)]
        sg = sgs[0]
        sm = sb("sm", [128, 32], F32)
        Fcar = sb("Fcar", [128, 2], F32)
        Fk = sb("Fk", [128, NT, 2], F32)
        nbias = sb("nbias", [128, 2, NT], F32)
        Fref = sb("Fref", [128, 2], F32)
        fq = sb("fq", [128, 4, 2], F32)
        Osb = sb("Osb", [128, 4, 129], F32)
        f3b = sb("f3b", [128, 2, 3], BF16)
        f3n = sb("f3n", [128, 2, 3], BF16)
        fw = sb("fw", [128, 8], F32)
        qfk = sb("qfk", [128, 2, 2, 128], BF16)
        CT = sb("CT", [128, 2, 257], F32)
        CTb = sb("CTb", [128, 2, 257], BF16)
        Bm = sb("Bm", [128, 128], F32)
        ET = sb("ET", [128, 128], F32)
        PTm = sb("PTm", [128, 128], BF16)
        Ktok = sb("Ktok", [128, 256], BF16)
        Vw = sb("Vw", [128, 257], BF16)
        Gs = sb("Gs", [128, 257], F32)
        dTb = sb("dTb", [128, 2, 128], BF16)
        PT = [sb(f"PT{i}", [128, 512], BF16) for i in range(2)]
        yst = [sb(f"yst{i}", [128, 768], BF16) for i in range(4)]

        IDb = cst_b[:, 0, :]
        IDf = cst_f[:, 0, :]
        UIN = cst_f[:, 1, :]
        UREV = cst_f[:, 2, :]
        ONESf = cst_f[:, 3, :]
        MASKb = cst_b[:, 4, :]
        MCUR = cst_b[:, 5, :]
        MFIRST = cst_b[:, 6, :]
        MPREV = cst_b[:, 7, :]

        P.dma(cst_f[:], cst_d.rearrange("p (c n) -> p c n", n=128), writes=["cst_f"])
        P.dma(vec[:], vec_d, writes=["vec"])
        P.dma(rows[:], row_d, writes=["rows"])
        P.cp("dve", cst_b[:], cst_f[:], ["cst_f"], ["cst_b"])
        P.dma(pw_b[:], pw_d.rearrange("(k p) n -> p k n", p=128), writes=["pw_b"], q="pool")
        P.op("pool", lambda e: e.memset(VA[:], 1.0), (), [("VA", j) for j in range(NST)])
        for i4 in range(4):
            P.op("pool", lambda e, i4=i4: e.memset(VAm4[i4][:], 1.0), (), [("VAm", i4)])
        P.op("pool", lambda e: e.memset(halo[:], 0.0), (), [("halo", c) for c in range(4)])
        P.op("pool", lambda e: e.memset(Fcar[:], 0.0), (), ["Fcar"])
        P.op("pool", lambda e: e.memset(CT[:], 0.0), (), ["CT"])
        P.op("pool", lambda e: e.memset(CTb[:], 0.0), (), ["CTb"])
        P.op("pool", lambda e: e.memset(qfk[:], 0.0), (), ["qfk"])
        P.op("pool", lambda e: e.memset(KF[:], 0.0), (), [("KF", j) for j in range(NST)])
        for h in range(2):
            qv = qfk[:, h, 0, :].rearrange("p (a c) -> p a c", c=32)
            kv = qfk[:, h, 1, :].rearrange("p (a c) -> p a c", c=32)
            P.op("pool", lambda e, qv=qv: e.memset(qv[:, :, 3:6], 1.0), (), ["qfk"])
            P.op("pool", lambda e, kv=kv: e.memset(kv[:, :, 0:3], 1.0), (), ["qfk"])

        w1v = w1_d.rearrange("(k p) n -> p k n", p=128)
        for k in range(8):
            P.dma(W1[:, k, :], w1v[:, k, :], writes=[("W1", k)], q="pool")

        def w1keys(c0, c1, k):
            return [("W1", k)]

        prot = 0
        for j in range(NST):
            for a in range(4 if 'noA' not in dbg else 0):
                i = 4 * j + a
                xb = xt[i % 2]
                xk = ("xt", i % 2)
                P.dma(xb[:], xtile(i), writes=[xk])
                P.act(junk_b[:], xb[:], AF.Square, [xk], ["xn", "ss"], accum_out=ss[:, 0:1])
                P.act(ss[:, 1:2], ss[:, 0:1], AF.Ln, ["ss"], ["ss1"], scale=1.0 / 1024, bias=EPS)
                P.act(ss[:, 1:2], ss[:, 1:2], AF.Exp, ["ss1"], ["ss1"], scale=-0.5)
                P.stt(xn[:], xb[:], ss[:, 1:2], rows[:, 512:1536], ALU.mult, ALU.mult, [xk, "ss1", "rows"], ["xn"])
                for k in range(8):
                    P.tr(b5b[:, k * 128:(k + 1) * 128], xn[:, k * 128:(k + 1) * 128], IDb, ["xn", "cst_b"], [bkey[5]])
                P.cp("act", hT[:, :, a * 128:(a + 1) * 128], b5b[:, 0:1024].rearrange("p (k n) -> p k n", n=128),
                     [bkey[5]], ["hT"])

            PROT = [4, 0, 1, 2, 3]
            for c in range(8 if stage >= 1 else 0):
                pb = PROT[prot % 5]; prot += 1
                bank4, bkey4 = bank[pb], bkey[pb]
                for k in range(8):
                    P.mm(bank4[:, 0:512], W1[:, k, c * 128:(c + 1) * 128], hT[:, k, :], k == 0, k == 7,
                         ["hT"] + w1keys(c * 128, (c + 1) * 128, k), [bkey4])
                if c < 2:
                    P.act(QT[:, c, :], bank4[:, 0:512], AF.Copy, [bkey4], ["QT"], scale=128 ** -0.5)
                elif c < 4:
                    P.cp("dve", KT[:, c - 2, j * 512:(j + 1) * 512], bank4[:, 0:512], [bkey4], [("KT", j)])
                else:
                    cc = c - 4
                    P.cp("act", cin[:, 3:515], bank4[:, 0:512], [bkey4], ["cin"])
                    P.cp("dve", cin[:, 0:3], halo[:, cc, :], [("halo", cc)], ["cin"])
                    P.ts(ctmp[:], cin[:, 0:512], vec[:, 8 + cc * 4:9 + cc * 4], None, ALU.mult, None,
                         ["cin", "vec"], ["ctmp"])
                    for t in range(1, 4):
                        P.stt(ctmp[:], cin[:, t:t + 512], vec[:, 8 + cc * 4 + t:9 + cc * 4 + t], ctmp[:],
                              ALU.mult, ALU.add, ["cin", "vec", "ctmp"], ["ctmp"])
                    P.cp("dve", halo[:, cc, :], cin[:, 512:515], ["cin"], [("halo", cc)])
                    P.act(csig[:], ctmp[:], AF.Sigmoid, ["ctmp"], ["csig"])
                    P.stt(qkT[:, cc, :], ctmp[:], (256 ** -0.5) if cc < 2 else 1.0, csig[:], ALU.mult, ALU.mult,
                          ["ctmp", "csig"], ["qkT"])

            for a in range(4 if stage >= 2 else 0):
                i = 4 * j + a
                ts_ = slice(a * 128, (a + 1) * 128)
                VAm, gateA, gateC, sp_, cs, lsv = VAm4[a], gateA4[a], gateC4[a], sp4[a], cs4[a], ls4[a]
                kVAm, kgA, kgC, ksp, kcs, kls = ("VAm", a), ("gateA", a), ("gateC", a), ("sp", a), ("cs", a), ("ls", a)
                groups = [(0, 512), (512, 1024), (1024, 1536), (1536, 1796)]
                for gi, (c0, c1) in enumerate(groups):
                    n = c1 - c0
                    pb = PROT[prot % 5]; prot += 1
                    bank4, bkey4 = bank[pb], bkey[pb]
                    sg = sgs[prot % 2]
                    sgk = ("sg", prot % 2)
                    for k in range(8):
                        P.mm(bank4[:, 0:n], hT[:, k, ts_], W1[:, k, NFM + c0:NFM + c1], k == 0, k == 7,
                             ["hT"] + w1keys(NFM + c0, NFM + c1, k), [bkey4])
                    lo = bank4[:, 0:256]
                    hi = bank4[:, 256:512]
                    if gi == 0:
                        P.cp("dve", VA[:, i, :, 0:128], lo.rearrange("p (h d) -> p h d", d=128), [bkey4], [("VA", j)])
                        P.act(sg[:], hi, AF.Sigmoid, [bkey4], [sgk])
                        P.tt(gateB[a][:], sg[:], hi, ALU.mult, [sgk, bkey4], [("gateB", a)])
                    elif gi == 1:
                        P.cp("dve", VAm[:, 0:256], lo, [bkey4], [kVAm])
                        P.act(sg[:], hi, AF.Sigmoid, [bkey4], [sgk])
                        P.tt(gateA[:], sg[:], rows[:, 0:256], ALU.mult, [sgk, "rows"], [kgA])
                    elif gi == 2:
                        P.act(sg[:], lo, AF.Sigmoid, [bkey4], [sgk])
                        P.tt(sg[:], sg[:], lo, ALU.mult, [sgk, bkey4], [sgk])
                        P.tt(gateA[:], gateA[:], sg[:], ALU.mult, [sgk, kgA], [kgA])
                        P.cp("act", ut[i % 5][:], hi, [bkey4], [("ut", i % 5)])
                    else:
                        P.act(sg[:], lo, AF.Sigmoid, [bkey4], [sgk])
                        P.tt(sg[:], sg[:], lo, ALU.mult, [sgk, bkey4], [sgk])
                        P.tt(gateC[:], sg[:], rows[:, 256:512], ALU.mult, [sgk, "rows"], [kgC])
                        P.tt(sp_[:], vec[:, 24:28], bank4[:, 256:260], ALU.add, ["vec", bkey4], [ksp])

                if stage < 3:
                    continue
                P.act(sm[:, 0:3], sp_[:, 1:4], AF.Abs, [ksp], ["sm0"])
                P.act(sm[:, 3:6], sm[:, 0:3], AF.Exp, ["sm0"], ["sm3"], scale=-1.0)
                P.act(sm[:, 6:9], sm[:, 3:6], AF.Ln, ["sm3"], ["sm6"], bias=1.0)
                P.ts(sm[:, 9:12], sp_[:, 1:4], 0.0, None, ALU.min, None, [ksp], ["sm9"])
                P.tt(lsv[:, 0:3], sm[:, 9:12], sm[:, 6:9], ALU.subtract, ["sm9", "sm6"], [kls])
                ls = lsv[:, 0:3]
                P.mm(bank[7][:, 260:263], UIN, ls, True, True, [kls, "cst_f"], [bkey[7]])
                P.mm(bank[7][:, 263:264], UREV, lsv[:, 0:1], True, True, [kls, "cst_f"], [bkey[7]])
                P.mm(bank[7][:, 264:267], ONESf, ls, True, True, [kls, "cst_f"], [bkey[7]])
                P.cp("dve", cs[:, 0:7], bank[7][:, 260:267], [bkey[7]], [kcs])
                if a == 0:
                    P.cp("dve", Fref[:], Fcar[:], ["Fcar"], ["Fref"])
                P.tt(fw[:, 0:2], cs[:, 1:3], Fcar[:], ALU.add, [kcs, "Fcar"], ["fw0"])
                P.cp("dve", Fk[:, i, :], fw[:, 0:2], ["fw0"], [("Fk", j)])
                P.tt(sm[:, 29:31], fw[:, 0:2], Fref[:], ALU.subtract, ["fw0", "Fref"], ["sm29"])
                P.act(fq[:, a, :], sm[:, 29:31], AF.Exp, ["sm29"], [("fq", a)])
                P.tt(Fcar[:], Fcar[:], cs[:, 5:7], ALU.add, [kcs, "Fcar"], ["Fcar"])
                P.cp("dve", f3b[:, :, 0], fw[:, 0:2], ["fw0"], ["f3b0"])
                P.cp("dve", fw[:, 2:4], f3b[:, :, 0], ["f3b0"], ["fw2"])
                P.tt(fw[:, 4:6], fw[:, 0:2], fw[:, 2:4], ALU.subtract, ["fw0", "fw2"], ["fw4"])
                P.cp("dve", f3b[:, :, 1], fw[:, 4:6], ["fw4"], ["f3b1"])
                P.cp("dve", fw[:, 2:4], f3b[:, :, 1], ["f3b1"], ["fw2"])
                P.tt(fw[:, 6:8], fw[:, 4:6], fw[:, 2:4], ALU.subtract, ["fw4", "fw2"], ["fw6"])
                P.cp("dve", f3b[:, :, 2], fw[:, 6:8], ["fw6"], ["f3b2"])
                P.ts(f3n[:], f3b[:], -1.0, None, ALU.mult, None, ["f3b0", "f3b1", "f3b2"], ["f3n"])
                ak = i // GT
                for h in range(2):
                    qv = qfk[:, h, 0, :].rearrange("p (a c) -> p a c", c=32)
                    for a4 in range(4):
                        P.cp("dve", qv[:, a4, 0:3], f3b[:, h, :], ["f3b0", "f3b1", "f3b2"], ["qfk"])
                    P.cp("dve", qfk[:, h, 1, 32 * ak + 3:32 * ak + 6], f3n[:, h, :], ["f3n"], ["qfk"])
                for h in range(2):
                    for w in range(2):
                        P.tr(b5b[:, (2 * h + w) * 128:(2 * h + w + 1) * 128], qfk[:, h, w, :], IDb,
                             ["qfk", "cst_b"], [bkey[5]])
                for h in range(2):
                    P.cp("act", QF[:, h, ts_], b5b[:, (2 * h) * 128:(2 * h + 1) * 128], [bkey[5]], ["QF"])
                    kcol = (i % GT) * 128
                    P.cp("dve", KF[32 * ak:32 * ak + 6, h, kcol:kcol + 128],
                         b5b[32 * ak:32 * ak + 6, (2 * h + 1) * 128:(2 * h + 2) * 128], [bkey[5]], [("KF", j)])

            P.capture_begin()
            for a in range(4 if stage >= 4 else 0):
                i = 4 * j + a
                ts_ = slice(a * 128, (a + 1) * 128)
                VAm, gateA, gateC, sp_, cs, lsv = VAm4[a], gateA4[a], gateC4[a], sp4[a], cs4[a], ls4[a]
                kVAm, kgA, kgC, ksp, kcs, kls = ("VAm", a), ("gateA", a), ("gateC", a), ("sp", a), ("cs", a), ("ls", a)
                P.mm(bank[6][:, 0:128], qkT[:, 2, ts_], qkT[:, 0, ts_], True, False, ["qkT"], [bkey[6]])
                P.mm(bank[6][:, 0:128], qkT[:, 3, ts_], qkT[:, 1, ts_], False, True, ["qkT"], [bkey[6]])
                P.ts(Bm[:], UIN, lsv[:, 0:1], None, ALU.mult, None, ["cst_f", kls], ["Bm"])
                P.mm(bank[6][:, 128:256], UREV, Bm[:], True, False, ["Bm", "cst_f"], [bkey[6]])
                P.mm(bank[6][:, 128:256], IDb, MASKb, False, True, ["cst_b"], [bkey[6]])
                P.act(ET[:], bank[6][:, 128:256], AF.Exp, [bkey[6], ksp], ["ET"], bias=sp_[:, 0:1])
                P.tt(PTm[:], ET[:], bank[6][:, 0:128], ALU.mult, ["ET", bkey[6]], ["PTm"])
                P.tr(b5b[:, 0:128], qkT[:, 2, ts_], IDb, ["qkT", "cst_b"], [bkey[5]])
                P.tr(b5b[:, 128:256], qkT[:, 3, ts_], IDb, ["qkT", "cst_b"], [bkey[5]])
                P.cp("act", Ktok[:], b5b[:, 0:256], [bkey[5]], ["Ktok"])
                P.act(sm[:, 16:17], cs[:, 0:1], AF.Exp, [kcs], ["sm16"])
                P.act(sm[:, 17:18], cs[:, 3:4], AF.Exp, [kcs, ksp], ["sm17"], bias=sp_[:, 0:1])
                P.act(sm[:, 18:19], cs[:, 4:5], AF.Exp, [kcs], ["sm18"])
                P.ts(Vw[:], VAm[:], sm[:, 17:18], None, ALU.mult, None, [kVAm, "sm17"], ["Vw"])
                P.mm(bank[7][:, 0:257], qkT[:, 0, ts_], CTb[:, 0, :], True, False, ["qkT", "CTb"], [bkey[7]])
                P.mm(bank[7][:, 0:257], qkT[:, 1, ts_], CTb[:, 1, :], False, True, ["qkT", "CTb"], [bkey[7]])
                P.act(Gs[:], bank[7][:, 0:257], AF.Copy, [bkey[7], "sm16"], ["Gs"], scale=sm[:, 16:17])
                P.mm(bank[7][:, 0:257], PTm[:], VAm[:], True, True, ["PTm", kVAm], [bkey[7]])
                P.tt(Gs[:], Gs[:], bank[7][:, 0:257], ALU.add, ["Gs", bkey[7]], ["Gs"])
                for c in range(2):
                    P.mm(bank[7][:, 0:257], Ktok[:, c * 128:(c + 1) * 128], Vw[:], True, True, ["Ktok", "Vw"], [bkey[7]])
                    P.stt(CT[:, c, :], CT[:, c, :], sm[:, 18:19], bank[7][:, 0:257], ALU.mult, ALU.add,
                          ["CT", "sm18", bkey[7]], ["CT"])
                    P.cp("act", CTb[:, c, :], CT[:, c, :], ["CT"], ["CTb"])
                P.act(sm[:, 19:20], Gs[:, 256:257], AF.Abs, ["Gs"], ["sm19"])
                P.ts(sm[:, 20:21], sm[:, 19:20], 1.0, None, ALU.max, None, ["sm19"], ["sm20"])
                P.op("dve", lambda e: e.reciprocal(out=sm[:, 21:22], in_=sm[:, 20:21]), ["sm20"], ["sm21"])
                P.act(sgs[0][:], Gs[:, 0:256], AF.Square, ["Gs"], [("sg", 0), "sm22"], accum_out=sm[:, 22:23])
                P.tt(sm[:, 23:24], sm[:, 21:22], sm[:, 21:22], ALU.mult, ["sm21"], ["sm23"])
                P.tt(sm[:, 24:25], sm[:, 23:24], sm[:, 22:23], ALU.mult, ["sm23", "sm22"], ["sm24"])
                P.act(sm[:, 25:26], sm[:, 24:25], AF.Ln, ["sm24"], ["sm25"], scale=1.0 / 256, bias=EPS)
                P.act(sm[:, 26:27], sm[:, 25:26], AF.Exp, ["sm25"], ["sm26"], scale=-0.5)
                P.tt(sm[:, 27:28], sm[:, 26:27], sm[:, 21:22], ALU.mult, ["sm26", "sm21"], ["sm27"])
                yk = ("yst", a)
                P.stt(yst[a][:, 0:256], Gs[:, 0:256], sm[:, 27:28], gateA[:], ALU.mult, ALU.mult,
                      ["Gs", "sm27", kgA], [yk])

                if stage < 5:
                    continue
                ucur = ut[i % 5]
                uprev = ut[(i - 1) % 5]
                for cc in range(2):
                    o_ = bank[6][:, 256 + cc * 128:384 + cc * 128]
                    if i == 0:
                        P.mm(o_, ucur[:, cc * 128:(cc + 1) * 128], MFIRST, True, True, [("ut", i % 5), "cst_b"], [bkey[6]])
                    else:
                        P.mm(o_, ucur[:, cc * 128:(cc + 1) * 128], MCUR, True, False, [("ut", i % 5), "cst_b"], [bkey[6]])
                        P.mm(o_, uprev[:, cc * 128:(cc + 1) * 128], MPREV, False, True,
                             [("ut", (i - 1) % 5), "cst_b"], [bkey[6]])
                P.cp("act", dTb[:], bank[6][:, 256:512].rearrange("p (c n) -> p c n", n=128), [bkey[6]], ["dTb"])
                for cc in range(2):
                    P.mm(bank[7][:, 0:256], dTb[:, cc, :], pw_b[:, cc, :], cc == 0, cc == 1, ["dTb", "pw_b"], [bkey[7]])
                P.tt(yst[a][:, 512:768], gateC[:], bank[7][:, 0:256], ALU.mult, [kgC, bkey[7]], [yk])

            deferred = P.capture_end()
            deferred.reverse()
            its = [(h, kt) for h in range(2 if stage >= 6 else 0) for kt in range(4 * j + 4)]

            if j > 0 and stage >= 6:
                for h in range(2):
                    P.ts(nbias[:, h, 0:4 * j], Fk[:, 0:4 * j, h], -1.0, Fref[:, h:h + 1], ALU.mult, ALU.add,
                         [("Fk", jj) for jj in range(j)] + ["Fref"], [("nbias", h)])

            def scores(idx):
                h, kt = its[idx]
                n0 = max(0, kt - 4 * j)
                q0 = n0 * 128
                ak = kt // GT
                kcol = (kt % GT) * 128
                sb_ = idx % 2
                stt_ = bank[sb_][:, q0:512]
                diag = kt >= 4 * j
                if not diag:
                    P.mm(stt_, KT[:, h, kt * 128:(kt + 1) * 128], QT[:, h, q0:512], True, True,
                         [("KT", kt // 4), "QT"], [bkey[sb_]])
                    P.act(PT[sb_][:, q0:512], stt_, AF.Exp, [bkey[sb_], ("nbias", h)], [("PT", sb_)],
                          bias=nbias[:, h, kt:kt + 1])
                    return
                P.mm(stt_, KT[:, h, kt * 128:(kt + 1) * 128], QT[:, h, q0:512], True, False,
                     [("KT", kt // 4), "QT"], [bkey[sb_]])
                P.mm(stt_, KF[32 * ak:32 * ak + 6, h, kcol:kcol + 128], QF[32 * ak:32 * ak + 6, h, q0:512],
                     False, False, [("KF", kt // 4), "QF"], [bkey[sb_]])
                P.mm(bank[sb_][:, q0:q0 + 128], IDb, MASKb, False, True, ["cst_b"], [bkey[sb_]])
                P.act(PT[sb_][:, q0:512], stt_, AF.Exp, [bkey[sb_]], [("PT", sb_)])

            per = (len(deferred) + max(1, len(its) - 2) - 1) // max(1, len(its) - 2)

            def replay(n):
                for _ in range(n):
                    if deferred:
                        P.op(*deferred.pop())
            if its:
                scores(0)
            for idx, (h, kt) in enumerate(its):
                if idx + 1 < len(its):
                    scores(idx + 1)
                replay(per)
                n0 = max(0, kt - 4 * j)
                sb_ = idx % 2
                for n in range(n0, 4):
                    ob = 2 + n // 2
                    oc = (n % 2) * 256
                    first = (kt == 0) or (kt == 4 * j)
                    P.mm(bank[ob][:, oc:oc + 129], PT[sb_][:, n * 128:(n + 1) * 128], VA[:, kt, h, :],
                         first and (n % 2 == 0), (kt == 4 * j + n) or (kt == 4 * j - 1),
                         [("PT", sb_), ("VA", kt // 4)], [bkey[ob]])
                if j > 0 and kt == 4 * j - 1:
                    for n in range(4):
                        ob = 2 + n // 2
                        oc = (n % 2) * 256
                        P.ts(Osb[:, n, :], bank[ob][:, oc:oc + 129], fq[:, n, h:h + 1], None, ALU.mult, None,
                             [bkey[ob], ("fq", n)], [("Osb", n)])
                if kt == 4 * j + 3:
                    for n in range(4):
                        ob = 2 + n // 2
                        oc = (n % 2) * 256
                        if j > 0:
                            P.tt(Osb[:, n, :], Osb[:, n, :], bank[ob][:, oc:oc + 129], ALU.add, [("Osb", n), bkey[ob]], [("Osb", n)])
                            P.op("dve", lambda e, n=n: e.reciprocal(out=sm[:, 28:29], in_=Osb[:, n, 128:129]), [("Osb", n)], ["sm28"])
                            P.stt(yst[n][:, 256 + h * 128:384 + h * 128], Osb[:, n, 0:128], sm[:, 28:29],
                                  gateB[n][:, h * 128:(h + 1) * 128], ALU.mult, ALU.mult,
                                  [("Osb", n), "sm28", ("gateB", n)], [("yst", n)])
                        else:
                            P.op("dve", lambda e, ob=ob, oc=oc: e.reciprocal(out=sm[:, 28:29], in_=bank[ob][:, oc + 128:oc + 129]),
                                 [bkey[ob]], ["sm28"])
                            P.stt(yst[n][:, 256 + h * 128:384 + h * 128], bank[ob][:, oc:oc + 128], sm[:, 28:29],
                                  gateB[n][:, h * 128:(h + 1) * 128], ALU.mult, ALU.mult,
                                  [bkey[ob], "sm28", ("gateB", n)], [("yst", n)])
            replay(len(deferred))
            for n in range(4):
                i = 4 * j + n
                P.dma(ys_d[i * 128:(i + 1) * 128, :], yst[n][:], reads=[("yst", n)], writes=[("ys", i)])
            if after_st is not None:
                after_st(j, [("ys", 4 * j + n) for n in range(4)])
    return [("ys", i) for i in range(NT)]


def _consts(g):
    s = np.arange(128)[:, None]
    t = np.arange(128)[None, :]
    ident = (s == t).astype(np.float32)
    uin = (s <= t).astype(np.float32)
    urev = (s > t).astype(np.float32)
    ones = np.ones((128, 128), np.float32)
    maskT = np.where(s > t, -30000.0, 0.0).astype(np.float32)
    W = POOL_WINDOWS[g]
    def mt(first):
        M = np.zeros((128, 256), np.float32)
        for tt in range(128):
            cnt = min(tt + 1, W) if first else W
            for jj in range(cnt):
                M[tt, 128 + tt - jj] += 1.0 / cnt
            M[tt, 128 + tt] -= 1.0
        return M
    Mg = mt(False)
    Mf = mt(True)
    mcur = Mg[:, 128:].T.copy()
    mprev = Mg[:, :128].T.copy()
    mfirst = Mf[:, 128:].T.copy()
    return np.concatenate([ident, uin, urev, ones, maskT, mcur, mfirst, mprev], axis=1).astype(np.float32)


def _p1_cols(g):
    W = 1024
    off = {}
    names = ["aq", "ak", "av", "ao", "az"]
    o = 0
    for nme in names:
        off[nme] = o
        o += W
    off["ai"] = o; o += 4
    off["af"] = o; o += 4
    for nme in ["bq", "bk", "bv", "bz"]:
        off[nme] = o
        o += W
    off["bf"] = o; o += 8
    off["cu"] = o; o += W
    off["cz"] = o; o += W
    off["gates"] = o
    r = lambda nme: np.arange(off[nme] + g * 256, off[nme] + (g + 1) * 256)
    cols = np.concatenate([
        r("bq"), r("bk"), r("aq"), r("ak"),
        r("bv"), r("bz"), r("av"), r("ao"), r("az"), r("cu"), r("cz"),
        np.array([off["ai"] + g, off["af"] + g, off["bf"] + 2 * g, off["bf"] + 2 * g + 1]),
    ])
    return cols, off


def p1_inputs(x_b, l, g, inp):
    cols, off = _p1_cols(g)
    w1 = np.ascontiguousarray(inp["w_in"][l][:, cols])
    vec = np.zeros((128, 32), np.float32)
    vec[:, 0:8] = inp["norm_g"][l].reshape(8, 128).T
    cw = inp["conv_w"][l]
    for cc in range(4):
        base = (0 if cc < 2 else 1024) + g * 256 + (cc % 2) * 128
        vec[:, 8 + cc * 4:12 + cc * 4] = cw[:, base:base + 128].T
    vec[:, 24] = inp["ml_bi"][l][g]
    vec[:, 25] = inp["ml_bf"][l][g]
    vec[:, 26] = inp["fox_bf"][l][2 * g]
    vec[:, 27] = inp["fox_bf"][l][2 * g + 1]
    rows = np.zeros((128, 1536), np.float32)
    rows[:, 0:256] = inp["ml_norm_g"][l][g * 256:(g + 1) * 256][None, :]
    rows[:, 256:512] = inp["pool_scale"][l][g * 256:(g + 1) * 256][None, :]
    rows[:, 512:1536] = inp["norm_g"][l][None, :]
    return dict(x=np.ascontiguousarray(x_b), w1=w1, vecs=vec, rows=rows,
                poolw=np.ascontiguousarray(inp["pool_w"][l][g]), consts=_consts(g))


def p2_record(P, nc, st, bank, T, last, xsrc, ysload, wg_d, wb_d, wo_d, vec_d, fg_d, id_d, odst, pre=None, after_tile=None, post_w=None):
    NT = T // 128
    if True:
        tag = f"sb{P.nflush}_"
        sb = lambda n, s, d: st.enter_context(nc.sbuf_tensor(tag + n, s, d))
        bkey = [f"bk{b}" for b in range(8)]
        Wg = sb("Wg", [128, 8, 3072], BF16)
        Wb = sb("Wb", [128, 24, 1024], BF16)
        Wo = sb("Wo", [128, 8, 1024], BF16)
        grow = sb("grow", [128, 1024], F32)
        fg = sb("fg", [128, 1024], F32)
        idf = sb("idf", [128, 128], F32)
        idb = sb("idb", [128, 128], BF16)
        xt = [sb(f"xt{i}", [128, 1024], F32) for i in range(2)]
        yst = sb("yst", [128, 3072], BF16)
        ysT = sb("ysT", [128, 24, 128], BF16)
        junk_b = sb("junk_b", [128, 1024], BF16)
        ss = sb("ss", [128, 4], F32)
        xn = sb("xn", [128, 1024], BF16)
        hT = sb("hT", [128, 8, 128], BF16)
        gsb = sb("gsb", [128, 3072], F32)
        mrg = sb("mrg", [128, 1024], F32)
        tmp = sb("tmp", [128, 512], F32)
        mb = sb("mb", [128, 1024], BF16)
        mT = sb("mT", [128, 8, 128], BF16)
        xo = [sb(f"xo{i}", [128, 1024], F32) for i in range(2)]

        P.dma(grow[:], vec_d, writes=["grow"])
        P.dma(fg[:], fg_d, writes=["fg"])
        P.dma(idf[:], id_d, writes=["idf"])
        P.cp("dve", idb[:], idf[:], ["idf"], ["idb"])
        wgv = wg_d.rearrange("(k p) n -> p k n", p=128)
        wbv = wb_d.rearrange("(k p) n -> p k n", p=128)
        wov = wo_d.rearrange("(k p) n -> p k n", p=128)
        for k in range(8):
            P.dma(Wg[:, k, :], wgv[:, k, :], writes=[("Wg", k)], q="pool")
        for k in range(24):
            P.dma(Wb[:, k, :], wbv[:, k, :], writes=[("Wb", k)], q="pool")
        for k in range(8):
            P.dma(Wo[:, k, :], wov[:, k, :], writes=[("Wo", k)], q="pool")
        if post_w is not None:
            post_w()
        wkeys = lambda key, k, c0, c1: [(key, k)]

        mmb = 0
        trb = 0
        for i in range(NT):
            xb = xt[i % 2]
            xk = ("xt", i % 2)
            pr = list(pre(i)) if pre is not None else []
            P.dma(xb[:], xsrc(i), reads=pr, writes=[xk])
            for (o_, i_) in ysload(i, yst):
                P.dma(o_, i_, reads=pr, writes=["yst"])
            P.act(junk_b[:], xb[:], AF.Square, [xk], ["junk_b", "ss"], accum_out=ss[:, 0:1])
            P.act(ss[:, 1:2], ss[:, 0:1], AF.Ln, ["ss"], ["ss1"], scale=1.0 / 1024, bias=EPS)
            P.act(ss[:, 1:2], ss[:, 1:2], AF.Exp, ["ss1"], ["ss1"], scale=-0.5)
            P.stt(xn[:], xb[:], ss[:, 1:2], grow[:], ALU.mult, ALU.mult, [xk, "ss1", "grow"], ["xn"])
            tb = 4 + trb % 2; trb += 1
            tbv = bank[tb][:].bitcast(BF16)
            for k in range(8):
                P.tr(tbv[:, k * 128:(k + 1) * 128], xn[:, k * 128:(k + 1) * 128], idb[:], ["xn", "idb"], [bkey[tb]])
            P.cp("act", hT[:], tbv[:, 0:1024].rearrange("p (k n) -> p k n", n=128), [bkey[tb]], ["hT"])
            for n in range(3):
                tb = 4 + trb % 2; trb += 1
                tbv = bank[tb][:].bitcast(BF16)
                for k in range(8):
                    c = n * 1024 + k * 128
                    P.tr(tbv[:, k * 128:(k + 1) * 128], yst[:, c:c + 128], idb[:], ["yst", "idb"], [bkey[tb]])
                P.cp("dve" if n % 2 == 0 else "act", ysT[:, n * 8:(n + 1) * 8, :],
                     tbv[:, 0:1024].rearrange("p (k n) -> p k n", n=128), [bkey[tb]], [("ysT", n)])
            for cg in range(6):
                b = mmb % 4; mmb += 1
                for k in range(8):
                    P.mm(bank[b][:, 0:512], hT[:, k, :], Wg[:, k, cg * 512:(cg + 1) * 512], k == 0, k == 7,
                         ["hT"] + wkeys("Wg", k, cg * 512, (cg + 1) * 512), [bkey[b]])
                P.act(gsb[:, cg * 512:(cg + 1) * 512], bank[b][:, 0:512], AF.Sigmoid, [bkey[b]], [("gsb", cg)])
            for n in range(3):
                for hf in range(2):
                    b = mmb % 4; mmb += 1
                    for k in range(8):
                        P.mm(bank[b][:, 0:512], ysT[:, n * 8 + k, :], Wb[:, n * 8 + k, hf * 512:(hf + 1) * 512], k == 0, k == 7,
                             [("ysT", n)] + wkeys("Wb", n * 8 + k, hf * 512, (hf + 1) * 512), [bkey[b]])
                    gv = gsb[:, n * 1024 + hf * 512:n * 1024 + (hf + 1) * 512]
                    gk = ("gsb", n * 2 + hf)
                    mk = ("mrg", hf)
                    mv = mrg[:, hf * 512:(hf + 1) * 512]
                    if n == 0:
                        P.tt(mv, gv, bank[b][:, 0:512], ALU.mult, [gk, bkey[b]], [mk])
                    else:
                        P.tt(tmp[:], gv, bank[b][:, 0:512], ALU.mult, [gk, bkey[b]], ["tmp"])
                        if n == 1:
                            P.tt(mv, mv, tmp[:], ALU.add, [mk, "tmp"], [mk], eng="pool")
                        else:
                            P.tt(mb[:, hf * 512:(hf + 1) * 512], mv, tmp[:], ALU.add, [mk, "tmp"], [("mb", hf)], eng="pool")
            tb = 4 + trb % 2; trb += 1
            tbv = bank[tb][:].bitcast(BF16)
            for k in range(8):
                P.tr(tbv[:, k * 128:(k + 1) * 128], mb[:, k * 128:(k + 1) * 128], idb[:], [("mb", k // 4), "idb"], [bkey[tb]])
            P.cp("act", mT[:], tbv[:, 0:1024].rearrange("p (k n) -> p k n", n=128), [bkey[tb]], ["mT"])
            ob = xo[i % 2]
            ok = ("xo", i % 2)
            for hf in range(2):
                b = mmb % 4; mmb += 1
                for k in range(8):
                    P.mm(bank[b][:, 0:512], mT[:, k, :], Wo[:, k, hf * 512:(hf + 1) * 512], k == 0, k == 7,
                         ["mT"] + wkeys("Wo", k, hf * 512, (hf + 1) * 512), [bkey[b]])
                P.tt(ob[:, hf * 512:(hf + 1) * 512], xb[:, hf * 512:(hf + 1) * 512], bank[b][:, 0:512], ALU.add,
                     [xk, bkey[b]], [ok])
            if last:
                P.act(junk_b[:], ob[:], AF.Square, [ok], ["junk_b", "ss2"], accum_out=ss[:, 2:3])
                P.act(ss[:, 3:4], ss[:, 2:3], AF.Ln, ["ss2"], ["ss3"], scale=1.0 / 1024, bias=EPS)
                P.act(ss[:, 3:4], ss[:, 3:4], AF.Exp, ["ss3"], ["ss3"], scale=-0.5)
                P.stt(ob[:], ob[:], ss[:, 3:4], fg[:], ALU.mult, ALU.mult, [ok, "ss3", "fg"], [ok])
            P.dma(odst(i), ob[:], reads=[ok], writes=[("o", i)])
            if after_tile is not None:
                after_tile(i)
    return [("o", i) for i in range(NT)]


def p2_inputs(x_sh, ys_sh, l, inp):
    o = 5 * 1024 + 8 + 4 * 1024 + 8 + 2 * 1024
    return dict(x=np.ascontiguousarray(x_sh), ysin=np.ascontiguousarray(ys_sh),
                wg=np.ascontiguousarray(inp["w_in"][l][:, o:o + 3072]),
                wb=np.ascontiguousarray(inp["w_branch"][l].reshape(3072, 1024)),
                wo=np.ascontiguousarray(inp["w_out"][l]),
                vecs=np.ascontiguousarray(inp["norm_g"][l].reshape(8, 128).T),
                fgrows=np.ascontiguousarray(np.broadcast_to(inp["final_g"][None, :], (128, 1024))),
                ident=np.eye(128, dtype=np.float32))


GROUPS = [[0, 1, 2, 3], [4, 5, 6, 7]]


class _QCtx:
    def __init__(self, h):
        self.q = h.partition_id() % 4
        self.cache = {}

    def mul(self, m):
        if m == 1:
            return self.q
        if m not in self.cache:
            self.cache[m] = self.q * m
        return self.cache[m]


def build_fused(S=SEQ):
    TS = S // 4
    nc = bass.Bass("TRN2", target_bir_lowering=False)
    dram = lambda n, sh, dt, kind: nc.dram_tensor(n, sh, dt, kind=kind).ap()
    x_d = dram("x", [S, 1024], F32, "ExternalInput")
    cst_d = dram("consts", [128, 1024], F32, "ExternalInput")
    fg_d = dram("fgrows", [128, 1024], F32, "ExternalInput")
    L = []
    for l in range(DEPTH):
        L.append(dict(
            w1=dram(f"w1_{l}", [1024, NW1], F32, "ExternalInput"),
            vec=dram(f"vecs_{l}", [128, 32], F32, "ExternalInput"),
            rows=dram(f"rows_{l}", [128, 1536], F32, "ExternalInput"),
            pw=dram(f"poolw_{l}", [256, 256], F32, "ExternalInput"),
            wg=dram(f"wg_{l}", [1024, 3072], F32, "ExternalInput"),
            wb=dram(f"wb_{l}", [3072, 1024], F32, "ExternalInput"),
            wo=dram(f"wo_{l}", [1024, 1024], F32, "ExternalInput"),
        ))
    out_d = dram("out", [TS, 1024], F32, "ExternalOutput")
    ysrc = nc.dram_tensor("ysrc", [S, 768], BF16).ap()
    yall = nc.dram_tensor("yall", [4 * S, 768], BF16).ap()
    x1src = nc.dram_tensor("x1src", [TS, 1024], F32).ap()
    x1all = nc.dram_tensor("x1all", [S, 1024], F32).ap()
    ysh = nc.dram_tensor("ysh", [4 * TS, 768], BF16).ap()
    xsh = nc.dram_tensor("xsh", [TS, 1024], F32).ap()

    with contextlib.ExitStack() as top:
        bank = [top.enter_context(nc.psum_tensor(f"bk{b}", [128, 512], F32)) for b in range(8)]
        fin = top.enter_context(nc.sbuf_tensor("sb_fin", [1, 8], F32))
        P = Prog(nc, top)
        P.op("pool", lambda e: e.memset(fin[:], 0.0), (), ["fin0"])
        NB = TS // 512
        for l in range(DEPTH):
            d = L[l]
            last = (l == DEPTH - 1)
            with contextlib.ExitStack() as st:
                if l == 0:
                    xtile = lambda i: x_d[i * 128:(i + 1) * 128, :]
                else:
                    def xtile(i):
                        tok = i * 128
                        r, k, t = tok // TS, (tok % TS) // 256, tok % 256
                        row = k * 1024 + r * 256 + t
                        return x1all[row:row + 128, :]

                def after_st(j, keys):
                    P.cc("AllGather", GROUPS, yall[j * 2048:(j + 1) * 2048, :], ysrc[j * 512:(j + 1) * 512, :],
                         reads=keys, writes=[("yall", j)])
                p1_record(P, nc, st, bank, S, xtile, d["w1"], d["vec"], d["rows"], d["pw"], cst_d, ysrc, after_st=after_st)
                P.flush(fin, bank[7])
            with contextlib.ExitStack() as st:
                yv = yall.rearrange("(blk p r) c -> blk p (r c)", p=128, r=16)
                yshv = ysh.rearrange("(k p r) c -> k p (r c)", p=128, r=16)
                def shard_copy(jj):
                    P.dma(yshv[jj:jj + 1], (lambda q, jj=jj: yv[jj:][bass.ds(q.mul(NB), 1)]), writes=[("ysh", jj)])
                    if l == 0:
                        P.dma(xshv[jj:jj + 1], (lambda q, jj=jj: xv[jj:][bass.ds(q.mul(NB), 1)]), writes=[("xsh", jj)])
                if l == 0:
                    xv = x_d.rearrange("(blk p r) c -> blk p (r c)", p=128, r=4)
                    xshv = xsh.rearrange("(k p r) c -> k p (r c)", p=128, r=4)
                    xsrc = lambda i: xsh[i * 128:(i + 1) * 128, :]
                else:
                    xsrc = lambda i: x1src[i * 128:(i + 1) * 128, :]
                shard_copy(0)

                def post_w():
                    for jj in range(1, NB):
                        shard_copy(jj)

                def ysload(i, yst):
                    prs = []
                    jj, t0 = i // 4, (i % 4) * 128
                    for r in range(4):
                        o_ = yst[:].rearrange("p (n r c) -> p n r c", n=3, r=4)[:, :, r, :]
                        row = jj * 2048 + r * 512 + t0
                        i_ = ysh[row:row + 128, :].rearrange("p (n c) -> p n c", n=3)
                        prs.append((o_, i_))
                    return prs
                if last:
                    odst = lambda i: out_d[i * 128:(i + 1) * 128, :]
                    after_tile = None
                else:
                    odst = lambda i: x1src[i * 128:(i + 1) * 128, :]

                    def after_tile(i):
                        if i % 2 == 1:
                            k = i // 2
                            P.cc("AllGather", GROUPS, x1all[k * 1024:(k + 1) * 1024, :], x1src[k * 256:(k + 1) * 256, :],
                                 reads=[("o", i - 1), ("o", i)], writes=[("x1all", k)])
                p2_record(P, nc, st, bank, TS, last, xsrc, ysload, d["wg"], d["wb"], d["wo"],
                          d["rows"][:, 512:1536], fg_d, cst_d[:, 0:128], odst, pre=lambda i: [("ysh", i // 4), ("xsh", i // 4)],
                          after_tile=after_tile, post_w=post_w)
                P.flush(fin, bank[7])
    return nc


_CACHE = {}


def kernel(x, norm_g, w_in, conv_w, ml_bi, ml_bf, ml_norm_g, fox_bf, pool_w, pool_scale, w_branch, w_out, final_g):
    inp = dict(norm_g=np.asarray(norm_g), w_in=np.asarray(w_in), conv_w=np.asarray(conv_w), ml_bi=np.asarray(ml_bi),
               ml_bf=np.asarray(ml_bf), ml_norm_g=np.asarray(ml_norm_g), fox_bf=np.asarray(fox_bf),
               pool_w=np.asarray(pool_w), pool_scale=np.asarray(pool_scale), w_branch=np.asarray(w_branch),
               w_out=np.asarray(w_out), final_g=np.asarray(final_g))
    x = np.asarray(x, dtype=np.float32)
    B, S, _ = x.shape
    TS = S // 4
    if "nc" not in _CACHE:
        _CACHE["nc"] = build_fused(S)
    nc = _CACHE["nc"]
    go = 5 * 1024 + 8 + 4 * 1024 + 8 + 2 * 1024
    fgrows = np.ascontiguousarray(np.broadcast_to(inp["final_g"][None, :], (128, 1024)))
    ins = []
    for c in range(8):
        b, g = c // 4, c % 4
        m = dict(x=np.ascontiguousarray(x[b]), consts=_consts(g), fgrows=fgrows)
        for l in range(DEPTH):
            p1 = p1_inputs(x[b], l, g, inp)
            m[f"w1_{l}"] = p1["w1"]
            m[f"vecs_{l}"] = p1["vecs"]
            m[f"rows_{l}"] = p1["rows"]
            m[f"poolw_{l}"] = p1["poolw"]
            m[f"wg_{l}"] = np.ascontiguousarray(inp["w_in"][l][:, go:go + 3072])
            m[f"wb_{l}"] = np.ascontiguousarray(inp["w_branch"][l].reshape(3072, 1024))
            m[f"wo_{l}"] = np.ascontiguousarray(inp["w_out"][l])
        ins.append(m)
    res = run_bass_kernel_spmd(nc, ins, core_ids=list(range(8)))
    out = np.empty((B, S, 1024), np.float32)
    for c in range(8):
        b, q = c // 4, c % 4
        out[b, q * TS:(q + 1) * TS] = np.asarray(res.results[c]["out"])
    return out
```

```python
import contextlib
import numpy as np
import ml_dtypes
import concourse.bass as bass
import concourse.mybir as mybir
from concourse.bass_utils import run_bass_kernel_spmd

F32 = mybir.dt.float32
BF16 = mybir.dt.bfloat16
AF = mybir.ActivationFunctionType
ALU = mybir.AluOpType

D_MODEL = 1024
SEQ = 8192
BATCH = 2
DEPTH = 2
N_IN = 14352
POOL_WINDOWS = (2, 4, 8, 16)
EPS = 1e-6

EPOCH = 20000
SAME_ENGINE_SYNC = True


class Prog:
    CE = ("pe", "act", "dve", "pool")
    ALLE = ("pe", "act", "dve", "pool", "sp")
    NEP = 4

    def __init__(self, nc, stack):
        self.nc = nc
        self.NDS = 24
        self.NCC = 8
        self.csem = {e: [stack.enter_context(nc.semaphore(f"s_{e}{i}")) for i in range(self.NEP)] for e in self.CE}
        self.dsem = [stack.enter_context(nc.semaphore(f"s_d{i}")) for i in range(self.NDS)]
        self.ccsem = [stack.enter_context(nc.semaphore(f"s_cc{i}")) for i in range(self.NCC)]
        self.sigcnt = {e: 0 for e in self.CE}
        self.dma_cnt = [0] * self.NDS
        self.dma_rr = 0
        self.dma_rr_pool = 0
        self.ncc = 0
        self.nflush = 0
        self._reset()

    def _reset(self):
        self.ops = {e: [] for e in self.ALLE}
        self.last_write = {}
        self.readers = {}
        self.seen = {e: {} for e in self.ALLE}
        self.signaled = {e: set() for e in self.CE}
        self.cc_pending = []

    def _need(self, eng, ev, waits):
        if ev is None:
            return
        if ev[0] == "c":
            _, e2, idx = ev
            if e2 == eng and (eng == "pe" or not SAME_ENGINE_SYNC):
                return
            k = ("c", e2)
            if self.seen[eng].get(k, -1) >= idx:
                return
            self.seen[eng][k] = idx
            waits.append(ev)
            self.signaled[e2].add(idx)
        elif ev[0] == "d":
            _, slot, cnt = ev
            k = ("d", slot)
            if self.seen[eng].get(k, 0) >= cnt:
                return
            self.seen[eng][k] = cnt
            waits.append(ev)
        else:
            k = ("x", ev[1])
            if k in self.seen[eng]:
                return
            self.seen[eng][k] = 1
            waits.append(ev)

    def _deps(self, eng, reads, writes):
        waits = []
        for r in reads:
            self._need(eng, self.last_write.get(r), waits)
        for w in writes:
            self._need(eng, self.last_write.get(w), waits)
            for ev in self.readers.get(w, ()):
                self._need(eng, ev, waits)
        return waits

    def _commit(self, ev, reads, writes):
        for r in reads:
            self.readers.setdefault(r, []).append(ev)
        for w in writes:
            self.last_write[w] = ev
            self.readers[w] = []

    def capture_begin(self):
        self._cap = []

    def capture_end(self):
        c, self._cap = self._cap, None
        return c

    def op(self, eng, fn, reads=(), writes=()):
        if getattr(self, "_cap", None) is not None:
            self._cap.append((eng, fn, reads, writes))
            return
        bk = [r for r in reads if isinstance(r, str) and r.startswith("bk")]
        if bk:
            writes = list(writes) + [b for b in bk if b not in writes]
            reads = [r for r in reads if r not in bk]
        waits = self._deps(eng, reads, writes)
        idx = len(self.ops[eng])
        self.ops[eng].append(dict(kind="c", fn=fn, waits=waits))
        self._commit(("c", eng, idx), reads, writes)

    def dma(self, out, in_, reads=(), writes=(), q="sp"):
        if q == "pool":
            slot = 16 + self.dma_rr_pool
            self.dma_rr_pool = (self.dma_rr_pool + 1) % 8
        else:
            slot = self.dma_rr
            self.dma_rr = (self.dma_rr + 1) % 16
        waits = self._deps(q, reads, writes)
        if self.dma_cnt[slot] > 0:
            self._need(q, ("d", slot, self.dma_cnt[slot]), waits)
        self.dma_cnt[slot] += 1
        ev = ("d", slot, self.dma_cnt[slot])
        self.ops[q].append(dict(kind="d", out=out, in_=in_, waits=waits, slot=slot))
        self._commit(ev, reads, writes)

    def cc(self, kind, groups, out, in_, reads=(), writes=()):
        waits = self._deps("pool", reads, writes)
        n = self.ncc
        self.ncc += 1
        ev = ("x", n)
        self.ops["pool"].append(dict(kind="cc", cck=kind, groups=groups, out=out, in_=in_, waits=waits, n=n))
        self.cc_pending.append(ev)
        self._commit(ev, reads, writes)

    def mm(self, out, lhsT, rhs, start, stop, reads, writes):
        self.op("pe", lambda e: e.matmul(out, lhsT=lhsT, rhs=rhs, start=start, stop=stop, skip_group_check=True),
                reads, writes)

    def tr(self, out, in_, ident, reads, writes):
        self.op("pe", lambda e: e.transpose(out=out, in_=in_, identity=ident), reads, writes)

    def act(self, out, in_, func, reads, writes, **kw):
        self.op("act", lambda e: e.activation(out=out, in_=in_, func=func, **kw), reads, writes)

    def ts(self, out, in0, s1, s2, op0, op1, reads, writes, eng="dve"):
        if op1 is None:
            self.op(eng, lambda e: e.tensor_scalar(out=out, in0=in0, scalar1=s1, scalar2=None, op0=op0), reads, writes)
        else:
            self.op(eng, lambda e: e.tensor_scalar(out=out, in0=in0, scalar1=s1, scalar2=s2, op0=op0, op1=op1),
                    reads, writes)

    def tt(self, out, in0, in1, op, reads, writes, eng="dve"):
        self.op(eng, lambda e: e.tensor_tensor(out=out, in0=in0, in1=in1, op=op), reads, writes)

    def stt(self, out, in0, scalar, in1, op0, op1, reads, writes, eng="dve"):
        self.op(eng, lambda e: e.scalar_tensor_tensor(out=out, in0=in0, scalar=scalar, in1=in1, op0=op0, op1=op1),
                reads, writes)

    def cp(self, eng, out, in_, reads, writes):
        if eng == "act":
            self.op("act", lambda e: e.copy(out=out, in_=in_), reads, writes)
        else:
            self.op(eng, lambda e: e.tensor_copy(out=out, in_=in_), reads, writes)

    def flush(self, fin, bank7):
        nc = self.nc
        fk = [("fin", e) for e in self.CE]
        waits = []
        for ev in self.cc_pending:
            self._need("pool", ev, waits)
        self.ops["pool"].append(dict(kind="w", waits=waits))
        self.op("act", lambda e: e.copy(out=fin[0:1, 0:1], in_=fin[0:1, 1:2]), ["fin0"], [fk[1]])
        self.op("dve", lambda e: e.memset(fin[0:1, 2:3], 0.0), (), [fk[2]])
        self.op("pool", lambda e: e.memset(fin[0:1, 3:4], 0.0), (), [fk[3]])
        self.op("pe", lambda e: e.matmul(bank7[0:1, 511:512], lhsT=fin[0:1, 4:5], rhs=fin[0:1, 5:6], start=True, stop=True,
                                         skip_group_check=True), ["bk7", "fin0"], [fk[0]])
        for e in self.ALLE:
            waits = []
            for k in fk:
                self._need(e, self.last_write.get(k), waits)
            for slot in range(self.NDS):
                if self.dma_cnt[slot] > 0:
                    self._need(e, ("d", slot, self.dma_cnt[slot]), waits)
            self.ops[e].append(dict(kind="w", waits=waits))
        semval = {}
        for e in self.CE:
            c = self.sigcnt[e]
            for idx in range(len(self.ops[e])):
                if idx in self.signaled[e]:
                    semval[(e, idx)] = (c // EPOCH, c % EPOCH + 1)
                    c += 1
            self.sigcnt[e] = c
            assert c <= EPOCH * self.NEP, (e, c)
        csem, dsem, ccsem = self.csem, self.dsem, self.ccsem
        need_q = any(o["kind"] == "d" and (callable(o["out"]) or callable(o["in_"])) for o in self.ops["sp"])
        with nc.Block() as block:
            def run(e):
                def body(h):
                    qv = None
                    if e == "sp" and need_q:
                        qv = _QCtx(h)
                    for i, o in enumerate(self.ops[e]):
                        for ev in o["waits"]:
                            if ev[0] == "c":
                                ep, v = semval[(ev[1], ev[2])]
                                h.wait_ge(csem[ev[1]][ep], v)
                            elif ev[0] == "d":
                                h.wait_ge(dsem[ev[1]], 16 * ev[2])
                            else:
                                h.wait_ge(ccsem[ev[1] % self.NCC], ev[1] // self.NCC + 1)
                        if o["kind"] == "c":
                            ins = o["fn"](h)
                            if (e, i) in semval:
                                ep, v = semval[(e, i)]
                                ins.then_inc(csem[e][ep], 1)
                        elif o["kind"] == "d":
                            o_ = o["out"](qv) if callable(o["out"]) else o["out"]
                            i_ = o["in_"](qv) if callable(o["in_"]) else o["in_"]
                            try:
                                h.dma_start(out=o_, in_=i_).then_inc(dsem[o["slot"]], 16)
                            except Exception:
                                print("DMA FAIL", o_, i_, flush=True)
                                raise
                        elif o["kind"] == "cc":
                            h.collective_compute(o["cck"], ALU.bypass, replica_groups=o["groups"], ins=[o["in_"].opt()],
                                                 outs=[o["out"].opt()]).then_inc(ccsem[o["n"] % self.NCC], 1)
                return body
            block.sync(run("sp"))
            block.tensor(run("pe"))
            block.scalar(run("act"))
            block.vector(run("dve"))
            block.gpsimd(run("pool"))
        self.nflush += 1
        self._reset()


NW1 = 2820
NFM = 1024


def p1_record(P, nc, st, bank, S, xtile, w1_d, vec_d, row_d, pw_d, cst_d, ys_d, after_st=None, stage=9, dbg=()):
    NT = S // 128
    NST = S // 512
    GT = (NT + 2) // 3
    if True:
        tag = f"sb{P.nflush}_"
        sb = lambda n, s, d: st.enter_context(nc.sbuf_tensor(tag + n, s, d))
        bkey = [f"bk{b}" for b in range(8)]
        b5b = bank[5][:].bitcast(BF16)

        cst_f = sb("cst_f", [128, 8, 128], F32)
        cst_b = sb("cst_b", [128, 8, 128], BF16)
        vec = sb("vec", [128, 32], F32)
        rows = sb("rows", [128, 1536], F32)
        pw_b = sb("pw_b", [128, 2, 256], BF16)
        W1 = sb("W1", [128, 8, NW1], BF16)
        KT = sb("KT", [128, 2, S], BF16)
        VA = sb("VA", [128, NT, 2, 129], BF16)
        KF = sb("KF", [128, 2, GT * 128], BF16)
        hT = sb("hT", [128, 8, 512], BF16)
        QT = sb("QT", [128, 2, 512], BF16)
        QF = sb("QF", [128, 2, 512], BF16)
        cin = sb("cin", [128, 515], F32)
        halo = sb("halo", [128, 4, 3], F32)
        ctmp = sb("ctmp", [128, 512], F32)
        csig = sb("csig", [128, 512], F32)
        qkT = sb("qkT", [128, 4, 512], BF16)
        xt = [sb(f"xt{i}", [128, 1024], F32) for i in range(2)]
        ss = sb("ss", [128, 2], F32)
        xn = sb("xn", [128, 1024], BF16)
        junk_b = xn
        gateB = [sb(f"gateB{i}", [128, 256], F32) for i in range(4)]
        VAm4 = [sb(f"VAm{i}", [128, 257], BF16) for i in range(4)]
        gateA4 = [sb(f"gateA{i}", [128, 256], F32) for i in range(4)]
        gateC4 = [sb(f"gateC{i}", [128, 256], F32) for i in range(4)]
        ut = [sb(f"ut{i}", [128, 256], BF16) for i in range(5)]
        sp4 = [sb(f"sp{i}", [128, 4], F32) for i in range(4)]
        cs4 = [sb(f"cs{i}", [128, 12], F32) for i in range(4)]
        ls4 = [sb(f"ls{i}", [128, 4], F32) for i in range(4)]
        sgs = [sb("sg0", [128, 256], F32), sb("sg1", [128, 256], F32)]
        sg = sgs[0]
        sm = sb("sm", [128, 32], F32)
        Fcar = sb("Fcar", [128, 2], F32)
        Fk = sb("Fk", [128, NT, 2], F32)
        nbias = sb("nbias", [128, 2, NT], F32)
        Fref = sb("Fref", [128, 2], F32)
        fq = sb("fq", [128, 4, 2], F32)
        Osb = sb("Osb", [128, 4, 129], F32)
        f3b = sb("f3b", [128, 2, 3], BF16)
        f3n = sb("f3n", [128, 2, 3], BF16)
        fw = sb("fw", [128, 8], F32)
        qfk = sb("qfk", [128, 2, 2, 128], BF16)
        CT = sb("CT", [128, 2, 257], F32)
        CTb = sb("CTb", [128, 2, 257], BF16)
        Bm = sb("Bm", [128, 128], F32)
        ET = sb("ET", [128, 128], F32)
        PTm = sb("PTm", [128, 128], BF16)
        Ktok = sb("Ktok", [128, 256], BF16)
        Vw = sb("Vw", [128, 257], BF16)
        Gs = sb("Gs", [128, 257], F32)
        dTb = sb("dTb", [128, 2, 128], BF16)
        PT = [sb(f"PT{i}", [128, 512], BF16) for i in range(2)]
        yst = [sb(f"yst{i}", [128, 768], BF16) for i in range(4)]

        IDb = cst_b[:, 0, :]
        IDf = cst_f[:, 0, :]
        UIN = cst_f[:, 1, :]
        UREV = cst_f[:, 2, :]
        ONESf = cst_f[:, 3, :]
        MASKb = cst_b[:, 4, :]
        MCUR = cst_b[:, 5, :]
        MFIRST = cst_b[:, 6, :]
        MPREV = cst_b[:, 7, :]

        P.dma(cst_f[:], cst_d.rearrange("p (c n) -> p c n", n=128), writes=["cst_f"])
        P.dma(vec[:], vec_d, writes=["vec"])
        P.dma(rows[:], row_d, writes=["rows"])
        P.cp("dve", cst_b[:], cst_f[:], ["cst_f"], ["cst_b"])
        P.dma(pw_b[:], pw_d.rearrange("(k p) n -> p k n", p=128), writes=["pw_b"], q="pool")
        P.op("pool", lambda e: e.memset(VA[:], 1.0), (), [("VA", j) for j in range(NST)])
        for i4 in range(4):
            P.op("pool", lambda e, i4=i4: e.memset(VAm4[i4][:], 1.0), (), [("VAm", i4)])
        P.op("pool", lambda e: e.memset(halo[:], 0.0), (), [("halo", c) for c in range(4)])
        P.op("pool", lambda e: e.memset(Fcar[:], 0.0), (), ["Fcar"])
        P.op("pool", lambda e: e.memset(CT[:], 0.0), (), ["CT"])
        P.op("pool", lambda e: e.memset(CTb[:], 0.0), (), ["CTb"])
        P.op("pool", lambda e: e.memset(qfk[:], 0.0), (), ["qfk"])
        P.op("pool", lambda e: e.memset(KF[:], 0.0), (), [("KF", j) for j in range(NST)])
        for h in range(2):
            qv = qfk[:, h, 0, :].rearrange("p (a c) -> p a c", c=32)
            kv = qfk[:, h, 1, :].rearrange("p (a c) -> p a c", c=32)
            P.op("pool", lambda e, qv=qv: e.memset(qv[:, :, 3:6], 1.0), (), ["qfk"])
            P.op("pool", lambda e, kv=kv: e.memset(kv[:, :, 0:3], 1.0), (), ["qfk"])

        w1v = w1_d.rearrange("(k p) n -> p k n", p=128)
        for k in range(8):
            P.dma(W1[:, k, :], w1v[:, k, :], writes=[("W1", k)], q="pool")

        def w1keys(c0, c1, k):
            return [("W1", k)]

        prot = 0
        for j in range(NST):
            for a in range(4 if 'noA' not in dbg else 0):
                i = 4 * j + a
                xb = xt[i % 2]
                xk = ("xt", i % 2)
                P.dma(xb[:], xtile(i), writes=[xk])
                P.act(junk_b[:], xb[:], AF.Square, [xk], ["xn", "ss"], accum_out=ss[:, 0:1])
                P.act(ss[:, 1:2], ss[:, 0:1], AF.Ln, ["ss"], ["ss1"], scale=1.0 / 1024, bias=EPS)
                P.act(ss[:, 1:2], ss[:, 1:2], AF.Exp, ["ss1"], ["ss1"], scale=-0.5)
                P.stt(xn[:], xb[:], ss[:, 1:2], rows[:, 512:1536], ALU.mult, ALU.mult, [xk, "ss1", "rows"], ["xn"])
                for k in range(8):
                    P.tr(b5b[:, k * 128:(k + 1) * 128], xn[:, k * 128:(k + 1) * 128], IDb, ["xn", "cst_b"], [bkey[5]])
                P.cp("act", hT[:, :, a * 128:(a + 1) * 128], b5b[:, 0:1024].rearrange("p (k n) -> p k n", n=128),
                     [bkey[5]], ["hT"])

            PROT = [4, 0, 1, 2, 3]
            for c in range(8 if stage >= 1 else 0):
                pb = PROT[prot % 5]; prot += 1
                bank4, bkey4 = bank[pb], bkey[pb]
                for k in range(8):
                    P.mm(bank4[:, 0:512], W1[:, k, c * 128:(c + 1) * 128], hT[:, k, :], k == 0, k == 7,
                         ["hT"] + w1keys(c * 128, (c + 1) * 128, k), [bkey4])
                if c < 2:
                    P.act(QT[:, c, :], bank4[:, 0:512], AF.Copy, [bkey4], ["QT"], scale=128 ** -0.5)
                elif c < 4:
                    P.cp("dve", KT[:, c - 2, j * 512:(j + 1) * 512], bank4[:, 0:512], [bkey4], [("KT", j)])
                else:
                    cc = c - 4
                    P.cp("act", cin[:, 3:515], bank4[:, 0:512], [bkey4], ["cin"])
                    P.cp("dve", cin[:, 0:3], halo[:, cc, :], [("halo", cc)], ["cin"])
                    P.ts(ctmp[:], cin[:, 0:512], vec[:, 8 + cc * 4:9 + cc * 4], None, ALU.mult, None,
                         ["cin", "vec"], ["ctmp"])
                    for t in range(1, 4):
                        P.stt(ctmp[:], cin[:, t:t + 512], vec[:, 8 + cc * 4 + t:9 + cc * 4 + t], ctmp[:],
                              ALU.mult, ALU.add, ["cin", "vec", "ctmp"], ["ctmp"])
                    P.cp("dve", halo[:, cc, :], cin[:, 512:515], ["cin"], [("halo", cc)])
                    P.act(csig[:], ctmp[:], AF.Sigmoid, ["ctmp"], ["csig"])
                    P.stt(qkT[:, cc, :], ctmp[:], (256 ** -0.5) if cc < 2 else 1.0, csig[:], ALU.mult, ALU.mult,
                          ["ctmp", "csig"], ["qkT"])

            for a in range(4 if stage >= 2 else 0):
                i = 4 * j + a
                ts_ = slice(a * 128, (a + 1) * 128)
                VAm, gateA, gateC, sp_, cs, lsv = VAm4[a], gateA4[a], gateC4[a], sp4[a], cs4[a], ls4[a]
                kVAm, kgA, kgC, ksp, kcs, kls = ("VAm", a), ("gateA", a), ("gateC", a), ("sp", a), ("cs", a), ("ls", a)
                groups = [(0, 512), (512, 1024), (1024, 1536), (1536, 1796)]
                for gi, (c0, c1) in enumerate(groups):
                    n = c1 - c0
                    pb = PROT[prot % 5]; prot += 1
                    bank4, bkey4 = bank[pb], bkey[pb]
                    sg = sgs[prot % 2]
                    sgk = ("sg", prot % 2)
                    for k in range(8):
                        P.mm(bank4[:, 0:n], hT[:, k, ts_], W1[:, k, NFM + c0:NFM + c1], k == 0, k == 7,
                             ["hT"] + w1keys(NFM + c0, NFM + c1, k), [bkey4])
                    lo = bank4[:, 0:256]
                    hi = bank4[:, 256:512]
                    if gi == 0:
                        P.cp("dve", VA[:, i, :, 0:128], lo.rearrange("p (h d) -> p h d", d=128), [bkey4], [("VA", j)])
                        P.act(sg[:], hi, AF.Sigmoid, [bkey4], [sgk])
                        P.tt(gateB[a][:], sg[:], hi, ALU.mult, [sgk, bkey4], [("gateB", a)])
                    elif gi == 1:
                        P.cp("dve", VAm[:, 0:256], lo, [bkey4], [kVAm])
                        P.act(sg[:], hi, AF.Sigmoid, [bkey4], [sgk])
                        P.tt(gateA[:], sg[:], rows[:, 0:256], ALU.mult, [sgk, "rows"], [kgA])
                    elif gi == 2:
                        P.act(sg[:], lo, AF.Sigmoid, [bkey4], [sgk])
                        P.tt(sg[:], sg[:], lo, ALU.mult, [sgk, bkey4], [sgk])
                        P.tt(gateA[:], gateA[:], sg[:], ALU.mult, [sgk, kgA], [kgA])
                        P.cp("act", ut[i % 5][:], hi, [bkey4], [("ut", i % 5)])
                    else:
                        P.act(sg[:], lo, AF.Sigmoid, [bkey4], [sgk])
                        P.tt(sg[:], sg[:], lo, ALU.mult, [sgk, bkey4], [sgk])
                        P.tt(gateC[:], sg[:], rows[:, 256:512], ALU.mult, [sgk, "rows"], [kgC])
                        P.tt(sp_[:], vec[:, 24:28], bank4[:, 256:260], ALU.add, ["vec", bkey4], [ksp])

                if stage < 3:
                    continue
                P.act(sm[:, 0:3], sp_[:, 1:4], AF.Abs, [ksp], ["sm0"])
                P.act(sm[:, 3:6], sm[:, 0:3], AF.Exp, ["sm0"], ["sm3"], scale=-1.0)
                P.act(sm[:, 6:9], sm[:, 3:6], AF.Ln, ["sm3"], ["sm6"], bias=1.0)
                P.ts(sm[:, 9:12], sp_[:, 1:4], 0.0, None, ALU.min, None, [ksp], ["sm9"])
                P.tt(lsv[:, 0:3], sm[:, 9:12], sm[:, 6:9], ALU.subtract, ["sm9", "sm6"], [kls])
                ls = lsv[:, 0:3]
                P.mm(bank[7][:, 260:263], UIN, ls, True, True, [kls, "cst_f"], [bkey[7]])
                P.mm(bank[7][:, 263:264], UREV, lsv[:, 0:1], True, True, [kls, "cst_f"], [bkey[7]])
                P.mm(bank[7][:, 264:267], ONESf, ls, True, True, [kls, "cst_f"], [bkey[7]])
                P.cp("dve", cs[:, 0:7], bank[7][:, 260:267], [bkey[7]], [kcs])
                if a == 0:
                    P.cp("dve", Fref[:], Fcar[:], ["Fcar"], ["Fref"])
                P.tt(fw[:, 0:2], cs[:, 1:3], Fcar[:], ALU.add, [kcs, "Fcar"], ["fw0"])
                P.cp("dve", Fk[:, i, :], fw[:, 0:2], ["fw0"], [("Fk", j)])
                P.tt(sm[:, 29:31], fw[:, 0:2], Fref[:], ALU.subtract, ["fw0", "Fref"], ["sm29"])
                P.act(fq[:, a, :], sm[:, 29:31], AF.Exp, ["sm29"], [("fq", a)])
                P.tt(Fcar[:], Fcar[:], cs[:, 5:7], ALU.add, [kcs, "Fcar"], ["Fcar"])
                P.cp("dve", f3b[:, :, 0], fw[:, 0:2], ["fw0"], ["f3b0"])
                P.cp("dve", fw[:, 2:4], f3b[:, :, 0], ["f3b0"], ["fw2"])
                P.tt(fw[:, 4:6], fw[:, 0:2], fw[:, 2:4], ALU.subtract, ["fw0", "fw2"], ["fw4"])
                P.cp("dve", f3b[:, :, 1], fw[:, 4:6], ["fw4"], ["f3b1"])
                P.cp("dve", fw[:, 2:4], f3b[:, :, 1], ["f3b1"], ["fw2"])
                P.tt(fw[:, 6:8], fw[:, 4:6], fw[:, 2:4], ALU.subtract, ["fw4", "fw2"], ["fw6"])
                P.cp("dve", f3b[:, :, 2], fw[:, 6:8], ["fw6"], ["f3b2"])
                P.ts(f3n[:], f3b[:], -1.0, None, ALU.mult, None, ["f3b0", "f3b1", "f3b2"], ["f3n"])
                ak = i // GT
                for h in range(2):
                    qv = qfk[:, h, 0, :].rearrange("p (a c) -> p a c", c=32)
                    for a4 in range(4):
                        P.cp("dve", qv[:, a4, 0:3], f3b[:, h, :], ["f3b0", "f3b1", "f3b2"], ["qfk"])
                    P.cp("dve", qfk[:, h, 1, 32 * ak + 3:32 * ak + 6], f3n[:, h, :], ["f3n"], ["qfk"])
                for h in range(2):
                    for w in range(2):
                        P.tr(b5b[:, (2 * h + w) * 128:(2 * h + w + 1) * 128], qfk[:, h, w, :], IDb,
                             ["qfk", "cst_b"], [bkey[5]])
                for h in range(2):
                    P.cp("act", QF[:, h, ts_], b5b[:, (2 * h) * 128:(2 * h + 1) * 128], [bkey[5]], ["QF"])
                    kcol = (i % GT) * 128
                    P.cp("dve", KF[32 * ak:32 * ak + 6, h, kcol:kcol + 128],
                         b5b[32 * ak:32 * ak + 6, (2 * h + 1) * 128:(2 * h + 2) * 128], [bkey[5]], [("KF", j)])

            P.capture_begin()
            for a in range(4 if stage >= 4 else 0):
                i = 4 * j + a
                ts_ = slice(a * 128, (a + 1) * 128)
                VAm, gateA, gateC, sp_, cs, lsv = VAm4[a], gateA4[a], gateC4[a], sp4[a], cs4[a], ls4[a]
                kVAm, kgA, kgC, ksp, kcs, kls = ("VAm", a), ("gateA", a), ("gateC", a), ("sp", a), ("cs", a), ("ls", a)
                P.mm(bank[6][:, 0:128], qkT[:, 2, ts_], qkT[:, 0, ts_], True, False, ["qkT"], [bkey[6]])
                P.mm(bank[6][:, 0:128], qkT[:, 3, ts_], qkT[:, 1, ts_], False, True, ["qkT"], [bkey[6]])
                P.ts(Bm[:], UIN, lsv[:, 0:1], None, ALU.mult, None, ["cst_f", kls], ["Bm"])
                P.mm(bank[6][:, 128:256], UREV, Bm[:], True, False, ["Bm", "cst_f"], [bkey[6]])
                P.mm(bank[6][:, 128:256], IDb, MASKb, False, True, ["cst_b"], [bkey[6]])
                P.act(ET[:], bank[6][:, 128:256], AF.Exp, [bkey[6], ksp], ["ET"], bias=sp_[:, 0:1])
                P.tt(PTm[:], ET[:], bank[6][:, 0:128], ALU.mult, ["ET", bkey[6]], ["PTm"])
                P.tr(b5b[:, 0:128], qkT[:, 2, ts_], IDb, ["qkT", "cst_b"], [bkey[5]])
                P.tr(b5b[:, 128:256], qkT[:, 3, ts_], IDb, ["qkT", "cst_b"], [bkey[5]])
                P.cp("act", Ktok[:], b5b[:, 0:256], [bkey[5]], ["Ktok"])
                P.act(sm[:, 16:17], cs[:, 0:1], AF.Exp, [kcs], ["sm16"])
                P.act(sm[:, 17:18], cs[:, 3:4], AF.Exp, [kcs, ksp], ["sm17"], bias=sp_[:, 0:1])
                P.act(sm[:, 18:19], cs[:, 4:5], AF.Exp, [kcs], ["sm18"])
                P.ts(Vw[:], VAm[:], sm[:, 17:18], None, ALU.mult, None, [kVAm, "sm17"], ["Vw"])
                P.mm(bank[7][:, 0:257], qkT[:, 0, ts_], CTb[:, 0, :], True, False, ["qkT", "CTb"], [bkey[7]])
                P.mm(bank[7][:, 0:257], qkT[:, 1, ts_], CTb[:, 1, :], False, True, ["qkT", "CTb"], [bkey[7]])
                P.act(Gs[:], bank[7][:, 0:257], AF.Copy, [bkey[7], "sm16"], ["Gs"], scale=sm[:, 16:17])
                P.mm(bank[7][:, 0:257], PTm[:], VAm[:], True, True, ["PTm", kVAm], [bkey[7]])
                P.tt(Gs[:], Gs[:], bank[7][:, 0:257], ALU.add, ["Gs", bkey[7]], ["Gs"])
                for c in range(2):
                    P.mm(bank[7][:, 0:257], Ktok[:, c * 128:(c + 1) * 128], Vw[:], True, True, ["Ktok", "Vw"], [bkey[7]])
                    P.stt(CT[:, c, :], CT[:, c, :], sm[:, 18:19], bank[7][:, 0:257], ALU.mult, ALU.add,
                          ["CT", "sm18", bkey[7]], ["CT"])
                    P.cp("act", CTb[:, c, :], CT[:, c, :], ["CT"], ["CTb"])
                P.act(sm[:, 19:20], Gs[:, 256:257], AF.Abs, ["Gs"], ["sm19"])
                P.ts(sm[:, 20:21], sm[:, 19:20], 1.0, None, ALU.max, None, ["sm19"], ["sm20"])
                P.op("dve", lambda e: e.reciprocal(out=sm[:, 21:22], in_=sm[:, 20:21]), ["sm20"], ["sm21"])
                P.act(sgs[0][:], Gs[:, 0:256], AF.Square, ["Gs"], [("sg", 0), "sm22"], accum_out=sm[:, 22:23])
                P.tt(sm[:, 23:24], sm[:, 21:22], sm[:, 21:22], ALU.mult, ["sm21"], ["sm23"])
                P.tt(sm[:, 24:25], sm[:, 23:24], sm[:, 22:23], ALU.mult, ["sm23", "sm22"], ["sm24"])
                P.act(sm[:, 25:26], sm[:, 24:25], AF.Ln, ["sm24"], ["sm25"], scale=1.0 / 256, bias=EPS)
                P.act(sm[:, 26:27], sm[:, 25:26], AF.Exp, ["sm25"], ["sm26"], scale=-0.5)
                P.tt(sm[:, 27:28], sm[:, 26:27], sm[:, 21:22], ALU.mult, ["sm26", "sm21"], ["sm27"])
                yk = ("yst", a)
                P.stt(yst[a][:, 0:256], Gs[:, 0:256], sm[:, 27:28], gateA[:], ALU.mult, ALU.mult,
                      ["Gs", "sm27", kgA], [yk])

                if stage < 5:
                    continue
                ucur = ut[i % 5]
                uprev = ut[(i - 1) % 5]
                for cc in range(2):
                    o_ = bank[6][:, 256 + cc * 128:384 + cc * 128]
                    if i == 0:
                        P.mm(o_, ucur[:, cc * 128:(cc + 1) * 128], MFIRST, True, True, [("ut", i % 5), "cst_b"], [bkey[6]])
                    else:
                        P.mm(o_, ucur[:, cc * 128:(cc + 1) * 128], MCUR, True, False, [("ut", i % 5), "cst_b"], [bkey[6]])
                        P.mm(o_, uprev[:, cc * 128:(cc + 1) * 128], MPREV, False, True,
                             [("ut", (i - 1) % 5), "cst_b"], [bkey[6]])
                P.cp("act", dTb[:], bank[6][:, 256:512].rearrange("p (c n) -> p c n", n=128), [bkey[6]], ["dTb"])
                for cc in range(2):
                    P.mm(bank[7][:, 0:256], dTb[:, cc, :], pw_b[:, cc, :], cc == 0, cc == 1, ["dTb", "pw_b"], [bkey[7]])
                P.tt(yst[a][:, 512:768], gateC[:], bank[7][:, 0:256], ALU.mult, [kgC, bkey[7]], [yk])

            deferred = P.capture_end()
            deferred.reverse()
            its = [(h, kt) for h in range(2 if stage >= 6 else 0) for kt in range(4 * j + 4)]

            if j > 0 and stage >= 6:
                for h in range(2):
                    P.ts(nbias[:, h, 0:4 * j], Fk[:, 0:4 * j, h], -1.0, Fref[:, h:h + 1], ALU.mult, ALU.add,
                         [("Fk", jj) for jj in range(j)] + ["Fref"], [("nbias", h)])

            def scores(idx):
                h, kt = its[idx]
                n0 = max(0, kt - 4 * j)
                q0 = n0 * 128
                ak = kt // GT
                kcol = (kt % GT) * 128
                sb_ = idx % 2
                stt_ = bank[sb_][:, q0:512]
                diag = kt >= 4 * j
                if not diag:
                    P.mm(stt_, KT[:, h, kt * 128:(kt + 1) * 128], QT[:, h, q0:512], True, True,
                         [("KT", kt // 4), "QT"], [bkey[sb_]])
                    P.act(PT[sb_][:, q0:512], stt_, AF.Exp, [bkey[sb_], ("nbias", h)], [("PT", sb_)],
                          bias=nbias[:, h, kt:kt + 1])
                    return
                P.mm(stt_, KT[:, h, kt * 128:(kt + 1) * 128], QT[:, h, q0:512], True, False,
                     [("KT", kt // 4), "QT"], [bkey[sb_]])
                P.mm(stt_, KF[32 * ak:32 * ak + 6, h, kcol:kcol + 128], QF[32 * ak:32 * ak + 6, h, q0:512],
                     False, False, [("KF", kt // 4), "QF"], [bkey[sb_]])
                P.mm(bank[sb_][:, q0:q0 + 128], IDb, MASKb, False, True, ["cst_b"], [bkey[sb_]])
                P.act(PT[sb_][:, q0:512], stt_, AF.Exp, [bkey[sb_]], [("PT", sb_)])

            per = (len(deferred) + max(1, len(its) - 2) - 1) // max(1, len(its) - 2)

            def replay(n):
                for _ in range(n):
                    if deferred:
                        P.op(*deferred.pop())
            if its:
                scores(0)
            for idx, (h, kt) in enumerate(its):
                if idx + 1 < len(its):
                    scores(idx + 1)
                replay(per)
                n0 = max(0, kt - 4 * j)
                sb_ = idx % 2
                for n in range(n0, 4):
                    ob = 2 + n // 2
                    oc = (n % 2) * 256
                    first = (kt == 0) or (kt == 4 * j)
                    P.mm(bank[ob][:, oc:oc + 129], PT[sb_][:, n * 128:(n + 1) * 128], VA[:, kt, h, :],
                         first and (n % 2 == 0), (kt == 4 * j + n) or (kt == 4 * j - 1),
                         [("PT", sb_), ("VA", kt // 4)], [bkey[ob]])
                if j > 0 and kt == 4 * j - 1:
                    for n in range(4):
                        ob = 2 + n // 2
                        oc = (n % 2) * 256
                        P.ts(Osb[:, n, :], bank[ob][:, oc:oc + 129], fq[:, n, h:h + 1], None, ALU.mult, None,
                             [bkey[ob], ("fq", n)], [("Osb", n)])
                if kt == 4 * j + 3:
                    for n in range(4):
                        ob = 2 + n // 2
                        oc = (n % 2) * 256
                        if j > 0:
                            P.tt(Osb[:, n, :], Osb[:, n, :], bank[ob][:, oc:oc + 129], ALU.add, [("Osb", n), bkey[ob]], [("Osb", n)])
                            P.op("dve", lambda e, n=n: e.reciprocal(out=sm[:, 28:29], in_=Osb[:, n, 128:129]), [("Osb", n)], ["sm28"])
                            P.stt(yst[n][:, 256 + h * 128:384 + h * 128], Osb[:, n, 0:128], sm[:, 28:29],
                                  gateB[n][:, h * 128:(h + 1) * 128], ALU.mult, ALU.mult,
                                  [("Osb", n), "sm28", ("gateB", n)], [("yst", n)])
                        else:
                            P.op("dve", lambda e, ob=ob, oc=oc: e.reciprocal(out=sm[:, 28:29], in_=bank[ob][:, oc + 128:oc + 129]),
                                 [bkey[ob]], ["sm28"])
                            P.stt(yst[n][:, 256 + h * 128:384 + h * 128], bank[ob][:, oc:oc + 128], sm[:, 28:29],
                                  gateB[n][:, h * 128:(h + 1) * 128], ALU.mult, ALU.mult,
                                  [bkey[ob], "sm28", ("gateB", n)], [("yst", n)])
            replay(len(deferred))
            for n in range(4):
                i = 4 * j + n
                P.dma(ys_d[i * 128:(i + 1) * 128, :], yst[n][:], reads=[("yst", n)], writes=[("ys", i)])
            if after_st is not None:
                after_st(j, [("ys", 4 * j + n) for n in range(4)])
    return [("ys", i) for i in range(NT)]


def _consts(g):
    s = np.arange(128)[:, None]
    t = np.arange(128)[None, :]
    ident = (s == t).astype(np.float32)
    uin = (s <= t).astype(np.float32)
    urev = (s > t).astype(np.float32)
    ones = np.ones((128, 128), np.float32)
    maskT = np.where(s > t, -30000.0, 0.0).astype(np.float32)
    W = POOL_WINDOWS[g]
    def mt(first):
        M = np.zeros((128, 256), np.float32)
        for tt in range(128):
            cnt = min(tt + 1, W) if first else W
            for jj in range(cnt):
                M[tt, 128 + tt - jj] += 1.0 / cnt
            M[tt, 128 + tt] -= 1.0
        return M
    Mg = mt(False)
    Mf = mt(True)
    mcur = Mg[:, 128:].T.copy()
    mprev = Mg[:, :128].T.copy()
    mfirst = Mf[:, 128:].T.copy()
    return np.concatenate([ident, uin, urev, ones, maskT, mcur, mfirst, mprev], axis=1).astype(np.float32)


def _p1_cols(g):
    W = 1024
    off = {}
    names = ["aq", "ak", "av", "ao", "az"]
    o = 0
    for nme in names:
        off[nme] = o
        o += W
    off["ai"] = o; o += 4
    off["af"] = o; o += 4
    for nme in ["bq", "bk", "bv", "bz"]:
        off[nme] = o
        o += W
    off["bf"] = o; o += 8
    off["cu"] = o; o += W
    off["cz"] = o; o += W
    off["gates"] = o
    r = lambda nme: np.arange(off[nme] + g * 256, off[nme] + (g + 1) * 256)
    cols = np.concatenate([
        r("bq"), r("bk"), r("aq"), r("ak"),
        r("bv"), r("bz"), r("av"), r("ao"), r("az"), r("cu"), r("cz"),
        np.array([off["ai"] + g, off["af"] + g, off["bf"] + 2 * g, off["bf"] + 2 * g + 1]),
    ])
    return cols, off


def p1_inputs(x_b, l, g, inp):
    cols, off = _p1_cols(g)
    w1 = np.ascontiguousarray(inp["w_in"][l][:, cols])
    vec = np.zeros((128, 32), np.float32)
    vec[:, 0:8] = inp["norm_g"][l].reshape(8, 128).T
    cw = inp["conv_w"][l]
    for cc in range(4):
        base = (0 if cc < 2 else 1024) + g * 256 + (cc % 2) * 128
        vec[:, 8 + cc * 4:12 + cc * 4] = cw[:, base:base + 128].T
    vec[:, 24] = inp["ml_bi"][l][g]
    vec[:, 25] = inp["ml_bf"][l][g]
    vec[:, 26] = inp["fox_bf"][l][2 * g]
    vec[:, 27] = inp["fox_bf"][l][2 * g + 1]
    rows = np.zeros((128, 1536), np.float32)
    rows[:, 0:256] = inp["ml_norm_g"][l][g * 256:(g + 1) * 256][None, :]
    rows[:, 256:512] = inp["pool_scale"][l][g * 256:(g + 1) * 256][None, :]
    rows[:, 512:1536] = inp["norm_g"][l][None, :]
    return dict(x=np.ascontiguousarray(x_b), w1=w1, vecs=vec, rows=rows,
                poolw=np.ascontiguousarray(inp["pool_w"][l][g]), consts=_consts(g))


def p2_record(P, nc, st, bank, T, last, xsrc, ysload, wg_d, wb_d, wo_d, vec_d, fg_d, id_d, odst, pre=None, after_tile=None, post_w=None):
    NT = T // 128
    if True:
        tag = f"sb{P.nflush}_"
        sb = lambda n, s, d: st.enter_context(nc.sbuf_tensor(tag + n, s, d))
        bkey = [f"bk{b}" for b in range(8)]
        Wg = sb("Wg", [128, 8, 3072], BF16)
        Wb = sb("Wb", [128, 24, 1024], BF16)
        Wo = sb("Wo", [128, 8, 1024], BF16)
        grow = sb("grow", [128, 1024], F32)
        fg = sb("fg", [128, 1024], F32)
        idf = sb("idf", [128, 128], F32)
        idb = sb("idb", [128, 128], BF16)
        xt = [sb(f"xt{i}", [128, 1024], F32) for i in range(2)]
        ysts = [sb(f"yst{i}", [128, 3072], BF16) for i in range(2)]
        ysTs = [sb(f"ysT{i}", [128, 24, 128], BF16) for i in range(2)]
        junk_b = sb("junk_b", [128, 1024], BF16)
        ss = sb("ss", [128, 4], F32)
        xn = sb("xn", [128, 1024], BF16)
        hTs = [sb(f"hT{i}", [128, 8, 128], BF16) for i in range(2)]
        gsb = sb("gsb", [128, 3072], F32)
        mrg = sb("mrg", [128, 1024], F32)
        tmp = sb("tmp", [128, 512], F32)
        mb = sb("mb", [128, 1024], BF16)
        mT = sb("mT", [128, 8, 128], BF16)
        xo = [sb(f"xo{i}", [128, 1024], F32) for i in range(2)]

        P.dma(grow[:], vec_d, writes=["grow"])
        P.dma(fg[:], fg_d, writes=["fg"])
        P.dma(idf[:], id_d, writes=["idf"])
        P.cp("dve", idb[:], idf[:], ["idf"], ["idb"])
        wgv = wg_d.rearrange("(k p) n -> p k n", p=128)
        wbv = wb_d.rearrange("(k p) n -> p k n", p=128)
        wov = wo_d.rearrange("(k p) n -> p k n", p=128)
        for k in range(8):
            P.dma(Wg[:, k, :], wgv[:, k, :], writes=[("Wg", k)], q="pool")
        for k in range(24):
            P.dma(Wb[:, k, :], wbv[:, k, :], writes=[("Wb", k)], q="pool")
        for k in range(8):
            P.dma(Wo[:, k, :], wov[:, k, :], writes=[("Wo", k)], q="pool")
        if post_w is not None:
            post_w()
        wkeys = lambda key, k, c0, c1: [(key, k)]

        mmb = 0
        trb = 0
        def prep(i):
            nonlocal trb
            hT, ysT, yst = hTs[i % 2], ysTs[i % 2], ysts[i % 2]
            khT, kyst = ("hT", i % 2), ("yst", i % 2)
            xb = xt[i % 2]
            xk = ("xt", i % 2)
            pr = list(pre(i)) if pre is not None else []
            P.dma(xb[:], xsrc(i), reads=pr, writes=[xk])
            for (o_, i_) in ysload(i, yst):
                P.dma(o_, i_, reads=pr, writes=[kyst])
            P.act(junk_b[:], xb[:], AF.Square, [xk], ["junk_b", "ss"], accum_out=ss[:, 0:1])
            P.act(ss[:, 1:2], ss[:, 0:1], AF.Ln, ["ss"], ["ss1"], scale=1.0 / 1024, bias=EPS)
            P.act(ss[:, 1:2], ss[:, 1:2], AF.Exp, ["ss1"], ["ss1"], scale=-0.5)
            P.stt(xn[:], xb[:], ss[:, 1:2], grow[:], ALU.mult, ALU.mult, [xk, "ss1", "grow"], ["xn"])
            tb = 4 + trb % 2; trb += 1
            tbv = bank[tb][:].bitcast(BF16)
            for k in range(8):
                P.tr(tbv[:, k * 128:(k + 1) * 128], xn[:, k * 128:(k + 1) * 128], idb[:], ["xn", "idb"], [bkey[tb]])
            P.cp("act", hT[:], tbv[:, 0:1024].rearrange("p (k n) -> p k n", n=128), [bkey[tb]], [khT])
            for n in range(3):
                tb = 4 + trb % 2; trb += 1
                tbv = bank[tb][:].bitcast(BF16)
                for k in range(8):
                    c = n * 1024 + k * 128
                    P.tr(tbv[:, k * 128:(k + 1) * 128], yst[:, c:c + 128], idb[:], [kyst, "idb"], [bkey[tb]])
                P.cp("dve" if n % 2 == 0 else "act", ysT[:, n * 8:(n + 1) * 8, :],
                     tbv[:, 0:1024].rearrange("p (k n) -> p k n", n=128), [bkey[tb]], [("ysT", i % 2, n)])
        def gates(i):
            nonlocal mmb
            hT = hTs[i % 2]
            khT = ("hT", i % 2)
            for cg in range(6):
                b = mmb % 4; mmb += 1
                for k in range(8):
                    P.mm(bank[b][:, 0:512], hT[:, k, :], Wg[:, k, cg * 512:(cg + 1) * 512], k == 0, k == 7,
                         [khT] + wkeys("Wg", k, cg * 512, (cg + 1) * 512), [bkey[b]])
                P.act(gsb[:, cg * 512:(cg + 1) * 512], bank[b][:, 0:512], AF.Sigmoid, [bkey[b]], [("gsb", cg)])
        def rest(i):
            nonlocal mmb, trb
            ysT = ysTs[i % 2]
            xb = xt[i % 2]
            xk = ("xt", i % 2)
            for n in range(3):
                for hf in range(2):
                    b = mmb % 4; mmb += 1
                    for k in range(8):
                        P.mm(bank[b][:, 0:512], ysT[:, n * 8 + k, :], Wb[:, n * 8 + k, hf * 512:(hf + 1) * 512], k == 0, k == 7,
                             [("ysT", i % 2, n)] + wkeys("Wb", n * 8 + k, hf * 512, (hf + 1) * 512), [bkey[b]])
                    gv = gsb[:, n * 1024 + hf * 512:n * 1024 + (hf + 1) * 512]
                    gk = ("gsb", n * 2 + hf)
                    mk = ("mrg", hf)
                    mv = mrg[:, hf * 512:(hf + 1) * 512]
                    if n == 0:
                        P.tt(mv, gv, bank[b][:, 0:512], ALU.mult, [gk, bkey[b]], [mk])
                    else:
                        P.tt(tmp[:], gv, bank[b][:, 0:512], ALU.mult, [gk, bkey[b]], ["tmp"])
                        if n == 1:
                            P.tt(mv, mv, tmp[:], ALU.add, [mk, "tmp"], [mk], eng="pool")
                        else:
                            P.tt(mb[:, hf * 512:(hf + 1) * 512], mv, tmp[:], ALU.add, [mk, "tmp"], [("mb", hf)], eng="pool")
            tb = 4 + trb % 2; trb += 1
            tbv = bank[tb][:].bitcast(BF16)
            for k in range(8):
                P.tr(tbv[:, k * 128:(k + 1) * 128], mb[:, k * 128:(k + 1) * 128], idb[:], [("mb", k // 4), "idb"], [bkey[tb]])
            P.cp("act", mT[:], tbv[:, 0:1024].rearrange("p (k n) -> p k n", n=128), [bkey[tb]], ["mT"])
            ob = xo[i % 2]
            ok = ("xo", i % 2)
            for hf in range(2):
                b = mmb % 4; mmb += 1
                for k in range(8):
                    P.mm(bank[b][:, 0:512], mT[:, k, :], Wo[:, k, hf * 512:(hf + 1) * 512], k == 0, k == 7,
                         ["mT"] + wkeys("Wo", k, hf * 512, (hf + 1) * 512), [bkey[b]])
                P.tt(ob[:, hf * 512:(hf + 1) * 512], xb[:, hf * 512:(hf + 1) * 512], bank[b][:, 0:512], ALU.add,
                     [xk, bkey[b]], [ok])
            if last:
                P.act(junk_b[:], ob[:], AF.Square, [ok], ["junk_b", "ss2"], accum_out=ss[:, 2:3])
                P.act(ss[:, 3:4], ss[:, 2:3], AF.Ln, ["ss2"], ["ss3"], scale=1.0 / 1024, bias=EPS)
                P.act(ss[:, 3:4], ss[:, 3:4], AF.Exp, ["ss3"], ["ss3"], scale=-0.5)
                P.stt(ob[:], ob[:], ss[:, 3:4], fg[:], ALU.mult, ALU.mult, [ok, "ss3", "fg"], [ok])
            P.dma(odst(i), ob[:], reads=[ok], writes=[("o", i)])
            if after_tile is not None:
                after_tile(i)
        prep(0)
        for i in range(NT):
            gates(i)
            if i + 1 < NT:
                prep(i + 1)
            rest(i)
    return [("o", i) for i in range(NT)]


def p2_inputs(x_sh, ys_sh, l, inp):
    o = 5 * 1024 + 8 + 4 * 1024 + 8 + 2 * 1024
    return dict(x=np.ascontiguousarray(x_sh), ysin=np.ascontiguousarray(ys_sh),
                wg=np.ascontiguousarray(inp["w_in"][l][:, o:o + 3072]),
                wb=np.ascontiguousarray(inp["w_branch"][l].reshape(3072, 1024)),
                wo=np.ascontiguousarray(inp["w_out"][l]),
                vecs=np.ascontiguousarray(inp["norm_g"][l].reshape(8, 128).T),
                fgrows=np.ascontiguousarray(np.broadcast_to(inp["final_g"][None, :], (128, 1024))),
                ident=np.eye(128, dtype=np.float32))


GROUPS = [[0, 1, 2, 3], [4, 5, 6, 7]]


class _QCtx:
    def __init__(self, h):
        self.q = h.partition_id() % 4
        self.cache = {}

    def mul(self, m):
        if m == 1:
            return self.q
        if m not in self.cache:
            self.cache[m] = self.q * m
        return self.cache[m]


def build_fused(S=SEQ):
    TS = S // 4
    nc = bass.Bass("TRN2", target_bir_lowering=False)
    dram = lambda n, sh, dt, kind: nc.dram_tensor(n, sh, dt, kind=kind).ap()
    x_d = dram("x", [S, 1024], F32, "ExternalInput")
    cst_d = dram("consts", [128, 1024], F32, "ExternalInput")
    fg_d = dram("fgrows", [128, 1024], F32, "ExternalInput")
    L = []
    for l in range(DEPTH):
        L.append(dict(
            w1=dram(f"w1_{l}", [1024, NW1], F32, "ExternalInput"),
            vec=dram(f"vecs_{l}", [128, 32], F32, "ExternalInput"),
            rows=dram(f"rows_{l}", [128, 1536], F32, "ExternalInput"),
            pw=dram(f"poolw_{l}", [256, 256], F32, "ExternalInput"),
            wg=dram(f"wg_{l}", [1024, 3072], F32, "ExternalInput"),
            wb=dram(f"wb_{l}", [3072, 1024], F32, "ExternalInput"),
            wo=dram(f"wo_{l}", [1024, 1024], F32, "ExternalInput"),
        ))
    out_d = dram("out", [TS, 1024], F32, "ExternalOutput")
    ysrc = nc.dram_tensor("ysrc", [S, 768], BF16).ap()
    yall = nc.dram_tensor("yall", [4 * S, 768], BF16).ap()
    x1src = nc.dram_tensor("x1src", [TS, 1024], F32).ap()
    x1all = nc.dram_tensor("x1all", [S, 1024], F32).ap()
    ysh = nc.dram_tensor("ysh", [4 * TS, 768], BF16).ap()
    xsh = nc.dram_tensor("xsh", [TS, 1024], F32).ap()

    with contextlib.ExitStack() as top:
        bank = [top.enter_context(nc.psum_tensor(f"bk{b}", [128, 512], F32)) for b in range(8)]
        fin = top.enter_context(nc.sbuf_tensor("sb_fin", [1, 8], F32))
        P = Prog(nc, top)
        P.op("pool", lambda e: e.memset(fin[:], 0.0), (), ["fin0"])
        NB = TS // 512
        for l in range(DEPTH):
            d = L[l]
            last = (l == DEPTH - 1)
            with contextlib.ExitStack() as st:
                if l == 0:
                    xtile = lambda i: x_d[i * 128:(i + 1) * 128, :]
                else:
                    def xtile(i):
                        tok = i * 128
                        r, k, t = tok // TS, (tok % TS) // 256, tok % 256
                        row = k * 1024 + r * 256 + t
                        return x1all[row:row + 128, :]

                def after_st(j, keys):
                    P.cc("AllGather", GROUPS, yall[j * 2048:(j + 1) * 2048, :], ysrc[j * 512:(j + 1) * 512, :],
                         reads=keys, writes=[("yall", j)])
                p1_record(P, nc, st, bank, S, xtile, d["w1"], d["vec"], d["rows"], d["pw"], cst_d, ysrc, after_st=after_st)
                P.flush(fin, bank[7])
            with contextlib.ExitStack() as st:
                yv = yall.rearrange("(blk p r) c -> blk p (r c)", p=128, r=16)
                yshv = ysh.rearrange("(k p r) c -> k p (r c)", p=128, r=16)
                def shard_copy(jj):
                    P.dma(yshv[jj:jj + 1], (lambda q, jj=jj: yv[jj:][bass.ds(q.mul(NB), 1)]), writes=[("ysh", jj)])
                    if l == 0:
                        P.dma(xshv[jj:jj + 1], (lambda q, jj=jj: xv[jj:][bass.ds(q.mul(NB), 1)]), writes=[("xsh", jj)])
                if l == 0:
                    xv = x_d.rearrange("(blk p r) c -> blk p (r c)", p=128, r=4)
                    xshv = xsh.rearrange("(k p r) c -> k p (r c)", p=128, r=4)
                    xsrc = lambda i: xsh[i * 128:(i + 1) * 128, :]
                else:
                    xsrc = lambda i: x1src[i * 128:(i + 1) * 128, :]
                shard_copy(0)

                def post_w():
                    for jj in range(1, NB):
                        shard_copy(jj)

                def ysload(i, yst):
                    prs = []
                    jj, t0 = i // 4, (i % 4) * 128
                    for r in range(4):
                        o_ = yst[:].rearrange("p (n r c) -> p n r c", n=3, r=4)[:, :, r, :]
                        row = jj * 2048 + r * 512 + t0
                        i_ = ysh[row:row + 128, :].rearrange("p (n c) -> p n c", n=3)
                        prs.append((o_, i_))
                    return prs
                if last:
                    odst = lambda i: out_d[i * 128:(i + 1) * 128, :]
                    after_tile = None
                else:
                    odst = lambda i: x1src[i * 128:(i + 1) * 128, :]

                    def after_tile(i):
                        if i % 2 == 1:
                            k = i // 2
                            P.cc("AllGather", GROUPS, x1all[k * 1024:(k + 1) * 1024, :], x1src[k * 256:(k + 1) * 256, :],
                                 reads=[("o", i - 1), ("o", i)], writes=[("x1all", k)])
                p2_record(P, nc, st, bank, TS, last, xsrc, ysload, d["wg"], d["wb"], d["wo"],
                          d["rows"][:, 512:1536], fg_d, cst_d[:, 0:128], odst, pre=lambda i: [("ysh", i // 4), ("xsh", i // 4)],
                          after_tile=after_tile, post_w=post_w)
                P.flush(fin, bank[7])
    return nc


_CACHE = {}


def kernel(x, norm_g, w_in, conv_w, ml_bi, ml_bf, ml_norm_g, fox_bf, pool_w, pool_scale, w_branch, w_out, final_g):
    inp = dict(norm_g=np.asarray(norm_g), w_in=np.asarray(w_in), conv_w=np.asarray(conv_w), ml_bi=np.asarray(ml_bi),
               ml_bf=np.asarray(ml_bf), ml_norm_g=np.asarray(ml_norm_g), fox_bf=np.asarray(fox_bf),
               pool_w=np.asarray(pool_w), pool_scale=np.asarray(pool_scale), w_branch=np.asarray(w_branch),
               w_out=np.asarray(w_out), final_g=np.asarray(final_g))
    x = np.asarray(x, dtype=np.float32)
    B, S, _ = x.shape
    TS = S // 4
    if "nc" not in _CACHE:
        _CACHE["nc"] = build_fused(S)
    nc = _CACHE["nc"]
    go = 5 * 1024 + 8 + 4 * 1024 + 8 + 2 * 1024
    fgrows = np.ascontiguousarray(np.broadcast_to(inp["final_g"][None, :], (128, 1024)))
    ins = []
    for c in range(8):
        b, g = c // 4, c % 4
        m = dict(x=np.ascontiguousarray(x[b]), consts=_consts(g), fgrows=fgrows)
        for l in range(DEPTH):
            p1 = p1_inputs(x[b], l, g, inp)
            m[f"w1_{l}"] = p1["w1"]
            m[f"vecs_{l}"] = p1["vecs"]
            m[f"rows_{l}"] = p1["rows"]
            m[f"poolw_{l}"] = p1["poolw"]
            m[f"wg_{l}"] = np.ascontiguousarray(inp["w_in"][l][:, go:go + 3072])
            m[f"wb_{l}"] = np.ascontiguousarray(inp["w_branch"][l].reshape(3072, 1024))
            m[f"wo_{l}"] = np.ascontiguousarray(inp["w_out"][l])
        ins.append(m)
    res = run_bass_kernel_spmd(nc, ins, core_ids=list(range(8)))
    out = np.empty((B, S, 1024), np.float32)
    for c in range(8):
        b, q = c // 4, c % 4
        out[b, q * TS:(q + 1) * TS] = np.asarray(res.results[c]["out"])
    return out
```

```python
import contextlib
import numpy as np
import ml_dtypes
import concourse.bass as bass
import concourse.mybir as mybir
from concourse.bass_utils import run_bass_kernel_spmd

F32 = mybir.dt.float32
BF16 = mybir.dt.bfloat16
AF = mybir.ActivationFunctionType
ALU = mybir.AluOpType

D_MODEL = 1024
SEQ = 8192
BATCH = 2
DEPTH = 2
N_IN = 14352
POOL_WINDOWS = (2, 4, 8, 16)
EPS = 1e-6

EPOCH = 20000
SAME_ENGINE_SYNC = True


class Prog:
    CE = ("pe", "act", "dve", "pool")
    ALLE = ("pe", "act", "dve", "pool", "sp")
    NEP = 4

    def __init__(self, nc, stack):
        self.nc = nc
        self.NDS = 24
        self.NCC = 8
        self.csem = {e: [stack.enter_context(nc.semaphore(f"s_{e}{i}")) for i in range(self.NEP)] for e in self.CE}
        self.dsem = [stack.enter_context(nc.semaphore(f"s_d{i}")) for i in range(self.NDS)]
        self.ccsem = [stack.enter_context(nc.semaphore(f"s_cc{i}")) for i in range(self.NCC)]
        self.sigcnt = {e: 0 for e in self.CE}
        self.dma_cnt = [0] * self.NDS
        self.dma_rr = 0
        self.dma_rr_pool = 0
        self.ncc = 0
        self.nflush = 0
        self._reset()

    def _reset(self):
        self.ops = {e: [] for e in self.ALLE}
        self.last_write = {}
        self.readers = {}
        self.seen = {e: {} for e in self.ALLE}
        self.signaled = {e: set() for e in self.CE}
        self.cc_pending = []

    def _need(self, eng, ev, waits):
        if ev is None:
            return
        if ev[0] == "c":
            _, e2, idx = ev
            if e2 == eng and (eng == "pe" or not SAME_ENGINE_SYNC):
                return
            k = ("c", e2)
            if self.seen[eng].get(k, -1) >= idx:
                return
            self.seen[eng][k] = idx
            waits.append(ev)
            self.signaled[e2].add(idx)
        elif ev[0] == "d":
            _, slot, cnt = ev
            k = ("d", slot)
            if self.seen[eng].get(k, 0) >= cnt:
                return
            self.seen[eng][k] = cnt
            waits.append(ev)
        else:
            k = ("x", ev[1])
            if k in self.seen[eng]:
                return
            self.seen[eng][k] = 1
            waits.append(ev)

    def _deps(self, eng, reads, writes):
        waits = []
        for r in reads:
            self._need(eng, self.last_write.get(r), waits)
        for w in writes:
            self._need(eng, self.last_write.get(w), waits)
            for ev in self.readers.get(w, ()):
                self._need(eng, ev, waits)
        return waits

    def _commit(self, ev, reads, writes):
        for r in reads:
            self.readers.setdefault(r, []).append(ev)
        for w in writes:
            self.last_write[w] = ev
            self.readers[w] = []

    def capture_begin(self):
        self._cap = []

    def capture_end(self):
        c, self._cap = self._cap, None
        return c

    def op(self, eng, fn, reads=(), writes=()):
        if getattr(self, "_cap", None) is not None:
            self._cap.append((eng, fn, reads, writes))
            return
        bk = [r for r in reads if isinstance(r, str) and r.startswith("bk")]
        if bk:
            writes = list(writes) + [b for b in bk if b not in writes]
            reads = [r for r in reads if r not in bk]
        waits = self._deps(eng, reads, writes)
        idx = len(self.ops[eng])
        self.ops[eng].append(dict(kind="c", fn=fn, waits=waits))
        self._commit(("c", eng, idx), reads, writes)

    def dma(self, out, in_, reads=(), writes=(), q="sp"):
        if q == "pool":
            slot = 16 + self.dma_rr_pool
            self.dma_rr_pool = (self.dma_rr_pool + 1) % 8
        else:
            slot = self.dma_rr
            self.dma_rr = (self.dma_rr + 1) % 16
        waits = self._deps(q, reads, writes)
        if self.dma_cnt[slot] > 0:
            self._need(q, ("d", slot, self.dma_cnt[slot]), waits)
        self.dma_cnt[slot] += 1
        ev = ("d", slot, self.dma_cnt[slot])
        self.ops[q].append(dict(kind="d", out=out, in_=in_, waits=waits, slot=slot))
        self._commit(ev, reads, writes)

    def cc(self, kind, groups, out, in_, reads=(), writes=()):
        waits = self._deps("pool", reads, writes)
        n = self.ncc
        self.ncc += 1
        ev = ("x", n)
        self.ops["pool"].append(dict(kind="cc", cck=kind, groups=groups, out=out, in_=in_, waits=waits, n=n))
        self.cc_pending.append(ev)
        self._commit(ev, reads, writes)

    def mm(self, out, lhsT, rhs, start, stop, reads, writes):
        self.op("pe", lambda e: e.matmul(out, lhsT=lhsT, rhs=rhs, start=start, stop=stop, skip_group_check=True),
                reads, writes)

    def tr(self, out, in_, ident, reads, writes):
        self.op("pe", lambda e: e.transpose(out=out, in_=in_, identity=ident), reads, writes)

    def act(self, out, in_, func, reads, writes, **kw):
        self.op("act", lambda e: e.activation(out=out, in_=in_, func=func, **kw), reads, writes)

    def ts(self, out, in0, s1, s2, op0, op1, reads, writes, eng="dve"):
        if op1 is None:
            self.op(eng, lambda e: e.tensor_scalar(out=out, in0=in0, scalar1=s1, scalar2=None, op0=op0), reads, writes)
        else:
            self.op(eng, lambda e: e.tensor_scalar(out=out, in0=in0, scalar1=s1, scalar2=s2, op0=op0, op1=op1),
                    reads, writes)

    def tt(self, out, in0, in1, op, reads, writes, eng="dve"):
        self.op(eng, lambda e: e.tensor_tensor(out=out, in0=in0, in1=in1, op=op), reads, writes)

    def stt(self, out, in0, scalar, in1, op0, op1, reads, writes, eng="dve"):
        self.op(eng, lambda e: e.scalar_tensor_tensor(out=out, in0=in0, scalar=scalar, in1=in1, op0=op0, op1=op1),
                reads, writes)

    def cp(self, eng, out, in_, reads, writes):
        if eng == "act":
            self.op("act", lambda e: e.copy(out=out, in_=in_), reads, writes)
        else:
            self.op(eng, lambda e: e.tensor_copy(out=out, in_=in_), reads, writes)

    def flush(self, fin, bank7):
        nc = self.nc
        fk = [("fin", e) for e in self.CE]
        waits = []
        for ev in self.cc_pending:
            self._need("pool", ev, waits)
        self.ops["pool"].append(dict(kind="w", waits=waits))
        self.op("act", lambda e: e.copy(out=fin[0:1, 0:1], in_=fin[0:1, 1:2]), ["fin0"], [fk[1]])
        self.op("dve", lambda e: e.memset(fin[0:1, 2:3], 0.0), (), [fk[2]])
        self.op("pool", lambda e: e.memset(fin[0:1, 3:4], 0.0), (), [fk[3]])
        self.op("pe", lambda e: e.matmul(bank7[0:1, 511:512], lhsT=fin[0:1, 4:5], rhs=fin[0:1, 5:6], start=True, stop=True,
                                         skip_group_check=True), ["bk7", "fin0"], [fk[0]])
        for e in self.ALLE:
            waits = []
            for k in fk:
                self._need(e, self.last_write.get(k), waits)
            for slot in range(self.NDS):
                if self.dma_cnt[slot] > 0:
                    self._need(e, ("d", slot, self.dma_cnt[slot]), waits)
            self.ops[e].append(dict(kind="w", waits=waits))
        semval = {}
        for e in self.CE:
            c = self.sigcnt[e]
            for idx in range(len(self.ops[e])):
                if idx in self.signaled[e]:
                    semval[(e, idx)] = (c // EPOCH, c % EPOCH + 1)
                    c += 1
            self.sigcnt[e] = c
            assert c <= EPOCH * self.NEP, (e, c)
        csem, dsem, ccsem = self.csem, self.dsem, self.ccsem
        need_q = any(o["kind"] == "d" and (callable(o["out"]) or callable(o["in_"])) for o in self.ops["sp"])
        with nc.Block() as block:
            def run(e):
                def body(h):
                    qv = None
                    if e == "sp" and need_q:
                        qv = _QCtx(h)
                    for i, o in enumerate(self.ops[e]):
                        for ev in o["waits"]:
                            if ev[0] == "c":
                                ep, v = semval[(ev[1], ev[2])]
                                h.wait_ge(csem[ev[1]][ep], v)
                            elif ev[0] == "d":
                                h.wait_ge(dsem[ev[1]], 16 * ev[2])
                            else:
                                h.wait_ge(ccsem[ev[1] % self.NCC], ev[1] // self.NCC + 1)
                        if o["kind"] == "c":
                            ins = o["fn"](h)
                            if (e, i) in semval:
                                ep, v = semval[(e, i)]
                                ins.then_inc(csem[e][ep], 1)
                        elif o["kind"] == "d":
                            o_ = o["out"](qv) if callable(o["out"]) else o["out"]
                            i_ = o["in_"](qv) if callable(o["in_"]) else o["in_"]
                            try:
                                h.dma_start(out=o_, in_=i_).then_inc(dsem[o["slot"]], 16)
                            except Exception:
                                print("DMA FAIL", o_, i_, flush=True)
                                raise
                        elif o["kind"] == "cc":
                            h.collective_compute(o["cck"], ALU.bypass, replica_groups=o["groups"], ins=[o["in_"].opt()],
                                                 outs=[o["out"].opt()]).then_inc(ccsem[o["n"] % self.NCC], 1)
                return body
            block.sync(run("sp"))
            block.tensor(run("pe"))
            block.scalar(run("act"))
            block.vector(run("dve"))
            block.gpsimd(run("pool"))
        self.nflush += 1
        self._reset()


NW1 = 2820
NFM = 1024


def p1_record(P, nc, st, bank, S, xtile, w1_d, vec_d, row_d, pw_d, cst_d, ys_d, after_st=None, stage=9, dbg=()):
    NT = S // 128
    NST = S // 512
    GT = (NT + 2) // 3
    if True:
        tag = f"sb{P.nflush}_"
        sb = lambda n, s, d: st.enter_context(nc.sbuf_tensor(tag + n, s, d))
        bkey = [f"bk{b}" for b in range(8)]
        b5b = bank[5][:].bitcast(BF16)

        cst_f = sb("cst_f", [128, 8, 128], F32)
        cst_b = sb("cst_b", [128, 8, 128], BF16)
        vec = sb("vec", [128, 32], F32)
        rows = sb("rows", [128, 1536], F32)
        pw_b = sb("pw_b", [128, 2, 256], BF16)
        W1 = sb("W1", [128, 8, NW1], BF16)
        KT = sb("KT", [128, 2, S], BF16)
        VA = sb("VA", [128, NT, 2, 129], BF16)
        KF = sb("KF", [128, 2, GT * 128], BF16)
        hT = sb("hT", [128, 8, 512], BF16)
        QT = sb("QT", [128, 2, 512], BF16)
        QF = sb("QF", [128, 2, 512], BF16)
        cin = sb("cin", [128, 515], F32)
        halo = sb("halo", [128, 4, 3], F32)
        ctmp = sb("ctmp", [128, 512], F32)
        csig = sb("csig", [128, 512], F32)
        qkT = sb("qkT", [128, 4, 512], BF16)
        xt = [sb(f"xt{i}", [128, 1024], F32) for i in range(2)]
        ss = sb("ss", [128, 2], F32)
        xn = sb("xn", [128, 1024], BF16)
        junk_b = xn
        gateB = [sb(f"gateB{i}", [128, 256], F32) for i in range(4)]
        VAm4 = [sb(f"VAm{i}", [128, 257], BF16) for i in range(4)]
        gateA4 = [sb(f"gateA{i}", [128, 256], F32) for i in range(4)]
        gateC4 = [sb(f"gateC{i}", [128, 256], F32) for i in range(4)]
        ut = [sb(f"ut{i}", [128, 256], BF16) for i in range(5)]
        sp4 = [sb(f"sp{i}", [128, 4], F32) for i in range(4)]
        cs4 = [sb(f"cs{i}", [128, 12], F32) for i in range(4)]
        ls4 = [sb(f"ls{i}", [128, 4], F32) for i in range(4)]
        sgs = [sb("sg0", [128, 256], F32), sb("sg1", [128, 256], F32)]
        sg = sgs[0]
        sm = sb("sm", [128, 32], F32)
        Fcar = sb("Fcar", [128, 2], F32)
        Fk = sb("Fk", [128, NT, 2], F32)
        nbias = sb("nbias", [128, 2, NT], F32)
        Fref = sb("Fref", [128, 2], F32)
        fq = sb("fq", [128, 4, 2], F32)
        Osb = sb("Osb", [128, 4, 129], F32)
        f3b = sb("f3b", [128, 2, 3], BF16)
        f3n = sb("f3n", [128, 2, 3], BF16)
        fw = sb("fw", [128, 8], F32)
        qfk = sb("qfk", [128, 2, 2, 128], BF16)
        CT = sb("CT", [128, 2, 257], F32)
        CTb = sb("CTb", [128, 2, 257], BF16)
        Bm = sb("Bm", [128, 128], F32)
        ET = sb("ET", [128, 128], F32)
        PTm = sb("PTm", [128, 128], BF16)
        Ktok = sb("Ktok", [128, 256], BF16)
        Vw = sb("Vw", [128, 257], BF16)
        Gs = sb("Gs", [128, 257], F32)
        dTb = sb("dTb", [128, 2, 128], BF16)
        PT = [sb(f"PT{i}", [128, 512], BF16) for i in range(2)]
        yst = [sb(f"yst{i}", [128, 768], BF16) for i in range(4)]

        IDb = cst_b[:, 0, :]
        IDf = cst_f[:, 0, :]
        UIN = cst_f[:, 1, :]
        UREV = cst_f[:, 2, :]
        ONESf = cst_f[:, 3, :]
        MASKb = cst_b[:, 4, :]
        MCUR = cst_b[:, 5, :]
        MFIRST = cst_b[:, 6, :]
        MPREV = cst_b[:, 7, :]

        P.dma(cst_f[:], cst_d.rearrange("p (c n) -> p c n", n=128), writes=["cst_f"])
        P.dma(vec[:], vec_d, writes=["vec"])
        P.dma(rows[:], row_d, writes=["rows"])
        P.cp("dve", cst_b[:], cst_f[:], ["cst_f"], ["cst_b"])
        P.dma(pw_b[:], pw_d.rearrange("(k p) n -> p k n", p=128), writes=["pw_b"], q="pool")
        P.op("pool", lambda e: e.memset(VA[:], 1.0), (), [("VA", j) for j in range(NST)])
        for i4 in range(4):
            P.op("pool", lambda e, i4=i4: e.memset(VAm4[i4][:], 1.0), (), [("VAm", i4)])
        P.op("pool", lambda e: e.memset(halo[:], 0.0), (), [("halo", c) for c in range(4)])
        P.op("pool", lambda e: e.memset(Fcar[:], 0.0), (), ["Fcar"])
        P.op("pool", lambda e: e.memset(CT[:], 0.0), (), ["CT"])
        P.op("pool", lambda e: e.memset(CTb[:], 0.0), (), ["CTb"])
        P.op("pool", lambda e: e.memset(qfk[:], 0.0), (), ["qfk"])
        P.op("pool", lambda e: e.memset(KF[:], 0.0), (), [("KF", j) for j in range(NST)])
        for h in range(2):
            qv = qfk[:, h, 0, :].rearrange("p (a c) -> p a c", c=32)
            kv = qfk[:, h, 1, :].rearrange("p (a c) -> p a c", c=32)
            P.op("pool", lambda e, qv=qv: e.memset(qv[:, :, 3:6], 1.0), (), ["qfk"])
            P.op("pool", lambda e, kv=kv: e.memset(kv[:, :, 0:3], 1.0), (), ["qfk"])

        w1v = w1_d.rearrange("(k p) n -> p k n", p=128)
        for k in range(8):
            P.dma(W1[:, k, :], w1v[:, k, :], writes=[("W1", k)], q="pool")

        def w1keys(c0, c1, k):
            return [("W1", k)]

        prot = 0
        for j in range(NST):
            for a in range(4 if 'noA' not in dbg else 0):
                i = 4 * j + a
                xb = xt[i % 2]
                xk = ("xt", i % 2)
                P.dma(xb[:], xtile(i), writes=[xk])
                P.act(junk_b[:], xb[:], AF.Square, [xk], ["xn", "ss"], accum_out=ss[:, 0:1])
                P.act(ss[:, 1:2], ss[:, 0:1], AF.Ln, ["ss"], ["ss1"], scale=1.0 / 1024, bias=EPS)
                P.act(ss[:, 1:2], ss[:, 1:2], AF.Exp, ["ss1"], ["ss1"], scale=-0.5)
                P.stt(xn[:], xb[:], ss[:, 1:2], rows[:, 512:1536], ALU.mult, ALU.mult, [xk, "ss1", "rows"], ["xn"])
                tbk = 5 if a % 2 == 0 else 6
                tbv = bank[tbk][:].bitcast(BF16)
                for k in range(8):
                    P.tr(tbv[:, k * 128:(k + 1) * 128], xn[:, k * 128:(k + 1) * 128], IDb, ["xn", "cst_b"], [bkey[tbk]])
                P.cp("act", hT[:, :, a * 128:(a + 1) * 128], tbv[:, 0:1024].rearrange("p (k n) -> p k n", n=128),
                     [bkey[tbk]], ["hT"])

            PROT = [4, 0, 1, 2, 3]
            for c in range(8 if stage >= 1 else 0):
                pb = PROT[prot % 5]; prot += 1
                bank4, bkey4 = bank[pb], bkey[pb]
                for k in range(8):
                    P.mm(bank4[:, 0:512], W1[:, k, c * 128:(c + 1) * 128], hT[:, k, :], k == 0, k == 7,
                         ["hT"] + w1keys(c * 128, (c + 1) * 128, k), [bkey4])
                if c < 2:
                    P.act(QT[:, c, :], bank4[:, 0:512], AF.Copy, [bkey4], ["QT"], scale=128 ** -0.5)
                elif c < 4:
                    P.cp("dve", KT[:, c - 2, j * 512:(j + 1) * 512], bank4[:, 0:512], [bkey4], [("KT", j)])
                else:
                    cc = c - 4
                    P.cp("act", cin[:, 3:515], bank4[:, 0:512], [bkey4], ["cin"])
                    P.cp("dve", cin[:, 0:3], halo[:, cc, :], [("halo", cc)], ["cin"])
                    P.ts(ctmp[:], cin[:, 0:512], vec[:, 8 + cc * 4:9 + cc * 4], None, ALU.mult, None,
                         ["cin", "vec"], ["ctmp"])
                    for t in range(1, 4):
                        P.stt(ctmp[:], cin[:, t:t + 512], vec[:, 8 + cc * 4 + t:9 + cc * 4 + t], ctmp[:],
                              ALU.mult, ALU.add, ["cin", "vec", "ctmp"], ["ctmp"])
                    P.cp("dve", halo[:, cc, :], cin[:, 512:515], ["cin"], [("halo", cc)])
                    P.act(csig[:], ctmp[:], AF.Sigmoid, ["ctmp"], ["csig"])
                    P.stt(qkT[:, cc, :], ctmp[:], (256 ** -0.5) if cc < 2 else 1.0, csig[:], ALU.mult, ALU.mult,
                          ["ctmp", "csig"], ["qkT"])

            for a in range(4 if stage >= 2 else 0):
                i = 4 * j + a
                ts_ = slice(a * 128, (a + 1) * 128)
                VAm, gateA, gateC, sp_, cs, lsv = VAm4[a], gateA4[a], gateC4[a], sp4[a], cs4[a], ls4[a]
                kVAm, kgA, kgC, ksp, kcs, kls = ("VAm", a), ("gateA", a), ("gateC", a), ("sp", a), ("cs", a), ("ls", a)
                groups = [(0, 512), (512, 1024), (1024, 1536), (1536, 1796)]
                for gi, (c0, c1) in enumerate(groups):
                    n = c1 - c0
                    pb = PROT[prot % 5]; prot += 1
                    bank4, bkey4 = bank[pb], bkey[pb]
                    sg = sgs[prot % 2]
                    sgk = ("sg", prot % 2)
                    for k in range(8):
                        P.mm(bank4[:, 0:n], hT[:, k, ts_], W1[:, k, NFM + c0:NFM + c1], k == 0, k == 7,
                             ["hT"] + w1keys(NFM + c0, NFM + c1, k), [bkey4])
                    lo = bank4[:, 0:256]
                    hi = bank4[:, 256:512]
                    if gi == 0:
                        P.cp("dve", VA[:, i, :, 0:128], lo.rearrange("p (h d) -> p h d", d=128), [bkey4], [("VA", j)])
                        P.act(sg[:], hi, AF.Sigmoid, [bkey4], [sgk])
                        P.tt(gateB[a][:], sg[:], hi, ALU.mult, [sgk, bkey4], [("gateB", a)])
                    elif gi == 1:
                        P.cp("dve", VAm[:, 0:256], lo, [bkey4], [kVAm])
                        P.act(sg[:], hi, AF.Sigmoid, [bkey4], [sgk])
                        P.tt(gateA[:], sg[:], rows[:, 0:256], ALU.mult, [sgk, "rows"], [kgA])
                    elif gi == 2:
                        P.act(sg[:], lo, AF.Sigmoid, [bkey4], [sgk])
                        P.tt(sg[:], sg[:], lo, ALU.mult, [sgk, bkey4], [sgk])
                        P.tt(gateA[:], gateA[:], sg[:], ALU.mult, [sgk, kgA], [kgA])
                        P.cp("act", ut[i % 5][:], hi, [bkey4], [("ut", i % 5)])
                    else:
                        P.act(sg[:], lo, AF.Sigmoid, [bkey4], [sgk])
                        P.tt(sg[:], sg[:], lo, ALU.mult, [sgk, bkey4], [sgk])
                        P.tt(gateC[:], sg[:], rows[:, 256:512], ALU.mult, [sgk, "rows"], [kgC])
                        P.tt(sp_[:], vec[:, 24:28], bank4[:, 256:260], ALU.add, ["vec", bkey4], [ksp])

                if stage < 3:
                    continue
                P.act(sm[:, 0:3], sp_[:, 1:4], AF.Abs, [ksp], ["sm0"])
                P.act(sm[:, 3:6], sm[:, 0:3], AF.Exp, ["sm0"], ["sm3"], scale=-1.0)
                P.act(sm[:, 6:9], sm[:, 3:6], AF.Ln, ["sm3"], ["sm6"], bias=1.0)
                P.ts(sm[:, 9:12], sp_[:, 1:4], 0.0, None, ALU.min, None, [ksp], ["sm9"])
                P.tt(lsv[:, 0:3], sm[:, 9:12], sm[:, 6:9], ALU.subtract, ["sm9", "sm6"], [kls])
                ls = lsv[:, 0:3]
                P.mm(bank[7][:, 260:263], UIN, ls, True, True, [kls, "cst_f"], [bkey[7]])
                P.mm(bank[7][:, 263:264], UREV, lsv[:, 0:1], True, True, [kls, "cst_f"], [bkey[7]])
                P.mm(bank[7][:, 264:267], ONESf, ls, True, True, [kls, "cst_f"], [bkey[7]])
                P.cp("dve", cs[:, 0:7], bank[7][:, 260:267], [bkey[7]], [kcs])
                if a == 0:
                    P.cp("dve", Fref[:], Fcar[:], ["Fcar"], ["Fref"])
                P.tt(fw[:, 0:2], cs[:, 1:3], Fcar[:], ALU.add, [kcs, "Fcar"], ["fw0"])
                P.cp("dve", Fk[:, i, :], fw[:, 0:2], ["fw0"], [("Fk", j)])
                P.tt(sm[:, 29:31], fw[:, 0:2], Fref[:], ALU.subtract, ["fw0", "Fref"], ["sm29"])
                P.act(fq[:, a, :], sm[:, 29:31], AF.Exp, ["sm29"], [("fq", a)])
                P.tt(Fcar[:], Fcar[:], cs[:, 5:7], ALU.add, [kcs, "Fcar"], ["Fcar"])
                P.cp("dve", f3b[:, :, 0], fw[:, 0:2], ["fw0"], ["f3b0"])
                P.cp("dve", fw[:, 2:4], f3b[:, :, 0], ["f3b0"], ["fw2"])
                P.tt(fw[:, 4:6], fw[:, 0:2], fw[:, 2:4], ALU.subtract, ["fw0", "fw2"], ["fw4"])
                P.cp("dve", f3b[:, :, 1], fw[:, 4:6], ["fw4"], ["f3b1"])
                P.cp("dve", fw[:, 2:4], f3b[:, :, 1], ["f3b1"], ["fw2"])
                P.tt(fw[:, 6:8], fw[:, 4:6], fw[:, 2:4], ALU.subtract, ["fw4", "fw2"], ["fw6"])
                P.cp("dve", f3b[:, :, 2], fw[:, 6:8], ["fw6"], ["f3b2"])
                P.ts(f3n[:], f3b[:], -1.0, None, ALU.mult, None, ["f3b0", "f3b1", "f3b2"], ["f3n"])
                ak = i // GT
                for h in range(2):
                    qv = qfk[:, h, 0, :].rearrange("p (a c) -> p a c", c=32)
                    for a4 in range(4):
                        P.cp("dve", qv[:, a4, 0:3], f3b[:, h, :], ["f3b0", "f3b1", "f3b2"], ["qfk"])
                    P.cp("dve", qfk[:, h, 1, 32 * ak + 3:32 * ak + 6], f3n[:, h, :], ["f3n"], ["qfk"])
                for h in range(2):
                    for w in range(2):
                        P.tr(b5b[:, (2 * h + w) * 128:(2 * h + w + 1) * 128], qfk[:, h, w, :], IDb,
                             ["qfk", "cst_b"], [bkey[5]])
                for h in range(2):
                    P.cp("act", QF[:, h, ts_], b5b[:, (2 * h) * 128:(2 * h + 1) * 128], [bkey[5]], ["QF"])
                    kcol = (i % GT) * 128
                    P.cp("dve", KF[32 * ak:32 * ak + 6, h, kcol:kcol + 128],
                         b5b[32 * ak:32 * ak + 6, (2 * h + 1) * 128:(2 * h + 2) * 128], [bkey[5]], [("KF", j)])

            P.capture_begin()
            for a in range(4 if stage >= 4 else 0):
                i = 4 * j + a
                ts_ = slice(a * 128, (a + 1) * 128)
                VAm, gateA, gateC, sp_, cs, lsv = VAm4[a], gateA4[a], gateC4[a], sp4[a], cs4[a], ls4[a]
                kVAm, kgA, kgC, ksp, kcs, kls = ("VAm", a), ("gateA", a), ("gateC", a), ("sp", a), ("cs", a), ("ls", a)
                P.mm(bank[6][:, 0:128], qkT[:, 2, ts_], qkT[:, 0, ts_], True, False, ["qkT"], [bkey[6]])
                P.mm(bank[6][:, 0:128], qkT[:, 3, ts_], qkT[:, 1, ts_], False, True, ["qkT"], [bkey[6]])
                P.ts(Bm[:], UIN, lsv[:, 0:1], None, ALU.mult, None, ["cst_f", kls], ["Bm"])
                P.mm(bank[6][:, 128:256], UREV, Bm[:], True, False, ["Bm", "cst_f"], [bkey[6]])
                P.mm(bank[6][:, 128:256], IDb, MASKb, False, True, ["cst_b"], [bkey[6]])
                P.act(ET[:], bank[6][:, 128:256], AF.Exp, [bkey[6], ksp], ["ET"], bias=sp_[:, 0:1])
                P.tt(PTm[:], ET[:], bank[6][:, 0:128], ALU.mult, ["ET", bkey[6]], ["PTm"])
                P.tr(b5b[:, 0:128], qkT[:, 2, ts_], IDb, ["qkT", "cst_b"], [bkey[5]])
                P.tr(b5b[:, 128:256], qkT[:, 3, ts_], IDb, ["qkT", "cst_b"], [bkey[5]])
                P.cp("act", Ktok[:], b5b[:, 0:256], [bkey[5]], ["Ktok"])
                P.act(sm[:, 16:17], cs[:, 0:1], AF.Exp, [kcs], ["sm16"])
                P.act(sm[:, 17:18], cs[:, 3:4], AF.Exp, [kcs, ksp], ["sm17"], bias=sp_[:, 0:1])
                P.act(sm[:, 18:19], cs[:, 4:5], AF.Exp, [kcs], ["sm18"])
                P.ts(Vw[:], VAm[:], sm[:, 17:18], None, ALU.mult, None, [kVAm, "sm17"], ["Vw"])
                P.mm(bank[7][:, 0:257], qkT[:, 0, ts_], CTb[:, 0, :], True, False, ["qkT", "CTb"], [bkey[7]])
                P.mm(bank[7][:, 0:257], qkT[:, 1, ts_], CTb[:, 1, :], False, True, ["qkT", "CTb"], [bkey[7]])
                P.act(Gs[:], bank[7][:, 0:257], AF.Copy, [bkey[7], "sm16"], ["Gs"], scale=sm[:, 16:17])
                P.mm(bank[7][:, 0:257], PTm[:], VAm[:], True, True, ["PTm", kVAm], [bkey[7]])
                P.tt(Gs[:], Gs[:], bank[7][:, 0:257], ALU.add, ["Gs", bkey[7]], ["Gs"])
                for c in range(2):
                    P.mm(bank[7][:, 0:257], Ktok[:, c * 128:(c + 1) * 128], Vw[:], True, True, ["Ktok", "Vw"], [bkey[7]])
                    P.stt(CT[:, c, :], CT[:, c, :], sm[:, 18:19], bank[7][:, 0:257], ALU.mult, ALU.add,
                          ["CT", "sm18", bkey[7]], ["CT"])
                    P.cp("act", CTb[:, c, :], CT[:, c, :], ["CT"], ["CTb"])
                P.act(sm[:, 19:20], Gs[:, 256:257], AF.Abs, ["Gs"], ["sm19"])
                P.ts(sm[:, 20:21], sm[:, 19:20], 1.0, None, ALU.max, None, ["sm19"], ["sm20"])
                P.op("dve", lambda e: e.reciprocal(out=sm[:, 21:22], in_=sm[:, 20:21]), ["sm20"], ["sm21"])
                P.act(sgs[0][:], Gs[:, 0:256], AF.Square, ["Gs"], [("sg", 0), "sm22"], accum_out=sm[:, 22:23])
                P.tt(sm[:, 23:24], sm[:, 21:22], sm[:, 21:22], ALU.mult, ["sm21"], ["sm23"])
                P.tt(sm[:, 24:25], sm[:, 23:24], sm[:, 22:23], ALU.mult, ["sm23", "sm22"], ["sm24"])
                P.act(sm[:, 25:26], sm[:, 24:25], AF.Ln, ["sm24"], ["sm25"], scale=1.0 / 256, bias=EPS)
                P.act(sm[:, 26:27], sm[:, 25:26], AF.Exp, ["sm25"], ["sm26"], scale=-0.5)
                P.tt(sm[:, 27:28], sm[:, 26:27], sm[:, 21:22], ALU.mult, ["sm26", "sm21"], ["sm27"])
                yk = ("yst", a)
                P.stt(yst[a][:, 0:256], Gs[:, 0:256], sm[:, 27:28], gateA[:], ALU.mult, ALU.mult,
                      ["Gs", "sm27", kgA], [yk])

                if stage < 5:
                    continue
                ucur = ut[i % 5]
                uprev = ut[(i - 1) % 5]
                for cc in range(2):
                    o_ = bank[6][:, 256 + cc * 128:384 + cc * 128]
                    if i == 0:
                        P.mm(o_, ucur[:, cc * 128:(cc + 1) * 128], MFIRST, True, True, [("ut", i % 5), "cst_b"], [bkey[6]])
                    else:
                        P.mm(o_, ucur[:, cc * 128:(cc + 1) * 128], MCUR, True, False, [("ut", i % 5), "cst_b"], [bkey[6]])
                        P.mm(o_, uprev[:, cc * 128:(cc + 1) * 128], MPREV, False, True,
                             [("ut", (i - 1) % 5), "cst_b"], [bkey[6]])
                P.cp("act", dTb[:], bank[6][:, 256:512].rearrange("p (c n) -> p c n", n=128), [bkey[6]], ["dTb"])
                for cc in range(2):
                    P.mm(bank[7][:, 0:256], dTb[:, cc, :], pw_b[:, cc, :], cc == 0, cc == 1, ["dTb", "pw_b"], [bkey[7]])
                P.tt(yst[a][:, 512:768], gateC[:], bank[7][:, 0:256], ALU.mult, [kgC, bkey[7]], [yk])

            deferred = P.capture_end()
            deferred.reverse()
            its = [(h, kt) for h in range(2 if stage >= 6 else 0) for kt in range(4 * j + 4)]

            if j > 0 and stage >= 6:
                for h in range(2):
                    P.ts(nbias[:, h, 0:4 * j], Fk[:, 0:4 * j, h], -1.0, Fref[:, h:h + 1], ALU.mult, ALU.add,
                         [("Fk", jj) for jj in range(j)] + ["Fref"], [("nbias", h)])

            def scores(idx):
                h, kt = its[idx]
                n0 = max(0, kt - 4 * j)
                q0 = n0 * 128
                ak = kt // GT
                kcol = (kt % GT) * 128
                sb_ = idx % 2
                stt_ = bank[sb_][:, q0:512]
                diag = kt >= 4 * j
                if not diag:
                    P.mm(stt_, KT[:, h, kt * 128:(kt + 1) * 128], QT[:, h, q0:512], True, True,
                         [("KT", kt // 4), "QT"], [bkey[sb_]])
                    P.act(PT[sb_][:, q0:512], stt_, AF.Exp, [bkey[sb_], ("nbias", h)], [("PT", sb_)],
                          bias=nbias[:, h, kt:kt + 1])
                    return
                P.mm(stt_, KT[:, h, kt * 128:(kt + 1) * 128], QT[:, h, q0:512], True, False,
                     [("KT", kt // 4), "QT"], [bkey[sb_]])
                P.mm(stt_, KF[32 * ak:32 * ak + 6, h, kcol:kcol + 128], QF[32 * ak:32 * ak + 6, h, q0:512],
                     False, False, [("KF", kt // 4), "QF"], [bkey[sb_]])
                P.mm(bank[sb_][:, q0:q0 + 128], IDb, MASKb, False, True, ["cst_b"], [bkey[sb_]])
                P.act(PT[sb_][:, q0:512], stt_, AF.Exp, [bkey[sb_]], [("PT", sb_)])

            per = (len(deferred) + max(1, len(its) - 2) - 1) // max(1, len(its) - 2)

            def replay(n):
                for _ in range(n):
                    if deferred:
                        P.op(*deferred.pop())
            if its:
                scores(0)
            for idx, (h, kt) in enumerate(its):
                if idx + 1 < len(its):
                    scores(idx + 1)
                replay(per)
                n0 = max(0, kt - 4 * j)
                sb_ = idx % 2
                for n in range(n0, 4):
                    ob = 2 + n // 2
                    oc = (n % 2) * 256
                    first = (kt == 0) or (kt == 4 * j)
                    P.mm(bank[ob][:, oc:oc + 129], PT[sb_][:, n * 128:(n + 1) * 128], VA[:, kt, h, :],
                         first and (n % 2 == 0), (kt == 4 * j + n) or (kt == 4 * j - 1),
                         [("PT", sb_), ("VA", kt // 4)], [bkey[ob]])
                if j > 0 and kt == 4 * j - 1:
                    for n in range(4):
                        ob = 2 + n // 2
                        oc = (n % 2) * 256
                        P.ts(Osb[:, n, :], bank[ob][:, oc:oc + 129], fq[:, n, h:h + 1], None, ALU.mult, None,
                             [bkey[ob], ("fq", n)], [("Osb", n)])
                if kt == 4 * j + 3:
                    for n in range(4):
                        ob = 2 + n // 2
                        oc = (n % 2) * 256
                        if j > 0:
                            P.tt(Osb[:, n, :], Osb[:, n, :], bank[ob][:, oc:oc + 129], ALU.add, [("Osb", n), bkey[ob]], [("Osb", n)])
                            P.op("dve", lambda e, n=n: e.reciprocal(out=sm[:, 28:29], in_=Osb[:, n, 128:129]), [("Osb", n)], ["sm28"])
                            P.stt(yst[n][:, 256 + h * 128:384 + h * 128], Osb[:, n, 0:128], sm[:, 28:29],
                                  gateB[n][:, h * 128:(h + 1) * 128], ALU.mult, ALU.mult,
                                  [("Osb", n), "sm28", ("gateB", n)], [("yst", n)])
                        else:
                            P.op("dve", lambda e, ob=ob, oc=oc: e.reciprocal(out=sm[:, 28:29], in_=bank[ob][:, oc + 128:oc + 129]),
                                 [bkey[ob]], ["sm28"])
                            P.stt(yst[n][:, 256 + h * 128:384 + h * 128], bank[ob][:, oc:oc + 128], sm[:, 28:29],
                                  gateB[n][:, h * 128:(h + 1) * 128], ALU.mult, ALU.mult,
                                  [bkey[ob], "sm28", ("gateB", n)], [("yst", n)])
            replay(len(deferred))
            for n in range(4):
                i = 4 * j + n
                P.dma(ys_d[i * 128:(i + 1) * 128, :], yst[n][:], reads=[("yst", n)], writes=[("ys", i)])
            if after_st is not None:
                after_st(j, [("ys", 4 * j + n) for n in range(4)])
    return [("ys", i) for i in range(NT)]


def _consts(g):
    s = np.arange(128)[:, None]
    t = np.arange(128)[None, :]
    ident = (s == t).astype(np.float32)
    uin = (s <= t).astype(np.float32)
    urev = (s > t).astype(np.float32)
    ones = np.ones((128, 128), np.float32)
    maskT = np.where(s > t, -30000.0, 0.0).astype(np.float32)
    W = POOL_WINDOWS[g]
    def mt(first):
        M = np.zeros((128, 256), np.float32)
        for tt in range(128):
            cnt = min(tt + 1, W) if first else W
            for jj in range(cnt):
                M[tt, 128 + tt - jj] += 1.0 / cnt
            M[tt, 128 + tt] -= 1.0
        return M
    Mg = mt(False)
    Mf = mt(True)
    mcur = Mg[:, 128:].T.copy()
    mprev = Mg[:, :128].T.copy()
    mfirst = Mf[:, 128:].T.copy()
    return np.concatenate([ident, uin, urev, ones, maskT, mcur, mfirst, mprev], axis=1).astype(np.float32)


def _p1_cols(g):
    W = 1024
    off = {}
    names = ["aq", "ak", "av", "ao", "az"]
    o = 0
    for nme in names:
        off[nme] = o
        o += W
    off["ai"] = o; o += 4
    off["af"] = o; o += 4
    for nme in ["bq", "bk", "bv", "bz"]:
        off[nme] = o
        o += W
    off["bf"] = o; o += 8
    off["cu"] = o; o += W
    off["cz"] = o; o += W
    off["gates"] = o
    r = lambda nme: np.arange(off[nme] + g * 256, off[nme] + (g + 1) * 256)
    cols = np.concatenate([
        r("bq"), r("bk"), r("aq"), r("ak"),
        r("bv"), r("bz"), r("av"), r("ao"), r("az"), r("cu"), r("cz"),
        np.array([off["ai"] + g, off["af"] + g, off["bf"] + 2 * g, off["bf"] + 2 * g + 1]),
    ])
    return cols, off


def p1_inputs(x_b, l, g, inp):
    cols, off = _p1_cols(g)
    w1 = np.ascontiguousarray(inp["w_in"][l][:, cols])
    vec = np.zeros((128, 32), np.float32)
    vec[:, 0:8] = inp["norm_g"][l].reshape(8, 128).T
    cw = inp["conv_w"][l]
    for cc in range(4):
        base = (0 if cc < 2 else 1024) + g * 256 + (cc % 2) * 128
        vec[:, 8 + cc * 4:12 + cc * 4] = cw[:, base:base + 128].T
    vec[:, 24] = inp["ml_bi"][l][g]
    vec[:, 25] = inp["ml_bf"][l][g]
    vec[:, 26] = inp["fox_bf"][l][2 * g]
    vec[:, 27] = inp["fox_bf"][l][2 * g + 1]
    rows = np.zeros((128, 1536), np.float32)
    rows[:, 0:256] = inp["ml_norm_g"][l][g * 256:(g + 1) * 256][None, :]
    rows[:, 256:512] = inp["pool_scale"][l][g * 256:(g + 1) * 256][None, :]
    rows[:, 512:1536] = inp["norm_g"][l][None, :]
    return dict(x=np.ascontiguousarray(x_b), w1=w1, vecs=vec, rows=rows,
                poolw=np.ascontiguousarray(inp["pool_w"][l][g]), consts=_consts(g))


def p2_record(P, nc, st, bank, T, last, xsrc, ysload, wg_d, wb_d, wo_d, vec_d, fg_d, id_d, odst, pre=None, after_tile=None, post_w=None):
    NT = T // 128
    if True:
        tag = f"sb{P.nflush}_"
        sb = lambda n, s, d: st.enter_context(nc.sbuf_tensor(tag + n, s, d))
        bkey = [f"bk{b}" for b in range(8)]
        Wg = sb("Wg", [128, 8, 3072], BF16)
        Wb = sb("Wb", [128, 24, 1024], BF16)
        Wo = sb("Wo", [128, 8, 1024], BF16)
        grow = sb("grow", [128, 1024], F32)
        fg = sb("fg", [128, 1024], F32)
        idf = sb("idf", [128, 128], F32)
        idb = sb("idb", [128, 128], BF16)
        xt = [sb(f"xt{i}", [128, 1024], F32) for i in range(2)]
        ysts = [sb(f"yst{i}", [128, 3072], BF16) for i in range(2)]
        ysTs = [sb(f"ysT{i}", [128, 24, 128], BF16) for i in range(2)]
        junk_b = sb("junk_b", [128, 1024], BF16)
        ss = sb("ss", [128, 4], F32)
        xn = sb("xn", [128, 1024], BF16)
        hTs = [sb(f"hT{i}", [128, 8, 128], BF16) for i in range(2)]
        gsb = sb("gsb", [128, 3072], F32)
        mrg = sb("mrg", [128, 1024], F32)
        tmp = sb("tmp", [128, 512], F32)
        mb = sb("mb", [128, 1024], BF16)
        mT = sb("mT", [128, 8, 128], BF16)
        xo = [sb(f"xo{i}", [128, 1024], F32) for i in range(2)]

        P.dma(grow[:], vec_d, writes=["grow"])
        P.dma(fg[:], fg_d, writes=["fg"])
        P.dma(idf[:], id_d, writes=["idf"])
        P.cp("dve", idb[:], idf[:], ["idf"], ["idb"])
        wgv = wg_d.rearrange("(k p) n -> p k n", p=128)
        wbv = wb_d.rearrange("(k p) n -> p k n", p=128)
        wov = wo_d.rearrange("(k p) n -> p k n", p=128)
        for k in range(8):
            P.dma(Wg[:, k, :], wgv[:, k, :], writes=[("Wg", k)], q="pool")
        for k in range(24):
            P.dma(Wb[:, k, :], wbv[:, k, :], writes=[("Wb", k)], q="pool")
        for k in range(8):
            P.dma(Wo[:, k, :], wov[:, k, :], writes=[("Wo", k)], q="pool")
        if post_w is not None:
            post_w()
        wkeys = lambda key, k, c0, c1: [(key, k)]

        mmb = 0
        trb = 0
        def prep(i):
            nonlocal trb
            hT, ysT, yst = hTs[i % 2], ysTs[i % 2], ysts[i % 2]
            khT, kyst = ("hT", i % 2), ("yst", i % 2)
            xb = xt[i % 2]
            xk = ("xt", i % 2)
            pr = list(pre(i)) if pre is not None else []
            P.dma(xb[:], xsrc(i), reads=pr, writes=[xk])
            for (o_, i_) in ysload(i, yst):
                P.dma(o_, i_, reads=pr, writes=[kyst])
            P.act(junk_b[:], xb[:], AF.Square, [xk], ["junk_b", "ss"], accum_out=ss[:, 0:1])
            P.act(ss[:, 1:2], ss[:, 0:1], AF.Ln, ["ss"], ["ss1"], scale=1.0 / 1024, bias=EPS)
            P.act(ss[:, 1:2], ss[:, 1:2], AF.Exp, ["ss1"], ["ss1"], scale=-0.5)
            P.stt(xn[:], xb[:], ss[:, 1:2], grow[:], ALU.mult, ALU.mult, [xk, "ss1", "grow"], ["xn"])
            tb = 4 + trb % 2; trb += 1
            tbv = bank[tb][:].bitcast(BF16)
            for k in range(8):
                P.tr(tbv[:, k * 128:(k + 1) * 128], xn[:, k * 128:(k + 1) * 128], idb[:], ["xn", "idb"], [bkey[tb]])
            P.cp("act", hT[:], tbv[:, 0:1024].rearrange("p (k n) -> p k n", n=128), [bkey[tb]], [khT])
            for n in range(3):
                tb = 4 + trb % 2; trb += 1
                tbv = bank[tb][:].bitcast(BF16)
                for k in range(8):
                    c = n * 1024 + k * 128
                    P.tr(tbv[:, k * 128:(k + 1) * 128], yst[:, c:c + 128], idb[:], [kyst, "idb"], [bkey[tb]])
                P.cp("dve" if n % 2 == 0 else "act", ysT[:, n * 8:(n + 1) * 8, :],
                     tbv[:, 0:1024].rearrange("p (k n) -> p k n", n=128), [bkey[tb]], [("ysT", i % 2, n)])
        def gates(i):
            nonlocal mmb
            hT = hTs[i % 2]
            khT = ("hT", i % 2)
            for cg in range(6):
                b = (0, 1, 2, 3, 6, 7)[mmb % 6]; mmb += 1
                for k in range(8):
                    P.mm(bank[b][:, 0:512], hT[:, k, :], Wg[:, k, cg * 512:(cg + 1) * 512], k == 0, k == 7,
                         [khT] + wkeys("Wg", k, cg * 512, (cg + 1) * 512), [bkey[b]])
                P.act(gsb[:, cg * 512:(cg + 1) * 512], bank[b][:, 0:512], AF.Sigmoid, [bkey[b]], [("gsb", cg)])
        def rest(i):
            nonlocal mmb, trb
            ysT = ysTs[i % 2]
            xb = xt[i % 2]
            xk = ("xt", i % 2)
            for n in range(3):
                for hf in range(2):
                    b = (0, 1, 2, 3, 6, 7)[mmb % 6]; mmb += 1
                    for k in range(8):
                        P.mm(bank[b][:, 0:512], ysT[:, n * 8 + k, :], Wb[:, n * 8 + k, hf * 512:(hf + 1) * 512], k == 0, k == 7,
                             [("ysT", i % 2, n)] + wkeys("Wb", n * 8 + k, hf * 512, (hf + 1) * 512), [bkey[b]])
                    gv = gsb[:, n * 1024 + hf * 512:n * 1024 + (hf + 1) * 512]
                    gk = ("gsb", n * 2 + hf)
                    mk = ("mrg", hf)
                    mv = mrg[:, hf * 512:(hf + 1) * 512]
                    if n == 0:
                        P.tt(mv, gv, bank[b][:, 0:512], ALU.mult, [gk, bkey[b]], [mk])
                    else:
                        P.tt(tmp[:], gv, bank[b][:, 0:512], ALU.mult, [gk, bkey[b]], ["tmp"])
                        if n == 1:
                            P.tt(mv, mv, tmp[:], ALU.add, [mk, "tmp"], [mk], eng="pool")
                        else:
                            P.tt(mb[:, hf * 512:(hf + 1) * 512], mv, tmp[:], ALU.add, [mk, "tmp"], [("mb", hf)], eng="pool")
            tb = 4 + trb % 2; trb += 1
            tbv = bank[tb][:].bitcast(BF16)
            for k in range(8):
                P.tr(tbv[:, k * 128:(k + 1) * 128], mb[:, k * 128:(k + 1) * 128], idb[:], [("mb", k // 4), "idb"], [bkey[tb]])
            P.cp("act", mT[:], tbv[:, 0:1024].rearrange("p (k n) -> p k n", n=128), [bkey[tb]], ["mT"])
            ob = xo[i % 2]
            ok = ("xo", i % 2)
            for hf in range(2):
                b = (0, 1, 2, 3, 6, 7)[mmb % 6]; mmb += 1
                for k in range(8):
                    P.mm(bank[b][:, 0:512], mT[:, k, :], Wo[:, k, hf * 512:(hf + 1) * 512], k == 0, k == 7,
                         ["mT"] + wkeys("Wo", k, hf * 512, (hf + 1) * 512), [bkey[b]])
                P.tt(ob[:, hf * 512:(hf + 1) * 512], xb[:, hf * 512:(hf + 1) * 512], bank[b][:, 0:512], ALU.add,
                     [xk, bkey[b]], [ok])
            if last:
                P.act(junk_b[:], ob[:], AF.Square, [ok], ["junk_b", "ss2"], accum_out=ss[:, 2:3])
                P.act(ss[:, 3:4], ss[:, 2:3], AF.Ln, ["ss2"], ["ss3"], scale=1.0 / 1024, bias=EPS)
                P.act(ss[:, 3:4], ss[:, 3:4], AF.Exp, ["ss3"], ["ss3"], scale=-0.5)
                P.stt(ob[:], ob[:], ss[:, 3:4], fg[:], ALU.mult, ALU.mult, [ok, "ss3", "fg"], [ok])
            P.dma(odst(i), ob[:], reads=[ok], writes=[("o", i)])
            if after_tile is not None:
                after_tile(i)
        prep(0)
        for i in range(NT):
            gates(i)
            if i + 1 < NT:
                prep(i + 1)
            rest(i)
    return [("o", i) for i in range(NT)]


def p2_inputs(x_sh, ys_sh, l, inp):
    o = 5 * 1024 + 8 + 4 * 1024 + 8 + 2 * 1024
    return dict(x=np.ascontiguousarray(x_sh), ysin=np.ascontiguousarray(ys_sh),
                wg=np.ascontiguousarray(inp["w_in"][l][:, o:o + 3072]),
                wb=np.ascontiguousarray(inp["w_branch"][l].reshape(3072, 1024)),
                wo=np.ascontiguousarray(inp["w_out"][l]),
                vecs=np.ascontiguousarray(inp["norm_g"][l].reshape(8, 128).T),
                fgrows=np.ascontiguousarray(np.broadcast_to(inp["final_g"][None, :], (128, 1024))),
                ident=np.eye(128, dtype=np.float32))


GROUPS = [[0, 1, 2, 3], [4, 5, 6, 7]]


class _QCtx:
    def __init__(self, h):
        self.q = h.partition_id() % 4
        self.cache = {}

    def mul(self, m):
        if m == 1:
            return self.q
        if m not in self.cache:
            self.cache[m] = self.q * m
        return self.cache[m]


def build_fused(S=SEQ):
    TS = S // 4
    nc = bass.Bass("TRN2", target_bir_lowering=False)
    dram = lambda n, sh, dt, kind: nc.dram_tensor(n, sh, dt, kind=kind).ap()
    x_d = dram("x", [S, 1024], F32, "ExternalInput")
    cst_d = dram("consts", [128, 1024], F32, "ExternalInput")
    fg_d = dram("fgrows", [128, 1024], F32, "ExternalInput")
    L = []
    for l in range(DEPTH):
        L.append(dict(
            w1=dram(f"w1_{l}", [1024, NW1], F32, "ExternalInput"),
            vec=dram(f"vecs_{l}", [128, 32], F32, "ExternalInput"),
            rows=dram(f"rows_{l}", [128, 1536], F32, "ExternalInput"),
            pw=dram(f"poolw_{l}", [256, 256], F32, "ExternalInput"),
            wg=dram(f"wg_{l}", [1024, 3072], F32, "ExternalInput"),
            wb=dram(f"wb_{l}", [3072, 1024], F32, "ExternalInput"),
            wo=dram(f"wo_{l}", [1024, 1024], F32, "ExternalInput"),
        ))
    out_d = dram("out", [TS, 1024], F32, "ExternalOutput")
    ysrc = nc.dram_tensor("ysrc", [S, 768], BF16).ap()
    yall = nc.dram_tensor("yall", [4 * S, 768], BF16).ap()
    x1src = nc.dram_tensor("x1src", [TS, 1024], F32).ap()
    x1all = nc.dram_tensor("x1all", [S, 1024], F32).ap()
    ysh = nc.dram_tensor("ysh", [4 * TS, 768], BF16).ap()
    xsh = nc.dram_tensor("xsh", [TS, 1024], F32).ap()

    with contextlib.ExitStack() as top:
        bank = [top.enter_context(nc.psum_tensor(f"bk{b}", [128, 512], F32)) for b in range(8)]
        fin = top.enter_context(nc.sbuf_tensor("sb_fin", [1, 8], F32))
        P = Prog(nc, top)
        P.op("pool", lambda e: e.memset(fin[:], 0.0), (), ["fin0"])
        NB = TS // 512
        for l in range(DEPTH):
            d = L[l]
            last = (l == DEPTH - 1)
            with contextlib.ExitStack() as st:
                if l == 0:
                    xtile = lambda i: x_d[i * 128:(i + 1) * 128, :]
                else:
                    def xtile(i):
                        tok = i * 128
                        r, k, t = tok // TS, (tok % TS) // 256, tok % 256
                        row = k * 1024 + r * 256 + t
                        return x1all[row:row + 128, :]

                def after_st(j, keys):
                    P.cc("AllGather", GROUPS, yall[j * 2048:(j + 1) * 2048, :], ysrc[j * 512:(j + 1) * 512, :],
                         reads=keys, writes=[("yall", j)])
                p1_record(P, nc, st, bank, S, xtile, d["w1"], d["vec"], d["rows"], d["pw"], cst_d, ysrc, after_st=after_st)
                P.flush(fin, bank[7])
            with contextlib.ExitStack() as st:
                yv = yall.rearrange("(blk p r) c -> blk p (r c)", p=128, r=16)
                yshv = ysh.rearrange("(k p r) c -> k p (r c)", p=128, r=16)
                def shard_copy(jj):
                    P.dma(yshv[jj:jj + 1], (lambda q, jj=jj: yv[jj:][bass.ds(q.mul(NB), 1)]), writes=[("ysh", jj)])
                    if l == 0:
                        P.dma(xshv[jj:jj + 1], (lambda q, jj=jj: xv[jj:][bass.ds(q.mul(NB), 1)]), writes=[("xsh", jj)])
                if l == 0:
                    xv = x_d.rearrange("(blk p r) c -> blk p (r c)", p=128, r=4)
                    xshv = xsh.rearrange("(k p r) c -> k p (r c)", p=128, r=4)
                    xsrc = lambda i: xsh[i * 128:(i + 1) * 128, :]
                else:
                    xsrc = lambda i: x1src[i * 128:(i + 1) * 128, :]
                shard_copy(0)

                def post_w():
                    for jj in range(1, NB):
                        shard_copy(jj)

                def ysload(i, yst):
                    prs = []
                    jj, t0 = i // 4, (i % 4) * 128
                    for r in range(4):
                        o_ = yst[:].rearrange("p (n r c) -> p n r c", n=3, r=4)[:, :, r, :]
                        row = jj * 2048 + r * 512 + t0
                        i_ = ysh[row:row + 128, :].rearrange("p (n c) -> p n c", n=3)
                        prs.append((o_, i_))
                    return prs
                if last:
                    odst = lambda i: out_d[i * 128:(i + 1) * 128, :]
                    after_tile = None
                else:
                    odst = lambda i: x1src[i * 128:(i + 1) * 128, :]

                    def after_tile(i):
                        if i % 2 == 1:
                            k = i // 2
                            P.cc("AllGather", GROUPS, x1all[k * 1024:(k + 1) * 1024, :], x1src[k * 256:(k + 1) * 256, :],
                                 reads=[("o", i - 1), ("o", i)], writes=[("x1all", k)])
                p2_record(P, nc, st, bank, TS, last, xsrc, ysload, d["wg"], d["wb"], d["wo"],
                          d["rows"][:, 512:1536], fg_d, cst_d[:, 0:128], odst, pre=lambda i: [("ysh", i // 4), ("xsh", i // 4)],
                          after_tile=after_tile, post_w=post_w)
                P.flush(fin, bank[7])
    return nc


_CACHE = {}


def kernel(x, norm_g, w_in, conv_w, ml_bi, ml_bf, ml_norm_g, fox_bf, pool_w, pool_scale, w_branch, w_out, final_g):
    inp = dict(norm_g=np.asarray(norm_g), w_in=np.asarray(w_in), conv_w=np.asarray(conv_w), ml_bi=np.asarray(ml_bi),
               ml_bf=np.asarray(ml_bf), ml_norm_g=np.asarray(ml_norm_g), fox_bf=np.asarray(fox_bf),
               pool_w=np.asarray(pool_w), pool_scale=np.asarray(pool_scale), w_branch=np.asarray(w_branch),
               w_out=np.asarray(w_out), final_g=np.asarray(final_g))
    x = np.asarray(x, dtype=np.float32)
    B, S, _ = x.shape
    TS = S // 4
    if "nc" not in _CACHE:
        _CACHE["nc"] = build_fused(S)
    nc = _CACHE["nc"]
    go = 5 * 1024 + 8 + 4 * 1024 + 8 + 2 * 1024
    fgrows = np.ascontiguousarray(np.broadcast_to(inp["final_g"][None, :], (128, 1024)))
    ins = []
    for c in range(8):
        b, g = c // 4, c % 4
        m = dict(x=np.ascontiguousarray(x[b]), consts=_consts(g), fgrows=fgrows)
        for l in range(DEPTH):
            p1 = p1_inputs(x[b], l, g, inp)
            m[f"w1_{l}"] = p1["w1"]
            m[f"vecs_{l}"] = p1["vecs"]
            m[f"rows_{l}"] = p1["rows"]
            m[f"poolw_{l}"] = p1["poolw"]
            m[f"wg_{l}"] = np.ascontiguousarray(inp["w_in"][l][:, go:go + 3072])
            m[f"wb_{l}"] = np.ascontiguousarray(inp["w_branch"][l].reshape(3072, 1024))
            m[f"wo_{l}"] = np.ascontiguousarray(inp["w_out"][l])
        ins.append(m)
    res = run_bass_kernel_spmd(nc, ins, core_ids=list(range(8)))
    out = np.empty((B, S, 1024), np.float32)
    for c in range(8):
        b, q = c // 4, c % 4
        out[b, q * TS:(q + 1) * TS] = np.asarray(res.results[c]["out"])
    return out
```
